# Optimizing a Trainium2 kernel written in Bass

```python
import jax, jax.numpy as jnp
from jax import lax
import numpy as np

D_MODEL = 2048
BATCH = 2
SEQ = 4096
DEPTH = 4
DEC_BATCH = 8
DEC_SEQ = 16
PAST_LEN = 4096

CHUNK = 64
N_MIXERS = 3
EPS = 1e-6

MLA_HEADS = 16
Q_LORA = 512
KV_LORA = 512
QK_NOPE = 128
QK_ROPE = 64
V_HEAD = 128
MLA_GATE = MLA_HEADS * V_HEAD
MLA_IN = Q_LORA + KV_LORA + QK_ROPE + MLA_GATE
MLA_SCALE = (QK_NOPE + QK_ROPE) ** -0.5
ROPE_THETA = 10000.0
Q_BLOCK = 128

D_CONV = D_MODEL
CONV_K = 31

SSM_INNER = 2 * D_MODEL
SSM_HEAD_DIM = 64
SSM_HEADS = SSM_INNER // SSM_HEAD_DIM
SSM_GROUPS = 8
D_STATE = 128
SSM_CONV_K = 4
SSM_XBC = SSM_INNER + 2 * SSM_GROUPS * D_STATE
SSM_IN = SSM_INNER + SSM_XBC + SSM_HEADS
SSD_BLOCK = 64
DT_MIN = 0.001
DT_MAX = 0.1

kernel_name = 'hybrid_mla_conv_ssd_stream_step'


def rms_norm(x, g):
    xf = x.astype(jnp.float32)
    y = xf * lax.rsqrt(jnp.mean(xf * xf, axis=-1, keepdims=True) + EPS)
    return y.astype(x.dtype) * g


def layer_norm(x, g, b):
    xf = x.astype(jnp.float32)
    mu = jnp.mean(xf, axis=-1, keepdims=True)
    var = jnp.mean(jnp.square(xf - mu), axis=-1, keepdims=True)
    return ((xf - mu) * lax.rsqrt(var + EPS)).astype(x.dtype) * g + b


def rope(x, pos):
    half = x.shape[-1] // 2
    freqs = ROPE_THETA ** (-jnp.arange(half, dtype=jnp.float32) / half)
    ang = pos.astype(jnp.float32)[:, None] * freqs[None, :]
    cos, sin = jnp.cos(ang)[:, None, :], jnp.sin(ang)[:, None, :]
    xf = x.astype(jnp.float32)
    x1, x2 = xf[..., :half], xf[..., half:]
    return jnp.concatenate([x1 * cos - x2 * sin, x1 * sin + x2 * cos], axis=-1).astype(x.dtype)


def causal_depthwise(xp, w, b):
    c = xp.shape[-1]
    y = lax.conv_general_dilated(xp, w[:, None, :], window_strides=(1,), padding='VALID',
                                 dimension_numbers=('NWC', 'WIO', 'NWC'), feature_group_count=c)
    return y + b


def mla_project(h, pos, w_in, q_norm, w_qb, kv_norm):
    q_a, c_kv, k_pe, gate = jnp.split(h @ w_in, [Q_LORA, Q_LORA + KV_LORA, Q_LORA + KV_LORA + QK_ROPE], axis=-1)
    q = jnp.einsum('blr,rhd->blhd', rms_norm(q_a, q_norm), w_qb)
    q_nope, q_pe = q[..., :QK_NOPE], rope(q[..., QK_NOPE:], pos)
    ckv = rms_norm(c_kv, kv_norm)
    kpe = rope(k_pe[:, :, None, :], pos)[:, :, 0]
    return q_nope, q_pe, ckv, kpe, gate


def mla_prompt(h, w_in, q_norm, w_qb, kv_norm, w_kvb, w_out):
    b, s, _ = h.shape
    q_nope, q_pe, ckv, kpe, gate = mla_project(h, jnp.arange(s), w_in, q_norm, w_qb, kv_norm)
    kv = jnp.einsum('bsc,chd->bshd', ckv, w_kvb)
    k = jnp.concatenate([kv[..., :QK_NOPE], jnp.broadcast_to(kpe[:, :, None, :], (b, s, MLA_HEADS, QK_ROPE))], axis=-1)
    v = kv[..., QK_NOPE:]
    q = jnp.concatenate([q_nope, q_pe], axis=-1)
    nb = s // Q_BLOCK
    q_blocks = jnp.moveaxis(q.reshape(b, nb, Q_BLOCK, MLA_HEADS, QK_NOPE + QK_ROPE), 1, 0)
    key_chunk = jnp.arange(s) // CHUNK

    def attend_block(args):
        qb, bi = args
        q_chunk = (bi * Q_BLOCK + jnp.arange(Q_BLOCK)) // CHUNK
        mask = key_chunk[None, :] <= q_chunk[:, None]
        sc = jnp.einsum('bqhd,bkhd->bhqk', qb, k).astype(jnp.float32) * MLA_SCALE
        p = jax.nn.softmax(jnp.where(mask, sc, -jnp.inf), axis=-1).astype(v.dtype)
        return jnp.einsum('bhqk,bkhd->bqhd', p, v)

    o = lax.map(attend_block, (q_blocks, jnp.arange(nb)))
    o = jnp.moveaxis(o, 0, 1).reshape(b, s, MLA_GATE)
    return (o * jax.nn.silu(gate)) @ w_out, ckv, kpe


def mla_sample(h, ckv_past, kpe_past, w_in, q_norm, w_qb, kv_norm, w_kvb, w_out):
    b, l, _ = h.shape
    pos = ckv_past.shape[1] + jnp.arange(l)
    q_nope, q_pe, ckv, kpe, gate = mla_project(h, pos, w_in, q_norm, w_qb, kv_norm)
    ckv_all = jnp.concatenate([ckv_past, ckv], axis=1)
    kpe_all = jnp.concatenate([kpe_past, kpe], axis=1)
    w_uk, w_uv = w_kvb[..., :QK_NOPE], w_kvb[..., QK_NOPE:]
    q_lat = jnp.einsum('blhd,chd->blhc', q_nope, w_uk)
    sc = (jnp.einsum('blhc,btc->bhlt', q_lat, ckv_all)
          + jnp.einsum('blhr,btr->bhlt', q_pe, kpe_all)).astype(jnp.float32) * MLA_SCALE
    p = jax.nn.softmax(sc, axis=-1).astype(ckv_all.dtype)
    o_lat = jnp.einsum('bhlt,btc->blhc', p, ckv_all)
    o = jnp.einsum('blhc,chd->blhd', o_lat, w_uv).reshape(b, l, MLA_GATE)
    return (o * jax.nn.silu(gate)) @ w_out, ckv, kpe


def conformer_conv(h, hist, w_in, dw_w, dw_b, ln_g, ln_b, w_out):
    val, glu_gate, gate = jnp.split(h @ w_in, [D_CONV, 2 * D_CONV], axis=-1)
    u = val * jax.nn.sigmoid(glu_gate)
    up = jnp.concatenate([hist, u], axis=1)
    v = jax.nn.silu(layer_norm(causal_depthwise(up, dw_w, dw_b), ln_g, ln_b))
    return (v * jax.nn.silu(gate)) @ w_out, up[:, -(CONV_K - 1):]


def ssd_scan(x, dt, a, bm, cm, h0, block):
    b, l, g, e, p = x.shape
    n = bm.shape[-1]
    nc = l // block
    dtx = (x * dt[..., None]).reshape(b, nc, block, g, e, p)
    a_cs = jnp.cumsum((dt * a).reshape(b, nc, block, g, e), axis=2)
    bm = bm.reshape(b, nc, block, g, n)
    cm = cm.reshape(b, nc, block, g, n)
    acs_t = jnp.moveaxis(a_cs, 2, -1)
    causal = jnp.tril(jnp.ones((block, block), dtype=bool))
    seg = jnp.exp(jnp.where(causal, acs_t[..., :, None] - acs_t[..., None, :], -jnp.inf))
    cb = jnp.einsum('bclgn,bcsgn->bcgls', cm, bm)
    y_diag = jnp.einsum('bcgls,bcgels,bcsgep->bclgep', cb, seg, dtx)
    decay_to_end = jnp.exp(a_cs[:, :, -1:] - a_cs)
    chunk_states = jnp.einsum('bclgn,bclge,bclgep->bcgepn', bm, decay_to_end, dtx)
    chunk_decay = jnp.exp(a_cs[:, :, -1])

    def step(state, inp):
        st, dec = inp
        return state * dec[..., None, None] + st, state

    h_final, h_in = lax.scan(step, h0, (jnp.moveaxis(chunk_states, 1, 0), jnp.moveaxis(chunk_decay, 1, 0)))
    h_in = jnp.moveaxis(h_in, 0, 1)
    y_off = jnp.einsum('bclgn,bcgepn,bclge->bclgep', cm, h_in, jnp.exp(a_cs))
    return (y_diag + y_off).reshape(b, l, g, e, p), h_final


def mamba2_ssd(h, conv_hist, ssm_state, block, w_in, conv_w, conv_b, dt_bias, a_log, d_skip, gnorm, w_out):
    b, l, _ = h.shape
    e = SSM_HEADS // SSM_GROUPS
    z, xbc, dt = jnp.split(h @ w_in, [SSM_INNER, SSM_INNER + SSM_XBC], axis=-1)
    xp = jnp.concatenate([conv_hist, xbc], axis=1)
    xbc = jax.nn.silu(causal_depthwise(xp, conv_w, conv_b))
    x, bm, cm = jnp.split(xbc.astype(jnp.float32), [SSM_INNER, SSM_INNER + SSM_GROUPS * D_STATE], axis=-1)
    x = x.reshape(b, l, SSM_GROUPS, e, SSM_HEAD_DIM)
    bm = bm.reshape(b, l, SSM_GROUPS, D_STATE)
    cm = cm.reshape(b, l, SSM_GROUPS, D_STATE)
    dt = jax.nn.softplus(dt.astype(jnp.float32) + dt_bias.astype(jnp.float32)).reshape(b, l, SSM_GROUPS, e)
    a = -jnp.exp(a_log.astype(jnp.float32)).reshape(SSM_GROUPS, e)
    h0 = ssm_state.astype(jnp.float32).reshape(b, SSM_GROUPS, e, SSM_HEAD_DIM, D_STATE)
    y, h_final = ssd_scan(x, dt, a, bm, cm, h0, block)
    y = y + d_skip.astype(jnp.float32).reshape(SSM_GROUPS, e, 1) * x
    yz = (y.reshape(b, l, SSM_INNER) * jax.nn.silu(z.astype(jnp.float32))).reshape(b, l, SSM_GROUPS, -1)
    yz = yz * lax.rsqrt(jnp.mean(yz * yz, axis=-1, keepdims=True) + EPS)
    yn = yz.reshape(b, l, SSM_INNER).astype(h.dtype) * gnorm
    new_state = h_final.reshape(b, SSM_HEADS, SSM_HEAD_DIM, D_STATE).astype(h.dtype)
    return yn @ w_out, xp[:, -(SSM_CONV_K - 1):], new_state


def setup_inputs(seed: int = 0) -> dict:
    key = jax.random.key(seed)
    ks = iter(jax.random.split(key, 64))

    def nrm(shape, scale):
        return jax.random.normal(next(ks), shape, jnp.float32) * scale

    def gain(n):
        return 1.0 + 0.02 * jax.random.normal(next(ks), (n,), jnp.float32)

    def mla_params(pre):
        return {
            pre + 'norm': gain(D_MODEL),
            pre + 'w_in': nrm((D_MODEL, MLA_IN), D_MODEL ** -0.5),
            pre + 'q_norm': gain(Q_LORA),
            pre + 'w_qb': nrm((Q_LORA, MLA_HEADS, QK_NOPE + QK_ROPE), Q_LORA ** -0.5),
            pre + 'kv_norm': gain(KV_LORA),
            pre + 'w_kvb': nrm((KV_LORA, MLA_HEADS, QK_NOPE + V_HEAD), KV_LORA ** -0.5),
            pre + 'w_out': nrm((MLA_GATE, D_MODEL), MLA_GATE ** -0.5),
        }

    inp = {}
    inp['x_prompt'] = nrm((BATCH, SEQ, D_MODEL), 1.0)
    inp['x_sample'] = nrm((DEC_BATCH, DEC_SEQ, D_MODEL), 1.0)
    inp['cache_l0_ckv'] = nrm((DEC_BATCH, PAST_LEN, KV_LORA), 1.0)
    inp['cache_l0_kpe'] = nrm((DEC_BATCH, PAST_LEN, QK_ROPE), 1.0)
    inp['state_l1_conv'] = nrm((DEC_BATCH, CONV_K - 1, D_CONV), 0.5)
    inp['state_l2_conv'] = nrm((DEC_BATCH, SSM_CONV_K - 1, SSM_XBC), 1.0)
    inp['state_l2_ssm'] = nrm((DEC_BATCH, SSM_HEADS, SSM_HEAD_DIM, D_STATE), 0.1)
    inp['cache_l3_ckv'] = nrm((DEC_BATCH, PAST_LEN, KV_LORA), 1.0)
    inp['cache_l3_kpe'] = nrm((DEC_BATCH, PAST_LEN, QK_ROPE), 1.0)
    inp.update(mla_params('l0_'))
    inp['l1_norm'] = gain(D_MODEL)
    inp['l1_w_in'] = nrm((D_MODEL, 3 * D_CONV), D_MODEL ** -0.5)
    inp['l1_dw_w'] = nrm((CONV_K, D_CONV), CONV_K ** -0.5)
    inp['l1_dw_b'] = nrm((D_CONV,), 0.02)
    inp['l1_ln_g'] = gain(D_CONV)
    inp['l1_ln_b'] = nrm((D_CONV,), 0.02)
    inp['l1_w_out'] = nrm((D_CONV, D_MODEL), D_CONV ** -0.5)
    inp['l2_norm'] = gain(D_MODEL)
    inp['l2_w_in'] = nrm((D_MODEL, SSM_IN), D_MODEL ** -0.5)
    inp['l2_conv_w'] = nrm((SSM_CONV_K, SSM_XBC), SSM_CONV_K ** -0.5)
    inp['l2_conv_b'] = nrm((SSM_XBC,), 0.02)
    u = jax.random.uniform(next(ks), (SSM_HEADS,), jnp.float32)
    dt0 = jnp.exp(u * (np.log(DT_MAX) - np.log(DT_MIN)) + np.log(DT_MIN))
    inp['l2_dt_bias'] = dt0 + jnp.log(-jnp.expm1(-dt0))
    inp['l2_a_log'] = jnp.log(jax.random.uniform(next(ks), (SSM_HEADS,), jnp.float32, 1.0, 16.0))
    inp['l2_d_skip'] = 1.0 + 0.1 * jax.random.normal(next(ks), (SSM_HEADS,), jnp.float32)
    inp['l2_gnorm'] = gain(SSM_INNER)
    inp['l2_w_out'] = nrm((SSM_INNER, D_MODEL), SSM_INNER ** -0.5)
    inp.update(mla_params('l3_'))
    inp['final_norm'] = gain(D_MODEL)
    return inp


def reference(x_prompt, x_sample, cache_l0_ckv, cache_l0_kpe, state_l1_conv, state_l2_conv, state_l2_ssm,
              cache_l3_ckv, cache_l3_kpe,
              l0_norm, l0_w_in, l0_q_norm, l0_w_qb, l0_kv_norm, l0_w_kvb, l0_w_out,
              l1_norm, l1_w_in, l1_dw_w, l1_dw_b, l1_ln_g, l1_ln_b, l1_w_out,
              l2_norm, l2_w_in, l2_conv_w, l2_conv_b, l2_dt_bias, l2_a_log, l2_d_skip, l2_gnorm, l2_w_out,
              l3_norm, l3_w_in, l3_q_norm, l3_w_qb, l3_kv_norm, l3_w_kvb, l3_w_out,
              final_norm):
    layer_norms = [l0_norm, l1_norm, l2_norm, l3_norm]
    layer_weights = [
        (l0_w_in, l0_q_norm, l0_w_qb, l0_kv_norm, l0_w_kvb, l0_w_out),
        (l1_w_in, l1_dw_w, l1_dw_b, l1_ln_g, l1_ln_b, l1_w_out),
        (l2_w_in, l2_conv_w, l2_conv_b, l2_dt_bias, l2_a_log, l2_d_skip, l2_gnorm, l2_w_out),
        (l3_w_in, l3_q_norm, l3_w_qb, l3_kv_norm, l3_w_kvb, l3_w_out),
    ]
    layer_states = [
        (cache_l0_ckv, cache_l0_kpe),
        (state_l1_conv,),
        (state_l2_conv, state_l2_ssm),
        (cache_l3_ckv, cache_l3_kpe),
    ]
    xp, xs = x_prompt, x_sample
    new_prompt, new_sample = [], []
    for i in range(DEPTH):
        kind = i % N_MIXERS
        hp = rms_norm(xp, layer_norms[i])
        hs = rms_norm(xs, layer_norms[i])
        w = layer_weights[i]
        st = layer_states[i]
        if kind == 0:
            yp, *newp = mla_prompt(hp, *w)
            ys, *news = mla_sample(hs, st[0], st[1], *w)
        elif kind == 1:
            zh = jnp.zeros((hp.shape[0], CONV_K - 1, D_CONV), hp.dtype)
            yp, *newp = conformer_conv(hp, zh, *w)
            ys, *news = conformer_conv(hs, st[0], *w)
        else:
            zc = jnp.zeros((hp.shape[0], SSM_CONV_K - 1, SSM_XBC), hp.dtype)
            zs = jnp.zeros((hp.shape[0], SSM_HEADS, SSM_HEAD_DIM, D_STATE), hp.dtype)
            yp, *newp = mamba2_ssd(hp, zc, zs, SSD_BLOCK, *w)
            ys, *news = mamba2_ssd(hs, st[0], st[1], hs.shape[1], *w)
        xp = xp + yp
        xs = xs + ys
        new_prompt.append(newp)
        new_sample.append(news)
    y_prompt = rms_norm(xp, final_norm)
    y_sample = rms_norm(xs, final_norm)
    (l0_ckv_p, l0_kpe_p), (l1_conv_p,), (l2_conv_p, l2_ssm_p), (l3_ckv_p, l3_kpe_p) = new_prompt
    (l0_ckv_s, l0_kpe_s), (l1_conv_s,), (l2_conv_s, l2_ssm_s), (l3_ckv_s, l3_kpe_s) = new_sample
    return (y_prompt, y_sample,
            l0_ckv_p, l0_kpe_p, l0_ckv_s, l0_kpe_s,
            l1_conv_p, l1_conv_s,
            l2_conv_p, l2_ssm_p, l2_conv_s, l2_ssm_s,
            l3_ckv_p, l3_kpe_p, l3_ckv_s, l3_kpe_s)
```

```python
import numpy as np
from contextlib import ExitStack
import concourse.bass as bass
import concourse.mybir as mybir
from concourse.bass_utils import run_bass_kernel_spmd

F32 = mybir.dt.float32
BF16 = mybir.dt.bfloat16
ALU = mybir.AluOpType
AF = mybir.ActivationFunctionType
AX = mybir.AxisListType

NCORES = 8
D = 2048
KC = 16
NP = 1024
NS = 16
NT = NP + NS
PAST = 4096
EPS = 1e-6
NEG = -30000.0
MLA_SCALE = 192 ** -0.5
GROUPS = [[0, 1, 2, 3], [4, 5, 6, 7]]


class Buf:
    __slots__ = ("name", "w", "r")

    def __init__(self, name=""):
        self.name = name
        self.w = None
        self.r = []


class Eng:
    def __init__(self, prog, name):
        self.prog = prog
        self.name = name
        self.ops = []
        self.count = 0
        self.waited = {}
        self.dma_n = 0
        self.dma_vals = {}
        self.pending = []

    def _need(self, tok, waits, same_ok):
        if tok is None:
            return
        eng, key, val = tok
        if eng is self and same_ok:
            return
        if self.waited.get(key, 0) >= val:
            return
        self.waited[key] = val
        waits.append((key, val))

    def op(self, fn, reads=(), writes=(), dma=False, cc=False):
        waits = []
        for key, val in self.pending:
            if self.waited.get(key, 0) < val:
                self.waited[key] = val
                waits.append((key, val))
        self.pending = []
        is_pe = self.name == "tensor"
        asyncop = dma or cc
        for b in reads:
            self._need(b.w, waits, same_ok=(is_pe and not asyncop))
        for b in writes:
            self._need(b.w, waits, same_ok=(not asyncop))
            for t in b.r:
                self._need(t, waits, same_ok=(not asyncop))
        if cc:
            key = ("cc", self.name)
            prev = self.dma_vals.get(key, 0)
            if prev and self.waited.get(key, 0) < prev:
                self.waited[key] = prev
                waits.append((key, prev))
            val = prev + 1
            self.dma_vals[key] = val
            tok = (None, key, val)
            inc = (key, 1)
        elif dma:
            k = self.dma_n % self.prog.ndma_sems
            self.dma_n += 1
            key = ("dma", self.name, k)
            prev = self.dma_vals.get(key, 0)
            if prev and self.waited.get(key, 0) < prev:
                self.waited[key] = prev
                waits.append((key, prev))
            val = prev + 16
            self.dma_vals[key] = val
            tok = (None, key, val)
            inc = (key, 16)
        else:
            self.count += 1
            key = ("eng", self.name)
            tok = (self, key, self.count)
            inc = (key, 1)
        self.ops.append((fn, waits, inc))
        for b in reads:
            b.r.append(tok)
        for b in writes:
            b.w = tok
            b.r = []
        return tok


class Prog:
    def __init__(self, nc, ndma_sems=8):
        self.nc = nc
        self.ndma_sems = ndma_sems
        self.pe = Eng(self, "tensor")
        self.act = Eng(self, "scalar")
        self.dve = Eng(self, "vector")
        self.pool = Eng(self, "gpsimd")
        self.sp = Eng(self, "sync")
        self.engs = [self.pe, self.act, self.dve, self.pool, self.sp]

    def all_tokens(self):
        toks = []
        for e in self.engs:
            if e.count:
                toks.append((("eng", e.name), e.count))
            for key, val in e.dma_vals.items():
                toks.append((key, val))
        return toks

    def barrier(self):
        toks = self.all_tokens()
        for e in self.engs:
            e.pending = list(toks)

    def emit(self, stack):
        nc = self.nc
        sems = {}
        for e in self.engs:
            sems[("eng", e.name)] = stack.enter_context(nc.semaphore("s_" + e.name))
            sems[("cc", e.name)] = stack.enter_context(nc.semaphore("c_" + e.name))
            for k in range(self.ndma_sems):
                sems[("dma", e.name, k)] = stack.enter_context(nc.semaphore("d_%s_%d" % (e.name, k)))
        fw = []
        for key, val in self.all_tokens():
            if self.sp.waited.get(key, 0) < val:
                self.sp.waited[key] = val
                fw.append((key, val))
        self.sp.ops.append((None, fw, None))
        block = stack.enter_context(nc.Block())

        def mk(e):
            def body(eng):
                for fn, waits, inc in e.ops:
                    for key, val in waits:
                        eng.wait_ge(sems[key], val)
                    if fn is None:
                        continue
                    ins = fn(eng)
                    ins.then_inc(sems[inc[0]], inc[1])
            return body

        block.tensor(mk(self.pe))
        block.scalar(mk(self.act))
        block.vector(mk(self.dve))
        block.gpsimd(mk(self.pool))
        block.sync(mk(self.sp))


class T:
    def __init__(self, t, name=""):
        self.t = t
        self.b = Buf(name)

    def __getitem__(self, idx):
        return self.t[idx]


class Cols:
    def __init__(self):
        self.off = {}
        self.n = 0
        self.parts = []

    def add(self, name, arr):
        arr = np.ascontiguousarray(arr, dtype=np.float32)
        assert arr.shape[0] == 128, (name, arr.shape)
        arr = arr.reshape(128, -1)
        self.off[name] = (self.n, arr.shape[1])
        self.n += arr.shape[1]
        self.parts.append(arr)

    def build(self):
        return np.ascontiguousarray(np.concatenate(self.parts, axis=1))


def fvec(v):
    v = np.asarray(v, np.float32)
    return np.ascontiguousarray(v.reshape(-1, 128).T)


def brow(v):
    v = np.asarray(v, np.float32).reshape(1, -1)
    return np.ascontiguousarray(np.broadcast_to(v, (128, v.shape[1])))


def rope_tables(pos):
    half = 32
    freqs = (10000.0 ** (-np.arange(half, dtype=np.float32) / np.float32(half))).astype(np.float32)
    ang = pos.astype(np.float32)[:, None] * freqs[None, :]
    return np.cos(ang).astype(np.float32), np.sin(ang).astype(np.float32)


def build_consts(core, inp):
    j = core % 4
    C = Cols()
    M = Cols()
    for L in (0, 3):
        C.add("norm%d" % L, fvec(inp["l%d_norm" % L]))
        C.add("qnorm%d" % L, fvec(inp["l%d_q_norm" % L]))
        M.add("kvnorm%d" % L, brow(inp["l%d_kv_norm" % L]))
    C.add("fnorm", fvec(inp["final_norm"]))
    C.add("norm1", fvec(inp["l1_norm"]))
    dww = np.asarray(inp["l1_dw_w"], np.float32)
    C.add("dww", np.ascontiguousarray(dww.reshape(31, 16, 128).transpose(2, 1, 0)))
    C.add("dwb", fvec(inp["l1_dw_b"]))
    C.add("lng", fvec(inp["l1_ln_g"]))
    C.add("lnb", fvec(inp["l1_ln_b"]))
    C.add("norm2", fvec(inp["l2_norm"]))
    cw = np.asarray(inp["l2_conv_w"], np.float32)
    C.add("cw2", np.ascontiguousarray(cw.reshape(4, 48, 128).transpose(2, 1, 0)))
    C.add("cb2", fvec(inp["l2_conv_b"]))
    C.add("dtb", brow(inp["l2_dt_bias"]))
    C.add("alog", brow(inp["l2_a_log"]))
    C.add("dskip", brow(inp["l2_d_skip"]))
    C.add("gn2", fvec(inp["l2_gnorm"]))
    selp = np.zeros((128, 4), np.float32)
    if j > 0:
        selp[:, j - 1] = 1.0
    C.add("selp", selp)
    val = np.zeros((128, 3), np.float32)
    for r in range(3):
        val[:, r] = 1.0 if r < j else 0.0
    C.add("valid", val)
    C.add("nvalid", 1.0 - val)
    pos = np.concatenate([j * NP + np.arange(NP), PAST + np.arange(NS)])
    cos, sin = rope_tables(pos)
    cosT = np.zeros((128, 9, 32), np.float32)
    sinT = np.zeros((128, 9, 32), np.float32)
    for t in range(8):
        cosT[:, t] = cos[t * 128:(t + 1) * 128]
        sinT[:, t] = sin[t * 128:(t + 1) * 128]
    cosT[:NS, 8] = cos[NP:]
    sinT[:NS, 8] = sin[NP:]
    M.add("cosT", cosT)
    M.add("sinT", sinT)
    cosF = np.zeros((128, NT), np.float32)
    sinF = np.zeros((128, NT), np.float32)
    cosF[0:32] = cos.T
    cosF[32:64] = cos.T
    sinF[0:32] = sin.T
    sinF[32:64] = sin.T
    M.add("cosF", cosF)
    M.add("sinF", sinF)
    ab = np.zeros((128, 3), np.float32)
    for r in range(3):
        ab[:, r] = 0.0 if r < j else NEG
    M.add("abias", ab)
    return C, M


class K:
    def __init__(self, nc, coff, ncst, moff, nmst):
        self.nc = nc
        self.P = Prog(nc)
        self.coff = coff
        self.ncst = ncst
        self.moff = moff
        self.nmst = nmst
        self.psn = 0

    def sb(self, st, name, shape, dt=F32):
        return T(st.enter_context(self.nc.sbuf_tensor("sb_" + name, shape, dt)), name)

    def cs(self, name, k0=0, k1=None):
        if name in self.coff:
            o, n = self.coff[name]
            t = self.cst
        else:
            o, n = self.moff[name]
            t = self.cstm
        if k1 is None:
            k1 = n
        return t[:, o + k0:o + k1]

    def bank(self):
        b = self.banks[self.psn % 6]
        self.psn += 1
        return b

    def build(self):
        nc = self.nc
        P = self.P
        dt = nc.dram_tensor
        io = {}

        def din(name, shape):
            io[name] = dt(name, shape, F32, kind="ExternalInput").ap()

        def dout(name, shape):
            io[name] = dt(name, shape, F32, kind="ExternalOutput").ap()

        din("xp", [NP, D])
        din("xs", [NS, D])
        din("cst", [128, self.ncst])
        din("cstm", [128, self.nmst])
        for L in (0, 3):
            if L not in LAYERS:
                continue
            din("c%d_ckv" % L, [PAST, 512])
            din("c%d_kpe" % L, [PAST, 64])
            din("l%d_w_in" % L, [D, 3136])
            din("l%d_w_qb" % L, [512, 16 * 192])
            din("l%d_w_kvb" % L, [512, 16 * 256])
            din("l%d_w_out" % L, [D, D])
            dout("o%d_ckv_p" % L, [NP, 512])
            dout("o%d_kpe_p" % L, [NP, 64])
            dout("o%d_ckv_s" % L, [NS, 512])
            dout("o%d_kpe_s" % L, [NS, 64])
        dout("yp", [NP, D])
        dout("ys", [NS, D])
        if 1 in LAYERS:
            din("st1", [30, D])
            din("l1_w_in", [D, 6144])
            din("l1_w_out", [D, D])
            dout("conv1p", [30, D])
            dout("conv1s", [30, D])
        if 2 in LAYERS:
            din("st2c", [3, 6144])
            din("st2s", [4096, 128])
            din("l2_w_in", [D, 10304])
            din("l2_w_out", [4096, D])
            dout("conv2p", [3, 6144])
            dout("conv2s", [3, 6144])
            dout("ssm2p", [4096, 128])
            dout("ssm2s", [4096, 128])
        self.todo = []
        for L in (0, 3):
            if L not in LAYERS:
                for nm, shp in (("o%d_ckv_p" % L, [NP, 512]), ("o%d_kpe_p" % L, [NP, 64]),
                                ("o%d_ckv_s" % L, [NS, 512]), ("o%d_kpe_s" % L, [NS, 64])):
                    dout(nm, shp)
                    self.todo.append((nm, shp))
        if 1 not in LAYERS:
            for nm, shp in (("conv1p", [30, D]), ("conv1s", [30, D])):
                dout(nm, shp)
                self.todo.append((nm, shp))
        if 2 not in LAYERS:
            for nm, shp in (("conv2p", [3, 6144]), ("conv2s", [3, 6144]), ("ssm2p", [4096, 128]), ("ssm2s", [4096, 128])):
                dout(nm, shp)
                self.todo.append((nm, shp))
        self.io = io
        self.gin = dt("gin", [128, 5120], BF16)
        self.gout = dt("gout", [4 * 128, 5120], BF16)

        with ExitStack() as st:
            self.st = st
            self.cst_t = self.sb(st, "cst", [128, self.ncst])
            self.cst = self.cst_t.t
            self.xT = self.sb(st, "xT", [128, KC, NT])
            self.ident = self.sb(st, "ident", [128, 128])
            self.identb = self.sb(st, "identb", [128, 128], BF16)
            self.ones = self.sb(st, "ones", [128, 128])
            self.onesb = self.sb(st, "onesb", [128, 128], BF16)
            self.banks = [T(st.enter_context(nc.psum_tensor("ps%d" % i, [128, 512], F32)), "ps%d" % i)
                          for i in range(8)]
            P.sp.op(lambda e: e.dma_start(out=self.cst[:], in_=io["cst"]), writes=[self.cst_t.b], dma=True)
            P.pool.op(lambda e: e.memset(self.ident[:], 0.0), writes=[self.ident.b])
            P.pool.op(lambda e: e.affine_select(out=self.ident[:], in_=self.ident[:], pattern=[[-1, 128]],
                                                compare_op=ALU.not_equal, fill=1.0, base=0,
                                                channel_multiplier=1),
                      reads=[self.ident.b], writes=[self.ident.b])
            P.pool.op(lambda e: e.tensor_copy(out=self.identb[:], in_=self.ident[:]),
                      reads=[self.ident.b], writes=[self.identb.b])
            P.pool.op(lambda e: e.memset(self.ones[:], 1.0), writes=[self.ones.b])
            P.pool.op(lambda e: e.memset(self.onesb[:], 1.0), writes=[self.onesb.b])
            self.load_x()
            for L in LAYERS:
                if L in (0, 3):
                    self.mla_layer(L)
                elif L == 1:
                    self.conv_layer()
                elif L == 2:
                    self.ssd_layer()
            self.final_norm()
            self.placeholders()
            P.emit(st)

    def load_x(self):
        P, io = self.P, self.io
        with ExitStack() as st:
            xin = [self.sb(st, "xin%d" % i, [128, D]) for i in range(2)]
            for t in range(9):
                rows = 128 if t < 8 else NS
                src = io["xp"][t * 128:(t + 1) * 128, :] if t < 8 else io["xs"]
                xi = xin[t % 2]
                P.sp.op(lambda e, xi=xi, src=src, rows=rows: e.dma_start(out=xi[0:rows, :], in_=src),
                        writes=[xi.b], dma=True)
                for k4 in range(4):
                    pb = self.bank()
                    for kk in range(4):
                        k = k4 * 4 + kk
                        P.pe.op(lambda e, pb=pb, xi=xi, k=k, kk=kk, rows=rows: e.transpose(
                            pb[:, kk * 128:kk * 128 + rows], xi[0:rows, k * 128:(k + 1) * 128],
                            self.ident[0:rows, 0:rows]),
                            reads=[xi.b, self.ident.b], writes=[pb.b])
                    eng = P.dve if k4 % 2 == 0 else P.act
                    dst = self.xT[:, k4 * 4:(k4 + 1) * 4, t * 128:t * 128 + rows]
                    srcp = pb[:].rearrange("p (a b) -> p a b", a=4)[:, :, 0:rows]
                    if eng is P.dve:
                        eng.op(lambda e, dst=dst, srcp=srcp: e.tensor_copy(out=dst, in_=srcp),
                               reads=[pb.b], writes=[self.xT.b])
                    else:
                        eng.op(lambda e, dst=dst, srcp=srcp: e.copy(out=dst, in_=srcp),
                               reads=[pb.b], writes=[self.xT.b])
            P.barrier()

    def rms_F(self, hT, c0, n, gname, sq, rstd, off=0, src=None):
        src = self.xT if src is None else src
        return self._rms_F(hT, c0, n, gname, sq, rstd, off, src)

    def _rms_F(self, hT, c0, n, gname, sq, rstd, off, src):
        P = self.P
        pb = self.bank()
        for k in range(KC):
            P.act.op(lambda e, k=k: e.activation(out=sq[:, 0:n], in_=src[:, k, c0:c0 + n], func=AF.Square),
                     reads=[src.b], writes=[sq.b])
            P.pe.op(lambda e, k=k: e.matmul(pb[:, 0:n], lhsT=self.ones[:], rhs=sq[:, 0:n],
                                           start=(k == 0), stop=(k == KC - 1)),
                    reads=[sq.b, self.ones.b], writes=[pb.b])
        P.dve.op(lambda e: e.tensor_scalar(out=rstd[:, 0:n], in0=pb[:, 0:n], scalar1=1.0 / D, scalar2=EPS,
                                           op0=ALU.mult, op1=ALU.add),
                 reads=[pb.b], writes=[rstd.b])
        P.act.op(lambda e: e.activation(out=rstd[:, 0:n], in_=rstd[:, 0:n], func=AF.Sqrt),
                 reads=[rstd.b], writes=[rstd.b])
        P.dve.op(lambda e: e.reciprocal(out=rstd[:, 0:n], in_=rstd[:, 0:n]), reads=[rstd.b], writes=[rstd.b])
        if hT is None:
            return
        for k in range(KC):
            eng = P.dve
            eng.op(lambda e, k=k: e.scalar_tensor_tensor(out=hT[:, k, off:off + n], in0=src[:, k, c0:c0 + n],
                                                          scalar=self.cs(gname, k, k + 1), in1=rstd[:, 0:n],
                                                          op0=ALU.mult, op1=ALU.mult),
                   reads=[src.b, rstd.b, self.cst_t.b], writes=[hT.b])


    def kv_from_lat(self, wkv, latm, src_b, kT, vS, ntok=1024):
        P = self.P
        for (c, nn) in self.subtiles(0, ntok):
            ps = self.bank()
            for m in range(4):
                P.pe.op(lambda e, m=m, c=c, nn=nn, ps=ps: e.matmul(ps[:, 0:nn], lhsT=wkv[:, m, 0:128], rhs=latm(m, c, c + nn),
                                                                  start=(m == 0), stop=(m == 3)),
                        reads=[wkv.b, src_b], writes=[ps.b])
            P.act.op(lambda e, c=c, nn=nn, ps=ps: e.copy(out=kT[:, c:c + nn], in_=ps[:, 0:nn]), reads=[ps.b], writes=[kT.b])
        ntc = (ntok + 127) // 128
        for tc4 in range((ntc + 3) // 4):
            ps = self.bank()
            cnt = min(4, ntc - tc4 * 4)
            rows = min(128, ntok)
            for tq in range(cnt):
                tc = tc4 * 4 + tq
                for m in range(4):
                    P.pe.op(lambda e, m=m, tc=tc, tq=tq, ps=ps, rows=rows: e.matmul(
                        ps[0:rows, tq * 128:(tq + 1) * 128], lhsT=latm(m, tc * 128, tc * 128 + rows), rhs=wkv[:, m, 128:256],
                        start=(m == 0), stop=(m == 3)), reads=[wkv.b, src_b], writes=[ps.b])
            P.dve.op(lambda e, tc4=tc4, cnt=cnt, ps=ps, rows=rows: e.tensor_copy(
                out=vS[0:rows, tc4 * 4:tc4 * 4 + cnt, :],
                in_=ps[0:rows, 0:cnt * 128].rearrange("p (a b) -> p a b", a=cnt)),
                reads=[ps.b], writes=[vS.b])

    def mla_attention(self, L, qanT, lat, sgT):
        P, io, nc = self.P, self.io, self.nc
        gins = [nc.dram_tensor("gin%d_%d" % (L, a), [128, NP], BF16) for a in range(5)]
        gouts = [nc.dram_tensor("gout%d_%d" % (L, a), [4 * 128, NP], BF16) for a in range(5)]
        b_gin = [Buf() for a in range(5)]
        b_gout = [Buf() for a in range(5)]
        for a in range(5):
            P.sp.op(lambda e, a=a: e.dma_start(out=gins[a][:], in_=lat[:, a, 0:NP]),
                    reads=[lat.b], writes=[b_gin[a]], dma=True)
            P.pool.op(lambda e, a=a: e.collective_compute("AllGather", ALU.bypass, replica_groups=GROUPS,
                                                          ins=[gins[a][:]], outs=[gouts[a][:]]),
                      reads=[b_gin[a]], writes=[b_gout[a]], cc=True)
        w_qb = io["l%d_w_qb" % L]
        w_kvb = io["l%d_w_kvb" % L]
        obank = [self.banks[6], self.banks[7]]
        with ExitStack() as stA:
            hb = []
            for i in range(2):
                hb.append(dict(
                    wq=self.sb(stA, "a%d_wq%d" % (L, i), [128, 4, 192], BF16),
                    wkv=self.sb(stA, "a%d_wkv%d" % (L, i), [128, 4, 256], BF16),
                    wqr=self.sb(stA, "a%d_wqr%d" % (L, i), [128, 4, 64], BF16),
                    qn=self.sb(stA, "a%d_qn%d" % (L, i), [128, NT], BF16),
                    qpe=self.sb(stA, "a%d_qpe%d" % (L, i), [128, NT], BF16)))
            t1 = self.sb(stA, "a%d_t1" % L, [128, 512])
            t2 = self.sb(stA, "a%d_t2" % L, [128, 512])
            segb = [(self.sb(stA, "a%d_kT%d" % (L, i), [128, 1024], BF16),
                     self.sb(stA, "a%d_vS%d" % (L, i), [128, 8, 128], BF16)) for i in range(2)]
            pTs = [self.sb(stA, "a%d_pT%d" % (L, i), [128, 512], BF16) for i in range(3)]
            acc = [self.sb(stA, "a%d_acc%d" % (L, i), [128, 512]) for i in range(2)]
            rinv = self.sb(stA, "a%d_rinv" % L, [128, 512])
            otmp = self.sb(stA, "a%d_otmp" % L, [128, 512])
            qs_n = self.sb(stA, "a%d_qsn" % L, [128, 16, NS], BF16)
            qs_pe = self.sb(stA, "a%d_qspe" % L, [128, 16, NS], BF16)
            cosF = self.cs("cosF")
            sinF = self.cs("sinF")
            pcount = 0
            with ExitStack() as stG:
                gat = self.sb(stG, "a%d_gat" % L, [128, 3, 5, NP], BF16)
                for r in range(3):
                    for a in range(5):
                        P.sp.op(lambda e, r=r, a=a: e.dma_start(out=gat[:, r, a, :], in_=gouts[a][r * 128:(r + 1) * 128, :]),
                                reads=[b_gout[a]], writes=[gat.b], dma=True)
                for h in range(16):
                    B = hb[h % 2]
                    wq, wkv, wqr, qn, qpe = B["wq"], B["wkv"], B["wqr"], B["qn"], B["qpe"]
                    P.pool.op(lambda e, h=h, wq=wq: e.dma_start(
                        out=wq[:], in_=w_qb[:, h * 192:(h + 1) * 192].rearrange("(k p) n -> p k n", p=128)),
                        writes=[wq.b], dma=True)
                    P.pool.op(lambda e, h=h, wkv=wkv: e.dma_start(
                        out=wkv[:], in_=w_kvb[:, h * 256:(h + 1) * 256].rearrange("(k p) n -> p k n", p=128)),
                        writes=[wkv.b], dma=True)
                    P.act.op(lambda e, wq=wq, wqr=wqr: e.mul(out=wqr[:, :, 0:32], in_=wq[:, :, 160:192], mul=-1.0),
                             reads=[wq.b], writes=[wqr.b])
                    P.dve.op(lambda e, wq=wq, wqr=wqr: e.tensor_copy(out=wqr[:, :, 32:64], in_=wq[:, :, 128:160]),
                             reads=[wq.b], writes=[wqr.b])
                    for (c, nn) in self.subtiles(0, NT):
                        ps = self.bank()
                        ps2 = self.bank()
                        ps3 = self.bank()
                        for m in range(4):
                            P.pe.op(lambda e, m=m, c=c, nn=nn, ps=ps, wq=wq: e.matmul(
                                ps[:, 0:nn], lhsT=wq[:, m, 0:128], rhs=qanT[:, m, c:c + nn], start=(m == 0), stop=(m == 3)),
                                reads=[wq.b, qanT.b], writes=[ps.b])
                        for m in range(4):
                            P.pe.op(lambda e, m=m, c=c, nn=nn, ps2=ps2, wq=wq: e.matmul(
                                ps2[0:64, 0:nn], lhsT=wq[:, m, 128:192], rhs=qanT[:, m, c:c + nn], start=(m == 0), stop=(m == 3)),
                                reads=[wq.b, qanT.b], writes=[ps2.b])
                        for m in range(4):
                            P.pe.op(lambda e, m=m, c=c, nn=nn, ps3=ps3, wqr=wqr: e.matmul(
                                ps3[0:64, 0:nn], lhsT=wqr[:, m, :], rhs=qanT[:, m, c:c + nn], start=(m == 0), stop=(m == 3)),
                                reads=[wqr.b, qanT.b], writes=[ps3.b])
                        P.act.op(lambda e, c=c, nn=nn, ps=ps, qn=qn: e.copy(out=qn[:, c:c + nn], in_=ps[:, 0:nn]),
                                 reads=[ps.b], writes=[qn.b])
                        P.dve.op(lambda e, c=c, nn=nn, ps2=ps2: e.tensor_tensor(out=t1[0:64, 0:nn], in0=ps2[0:64, 0:nn],
                                                                                in1=cosF[0:64, c:c + nn], op=ALU.mult),
                                 reads=[ps2.b, self.cst_t.b], writes=[t1.b])
                        P.dve.op(lambda e, c=c, nn=nn, ps3=ps3: e.tensor_tensor(out=t2[0:64, 0:nn], in0=ps3[0:64, 0:nn],
                                                                                in1=sinF[0:64, c:c + nn], op=ALU.mult),
                                 reads=[ps3.b, self.cst_t.b], writes=[t2.b])
                        P.dve.op(lambda e, c=c, nn=nn, qpe=qpe: e.tensor_tensor(out=qpe[0:64, c:c + nn], in0=t1[0:64, 0:nn],
                                                                                in1=t2[0:64, 0:nn], op=ALU.add),
                                 reads=[t1.b, t2.b], writes=[qpe.b])
                    P.act.op(lambda e, h=h, qn=qn: e.copy(out=qs_n[:, h, :], in_=qn[:, NP:NT]), reads=[qn.b], writes=[qs_n.b])
                    P.act.op(lambda e, h=h, qpe=qpe: e.copy(out=qs_pe[0:64, h, :], in_=qpe[0:64, NP:NT]),
                             reads=[qpe.b], writes=[qs_pe.b])
                    pend = None
                    for si in range(4):
                        own = si == 3
                        r = si
                        kT, vS = segb[(h * 4 + si) % 2]
                        if own:
                            latm = lambda m, a, b: lat[:, m, a:b]
                            kpes = lambda a, b: lat[0:64, 4, a:b]
                            src_b = lat.b
                        else:
                            latm = lambda m, a, b, r=r: gat[:, r, m, a:b]
                            kpes = lambda a, b, r=r: gat[0:64, r, 4, a:b]
                            src_b = gat.b
                        self.kv_from_lat(wkv, latm, src_b, kT, vS)
                        for qt in range(2):
                            for kt in range(8):
                                c_lo = 0
                                partial = False
                                if own:
                                    if kt >= 4 * (qt + 1):
                                        continue
                                    bq = kt - 4 * qt
                                    if bq >= 0:
                                        c_lo = 128 * bq
                                        partial = True
                                ps_s = self.bank()
                                q0 = qt * 512 + c_lo
                                q1 = qt * 512 + 512
                                P.pe.op(lambda e, ps_s=ps_s, c_lo=c_lo, kt=kt, q0=q0, q1=q1, kT=kT, qn=qn: e.matmul(
                                    ps_s[:, c_lo:512], lhsT=kT[:, kt * 128:(kt + 1) * 128], rhs=qn[:, q0:q1],
                                    start=True, stop=False), reads=[kT.b, qn.b], writes=[ps_s.b])
                                P.pe.op(lambda e, ps_s=ps_s, c_lo=c_lo, kt=kt, q0=q0, q1=q1, kpes=kpes, qpe=qpe: e.matmul(
                                    ps_s[:, c_lo:512], lhsT=kpes(kt * 128, (kt + 1) * 128), rhs=qpe[0:64, q0:q1],
                                    start=False, stop=True), reads=[src_b, qpe.b], writes=[ps_s.b])
                                pT = pTs[pcount % 3]
                                pcount += 1
                                bias = 0.0 if own else self.cs("abias", r, r + 1)
                                P.act.op(lambda e, ps_s=ps_s, c_lo=c_lo, pT=pT, bias=bias: e.activation(
                                    out=pT[:, c_lo:512], in_=ps_s[:, c_lo:512], func=AF.Exp, bias=bias, scale=MLA_SCALE),
                                    reads=[ps_s.b, self.cst_t.b], writes=[pT.b])
                                if partial:
                                    P.pool.op(lambda e, pT=pT, c_lo=c_lo: e.memset(pT[64:128, c_lo:c_lo + 64], 0.0),
                                              writes=[pT.b])
                                first = (si == 0 and kt == 0)
                                last = (own and kt == 4 * qt + 3)
                                ob = obank[qt]
                                ac = acc[qt]
                                use_pool = (pcount % 2 == 0)

                                def pv_step(ob=ob, c_lo=c_lo, vS=vS, kt=kt, pT=pT, first=first, last=last, ac=ac, use_pool=use_pool):
                                    P.pe.op(lambda e: e.matmul(ob[:, c_lo:512], lhsT=vS[:, kt, :], rhs=pT[:, c_lo:512], start=first, stop=last),
                                            reads=[vS.b, pT.b], writes=[ob.b])
                                    if first:
                                        P.pool.op(lambda e: e.tensor_copy(out=ac[:], in_=pT[:]), reads=[pT.b], writes=[ac.b])
                                    else:
                                        eng = P.pool if use_pool else P.dve
                                        eng.op(lambda e: e.tensor_tensor(out=ac[:, c_lo:512], in0=ac[:, c_lo:512], in1=pT[:, c_lo:512], op=ALU.add),
                                               reads=[pT.b, ac.b], writes=[ac.b])
                                if pend is not None:
                                    pend()
                                pend = pv_step
                    if pend is not None:
                        pend()
                        pend = None
                    for qt in range(2):
                        ps = self.bank()
                        P.pe.op(lambda e, ps=ps, qt=qt: e.matmul(ps[:], lhsT=self.ones[:], rhs=acc[qt][:], start=True, stop=True),
                                reads=[self.ones.b, acc[qt].b], writes=[ps.b])
                        P.dve.op(lambda e, ps=ps: e.reciprocal(out=rinv[:], in_=ps[:]), reads=[ps.b], writes=[rinv.b])
                        P.dve.op(lambda e, qt=qt: e.tensor_tensor(out=otmp[:], in0=obank[qt][:], in1=rinv[:], op=ALU.mult),
                                 reads=[obank[qt].b, rinv.b], writes=[otmp.b])
                        P.dve.op(lambda e, qt=qt, h=h: e.tensor_tensor(out=sgT[:, h, qt * 512:(qt + 1) * 512], in0=otmp[:],
                                                                       in1=sgT[:, h, qt * 512:(qt + 1) * 512], op=ALU.mult),
                                 reads=[otmp.b, sgT.b], writes=[sgT.b])
                P.barrier()
            with ExitStack() as stS:
                cts = [self.sb(stS, "a%d_ct%d" % (L, i), [128, 576]) for i in range(2)]
                ctbs = [self.sb(stS, "a%d_ctb%d" % (L, i), [128, 512], BF16) for i in range(2)]
                lTs = [self.sb(stS, "a%d_lT%d" % (L, i), [128, 5, 128], BF16) for i in range(2)]
                wuks = [self.sb(stS, "a%d_wuk%d" % (L, i), [128, 4, 128]) for i in range(2)]
                wukT = self.sb(stS, "a%d_wukT" % L, [128, 512], BF16)
                qlT = self.sb(stS, "a%d_qlT" % L, [128, 4, 256], BF16)
                olT = self.sb(stS, "a%d_olT" % L, [128, 4, 256], BF16)
                accs = self.sb(stS, "a%d_accs" % L, [128, 256])
                ckv_c = io["c%d_ckv" % L]
                kpe_c = io["c%d_kpe" % L]
                oacc = [self.banks[6], self.banks[7]]
                P.pool.op(lambda e: e.memset(accs[:], 0.0), writes=[accs.b])
                for i in range(2):
                    P.pool.op(lambda e, i=i: e.memset(lTs[i][:, 4, :], 0.0), writes=[lTs[i].b])
                for h in range(16):
                    wuk = wuks[h % 2]
                    P.sp.op(lambda e, h=h, wuk=wuk: e.dma_start(
                        out=wuk[:], in_=w_kvb[:, h * 256:h * 256 + 128].rearrange("(k p) n -> p k n", p=128)),
                        writes=[wuk.b], dma=True)
                    pt = self.bank()
                    for m in range(4):
                        P.pe.op(lambda e, m=m, pt=pt, wuk=wuk: e.transpose(pt[:, m * 128:(m + 1) * 128], wuk[:, m, :], self.ident[:]),
                                reads=[wuk.b, self.ident.b], writes=[pt.b])
                    P.act.op(lambda e, pt=pt: e.copy(out=wukT[:], in_=pt[:]), reads=[pt.b], writes=[wukT.b])
                    pq = self.bank()
                    for m in range(4):
                        P.pe.op(lambda e, m=m, pq=pq, h=h: e.matmul(pq[:, m * NS:(m + 1) * NS], lhsT=wukT[:, m * 128:(m + 1) * 128],
                                                                    rhs=qs_n[:, h, :], start=True, stop=True),
                                reads=[wukT.b, qs_n.b], writes=[pq.b])
                    P.dve.op(lambda e, pq=pq, h=h: e.tensor_copy(out=qlT[:, :, h * NS:(h + 1) * NS],
                                                                 in_=pq[:, 0:4 * NS].rearrange("p (a b) -> p a b", a=4)),
                             reads=[pq.b], writes=[qlT.b])
                qpe_flat = qs_pe[0:64, :, :].rearrange("p a b -> p (a b)")
                pend = None
                for ti in range(33):
                    own = ti == 32
                    rows = NS if own else 128
                    if not own:
                        ct = cts[ti % 2]
                        ctb = ctbs[ti % 2]
                        lT = lTs[ti % 2]
                        r0 = ti * 128
                        P.sp.op(lambda e, ct=ct, r0=r0: e.dma_start(out=ct[:, 0:512], in_=ckv_c[r0:r0 + 128, :]), writes=[ct.b], dma=True)
                        P.sp.op(lambda e, ct=ct, r0=r0: e.dma_start(out=ct[:, 512:576], in_=kpe_c[r0:r0 + 128, :]), writes=[ct.b], dma=True)
                        pt = self.bank()
                        for m in range(4):
                            P.pe.op(lambda e, m=m, pt=pt, ct=ct: e.transpose(pt[:, m * 128:(m + 1) * 128], ct[:, m * 128:(m + 1) * 128], self.ident[:]),
                                    reads=[ct.b, self.ident.b], writes=[pt.b])
                        P.act.op(lambda e, pt=pt, lT=lT: e.copy(out=lT[:, 0:4, :], in_=pt[:].rearrange("p (a b) -> p a b", a=4)),
                                 reads=[pt.b], writes=[lT.b])
                        pt2 = self.bank()
                        P.pe.op(lambda e, pt2=pt2, ct=ct: e.transpose(pt2[0:64, 0:128], ct[:, 512:576], self.ident[:]),
                                reads=[ct.b, self.ident.b], writes=[pt2.b])
                        P.dve.op(lambda e, pt2=pt2, lT=lT: e.tensor_copy(out=lT[0:64, 4, :], in_=pt2[0:64, 0:128]), reads=[pt2.b], writes=[lT.b])
                        P.pool.op(lambda e, ct=ct, ctb=ctb: e.tensor_copy(out=ctb[:], in_=ct[:, 0:512]), reads=[ct.b], writes=[ctb.b])
                        latm = lambda m, lT=lT: lT[:, m, 0:128]
                        kpem = lambda lT=lT: lT[0:64, 4, 0:128]
                        src_b = lT.b
                        tokm = lambda m, ctb=ctb: ctb[:, m * 128:(m + 1) * 128]
                        tok_b = ctb.b
                    else:
                        latm = lambda m: lat[:, m, NP:NT]
                        kpem = lambda: lat[0:64, 4, NP:NT]
                        src_b = lat.b
                        ctb = ctbs[0]
                        P.pool.op(lambda e, ctb=ctb: e.tensor_copy(out=ctb[0:NS, :], in_=self.ckvn_s[0:NS, :]),
                                  reads=[self.ckvn_s.b], writes=[ctb.b])
                        tokm = lambda m, ctb=ctb: ctb[0:NS, m * 128:(m + 1) * 128]
                        tok_b = ctb.b
                    ps_s = self.bank()
                    for m in range(4):
                        P.pe.op(lambda e, m=m, ps_s=ps_s, latm=latm, rows=rows: e.matmul(ps_s[0:rows, 0:256], lhsT=latm(m), rhs=qlT[:, m, :],
                                                                                     start=(m == 0), stop=False),
                                reads=[src_b, qlT.b], writes=[ps_s.b])
                    P.pe.op(lambda e, ps_s=ps_s, kpem=kpem, rows=rows: e.matmul(ps_s[0:rows, 0:256], lhsT=kpem(), rhs=qpe_flat, start=False, stop=True),
                            reads=[src_b, qs_pe.b], writes=[ps_s.b])
                    pT = pTs[pcount % 3]
                    pcount += 1
                    P.act.op(lambda e, ps_s=ps_s, pT=pT, rows=rows: e.activation(out=pT[0:rows, 0:256], in_=ps_s[0:rows, 0:256], func=AF.Exp,
                                                                                 scale=MLA_SCALE), reads=[ps_s.b], writes=[pT.b])
                    first = ti == 0
                    last = own

                    def pv_s(tokm=tokm, tok_b=tok_b, pT=pT, rows=rows, first=first, last=last):
                        for m in range(4):
                            ob = oacc[m // 2]
                            o0 = (m % 2) * 256
                            P.pe.op(lambda e, m=m, ob=ob, o0=o0: e.matmul(ob[:, o0:o0 + 256], lhsT=tokm(m), rhs=pT[0:rows, 0:256],
                                                                          start=first, stop=last), reads=[tok_b, pT.b], writes=[ob.b])
                        P.dve.op(lambda e: e.tensor_tensor(out=accs[0:rows, :], in0=accs[0:rows, :], in1=pT[0:rows, 0:256], op=ALU.add),
                                 reads=[pT.b, accs.b], writes=[accs.b])
                    if pend is not None:
                        pend()
                    pend = pv_s
                pend()
                for m in range(4):
                    ob = oacc[m // 2]
                    o0 = (m % 2) * 256
                    if m % 2 == 0:
                        P.act.op(lambda e, m=m, ob=ob, o0=o0: e.copy(out=olT[:, m, :], in_=ob[:, o0:o0 + 256]), reads=[ob.b], writes=[olT.b])
                    else:
                        P.act.op(lambda e, m=m, ob=ob, o0=o0: e.copy(out=olT[:, m, :], in_=ob[:, o0:o0 + 256]), reads=[ob.b], writes=[olT.b])
                osb = self.banks[6]
                for h in range(16):
                    wkv = hb[h % 2]["wkv"]
                    P.pool.op(lambda e, h=h, wkv=wkv: e.dma_start(
                        out=wkv[:], in_=w_kvb[:, h * 256:(h + 1) * 256].rearrange("(k p) n -> p k n", p=128)),
                        writes=[wkv.b], dma=True)
                    for m in range(4):
                        P.pe.op(lambda e, m=m, h=h, wkv=wkv: e.matmul(osb[:, h * NS:(h + 1) * NS], lhsT=wkv[:, m, 128:256],
                                                                      rhs=olT[:, m, h * NS:(h + 1) * NS], start=(m == 0), stop=(m == 3)),
                                reads=[wkv.b, olT.b], writes=[osb.b])
                ps = self.bank()
                P.pe.op(lambda e, ps=ps: e.matmul(ps[:, 0:256], lhsT=self.ones[:], rhs=accs[:], start=True, stop=True),
                        reads=[self.ones.b, accs.b], writes=[ps.b])
                P.dve.op(lambda e, ps=ps: e.reciprocal(out=rinv[:, 0:256], in_=ps[:, 0:256]), reads=[ps.b], writes=[rinv.b])
                P.dve.op(lambda e: e.tensor_tensor(out=otmp[:, 0:256], in0=osb[:, 0:256], in1=rinv[:, 0:256], op=ALU.mult),
                         reads=[osb.b, rinv.b], writes=[otmp.b])
                P.dve.op(lambda e: e.tensor_tensor(out=sgT[:, :, NP:NT], in0=otmp[:, 0:256].rearrange("p (a b) -> p a b", a=16),
                                                   in1=sgT[:, :, NP:NT], op=ALU.mult),
                         reads=[otmp.b, sgT.b], writes=[sgT.b])
                P.barrier()


    def transpose_out(self, srcF, src_b, ncols, c0, dst_ap, stT, nchunks=16, width=D):
        P = self.P
        for g in range(nchunks // 4):
            pb = self.bank()
            for q in range(4):
                ch = g * 4 + q
                P.pe.op(lambda e, pb=pb, q=q, ch=ch: e.transpose(pb[0:ncols, q * 128:(q + 1) * 128],
                                                               srcF[:, ch, c0:c0 + ncols], self.ident[:]),
                        reads=[src_b, self.ident.b], writes=[pb.b])
            P.act.op(lambda e, pb=pb, g=g: e.copy(out=stT[0:ncols, g * 512:(g + 1) * 512], in_=pb[0:ncols, :]),
                     reads=[pb.b], writes=[stT.b])
        P.sp.op(lambda e: e.dma_start(out=dst_ap, in_=stT[0:ncols, 0:width]), reads=[stT.b], dma=True)

    def conv_layer(self):
        P, io, nc = self.P, self.io, self.nc
        w_in = io["l1_w_in"]
        gin = nc.dram_tensor("gin1", [128, 512], F32)
        gout = nc.dram_tensor("gout1", [4 * 128, 512], F32)
        b_gin, b_gout = Buf(), Buf()
        with ExitStack() as stL:
            halo = self.sb(stL, "c1_halo", [128, 16, 32])
            uh = self.sb(stL, "c1_uh", [128, 16, 32])
            usall = self.sb(stL, "c1_usall", [128, 16, NS])
            stF = self.sb(stL, "c1_stF", [128, 16, 32])
            stT = self.sb(stL, "c1_stT", [32, D])
            wbs = [self.sb(stL, "c1_wb%d" % i, [128, KC, 256], BF16) for i in range(3)]
            hT = self.sb(stL, "c1_hT", [128, KC, 560], BF16)
            sg = self.sb(stL, "c1_sg", [128, KC, 528], BF16)
            cvT = self.sb(stL, "c1_cvT", [128, KC, 528])
            ups = [self.sb(stL, "c1_up%d" % i, [128, 544]) for i in range(2)]
            uss = [self.sb(stL, "c1_us%d" % i, [128, 48]) for i in range(2)]
            tv = self.sb(stL, "c1_tv", [128, 512])
            tg = self.sb(stL, "c1_tg", [128, 512])
            sq = self.sb(stL, "c1_sq", [128, 512])
            rstd = self.sb(stL, "c1_rstd", [128, 512])
            mu = self.sb(stL, "c1_mu", [128, 512])
            var = self.sb(stL, "c1_var", [128, 512])
            gat = self.sb(stL, "c1_gat", [128, 4, 512])
            acc2 = self.sb(stL, "c1_acc2", [128, 512])
            tmpA = [self.sb(stL, "c1_tmpA%d" % i, [128, 512]) for i in range(2)]
            P.sp.op(lambda e: e.dma_start(out=stT[0:30, :], in_=io["st1"]), writes=[stT.b], dma=True)
            P.pool.op(lambda e: e.memset(stF[:], 0.0), writes=[stF.b])
            for g in range(4):
                pb = self.bank()
                for q in range(4):
                    ch = g * 4 + q
                    P.pe.op(lambda e, pb=pb, q=q, ch=ch: e.transpose(pb[:, q * 32:q * 32 + 30], stT[0:30, ch * 128:(ch + 1) * 128],
                                                                   self.ident[0:30, 0:30]),
                            reads=[stT.b, self.ident.b], writes=[pb.b])
                P.act.op(lambda e, pb=pb, g=g: e.copy(out=stF[:, g * 4:(g + 1) * 4, 2:32],
                                                      in_=pb[:, 0:128].rearrange("p (a b) -> p a b", a=4)[:, :, 0:30]),
                         reads=[pb.b], writes=[stF.b])
            P.sp.op(lambda e: e.dma_start(out=io["conv1s"][0:14, :], in_=io["st1"][16:30, :]), dma=True)
            wi = 0
            for tile_i in range(2):
                if tile_i == 0:
                    x0, nloc = 480, 560
                    splits = [(0, 32), (32, 512), (544, 16)]
                    out0 = 512
                else:
                    x0, nloc = 0, 512
                    splits = [(0, 512)]
                    out0 = 0
                nout = 528 if tile_i == 0 else 512
                for (c, nn) in self.subtiles(x0, nloc):
                    self.rms_F(hT, c, nn, "norm1", sq, rstd, off=c - x0)
                gsplits = [(32, 512), (544, 16)] if tile_i == 0 else [(0, 512)]
                for gb in range(8):
                    wb = wbs[wi % 3]; wi += 1
                    self.load_wblock(wb, w_in, 4096 + gb * 256, 256)

                    def cons_g(ps, m, c, nn, gb=gb, tile_i=tile_i):
                        oc = c - 32 if tile_i == 0 else c
                        P.act.op(lambda e: e.activation(out=sg[:, gb * 2 + m, oc:oc + nn], in_=ps[:, 0:nn], func=AF.Silu),
                                 reads=[ps.b], writes=[sg.b])
                    self.proj_F(wb, 0, 2, hT, hT.b, 0, nloc, cons_g, splits=gsplits)
                for g8 in range(8):
                    wv = wbs[wi % 3]; wi += 1
                    wg = wbs[wi % 3]; wi += 1
                    self.load_wblock(wv, w_in, g8 * 256, 256)
                    self.load_wblock(wg, w_in, 2048 + g8 * 256, 256)
                    for m in range(2):
                        ch = g8 * 2 + m
                        up = ups[ch % 2]
                        us = uss[ch % 2]
                        if tile_i == 1:
                            P.act.op(lambda e, up=up, ch=ch: e.copy(out=up[:, 0:32], in_=halo[:, ch, :]),
                                     reads=[halo.b], writes=[up.b])
                        else:
                            P.act.op(lambda e, us=us, ch=ch: e.copy(out=us[:, 0:32], in_=stF[:, ch, :]),
                                     reads=[stF.b], writes=[us.b])
                        for (c, nn) in splits:
                            psv = self.bank()
                            psg = self.bank()
                            for k in range(KC):
                                P.pe.op(lambda e, k=k, m=m, c=c, nn=nn, psv=psv, wv=wv: e.matmul(
                                    psv[:, 0:nn], lhsT=wv[:, k, m * 128:(m + 1) * 128], rhs=hT[:, k, c:c + nn],
                                    start=(k == 0), stop=(k == KC - 1)), reads=[wv.b, hT.b], writes=[psv.b])
                            for k in range(KC):
                                P.pe.op(lambda e, k=k, m=m, c=c, nn=nn, psg=psg, wg=wg: e.matmul(
                                    psg[:, 0:nn], lhsT=wg[:, k, m * 128:(m + 1) * 128], rhs=hT[:, k, c:c + nn],
                                    start=(k == 0), stop=(k == KC - 1)), reads=[wg.b, hT.b], writes=[psg.b])
                            P.act.op(lambda e, nn=nn, psg=psg: e.activation(out=tg[:, 0:nn], in_=psg[:, 0:nn], func=AF.Sigmoid),
                                     reads=[psg.b], writes=[tg.b])
                            if tile_i == 0 and c == 544:
                                dst, dstb = us[:, 32:48], us.b
                            elif tile_i == 0:
                                dst, dstb = up[:, c:c + nn], up.b
                            else:
                                dst, dstb = up[:, 32 + c:32 + c + nn], up.b
                            P.dve.op(lambda e, nn=nn, psv=psv, dst=dst: e.tensor_tensor(out=dst, in0=psv[:, 0:nn], in1=tg[:, 0:nn],
                                                                                        op=ALU.mult),
                                     reads=[psv.b, tg.b], writes=[dstb])
                        if tile_i == 0:
                            P.act.op(lambda e, up=up, ch=ch: e.copy(out=uh[:, ch, :], in_=up[:, 512:544]),
                                     reads=[up.b], writes=[uh.b])
                            P.act.op(lambda e, us=us, ch=ch: e.copy(out=usall[:, ch, :], in_=us[:, 32:48]),
                                     reads=[us.b], writes=[usall.b])
                        jobs = [(up, 512, 0)]
                        if tile_i == 0:
                            jobs.append((us, NS, 512))
                        for (buf, n, oc) in jobs:
                            dstc = cvT[:, ch, oc:oc + n]
                            off_taps = list(range(31 - NOFF, 31)) if n == 512 else []
                            for ii, k in enumerate(off_taps):
                                dsta = acc2 if ii == 0 else tmpA[ii % 2]
                                P.act.op(lambda e, buf=buf, n=n, ch=ch, k=k, dsta=dsta: e.activation(
                                    out=dsta[:, 0:n], in_=buf[:, 2 + k:2 + k + n], func=AF.Copy,
                                    scale=self.cs("dww", ch * 31 + k, ch * 31 + k + 1)),
                                    reads=[buf.b, self.cst_t.b], writes=[dsta.b])
                                if ii > 0:
                                    P.pool.op(lambda e, n=n, dsta=dsta: e.tensor_tensor(out=acc2[:, 0:n], in0=acc2[:, 0:n], in1=dsta[:, 0:n],
                                                                                       op=ALU.add), reads=[dsta.b, acc2.b], writes=[acc2.b])
                            P.dve.op(lambda e, buf=buf, n=n, dstc=dstc, ch=ch: e.tensor_scalar(
                                out=dstc, in0=buf[:, 2:2 + n], scalar1=self.cs("dww", ch * 31, ch * 31 + 1),
                                scalar2=self.cs("dwb", ch, ch + 1), op0=ALU.mult, op1=ALU.add),
                                reads=[buf.b, self.cst_t.b], writes=[cvT.b])
                            for k in range(1, 31):
                                if k in off_taps:
                                    continue
                                P.dve.op(lambda e, buf=buf, n=n, dstc=dstc, ch=ch, k=k: e.scalar_tensor_tensor(
                                    out=dstc, in0=buf[:, 2 + k:2 + k + n], scalar=self.cs("dww", ch * 31 + k, ch * 31 + k + 1),
                                    in1=dstc, op0=ALU.mult, op1=ALU.add),
                                    reads=[buf.b, self.cst_t.b, cvT.b], writes=[cvT.b])
                            if off_taps:
                                P.dve.op(lambda e, n=n, dstc=dstc: e.tensor_tensor(out=dstc, in0=dstc, in1=acc2[:, 0:n], op=ALU.add),
                                         reads=[acc2.b, cvT.b], writes=[cvT.b])
                if tile_i == 0:
                    P.sp.op(lambda e: e.dma_start(out=gin[:], in_=uh[:].rearrange("p a b -> p (a b)")),
                            reads=[uh.b], writes=[b_gin], dma=True)
                    P.pool.op(lambda e: e.collective_compute("AllGather", ALU.bypass, replica_groups=GROUPS,
                                                             ins=[gin[:]], outs=[gout[:]]),
                              reads=[b_gin], writes=[b_gout], cc=True)
                    P.sp.op(lambda e: e.dma_start(out=gat[:], in_=gout[:].rearrange("(r p) n -> p r n", p=128)),
                            reads=[b_gout], writes=[gat.b], dma=True)
                    hflat = halo[:].rearrange("p a b -> p (a b)")
                    P.dve.op(lambda e: e.tensor_scalar(out=hflat, in0=gat[:, 0, :], scalar1=self.cs("selp", 0, 1), scalar2=None,
                                                       op0=ALU.mult), reads=[gat.b, self.cst_t.b], writes=[halo.b])
                    for r in range(1, 4):
                        P.dve.op(lambda e, r=r: e.scalar_tensor_tensor(out=hflat, in0=gat[:, r, :], scalar=self.cs("selp", r, r + 1),
                                                                      in1=hflat, op0=ALU.mult, op1=ALU.add),
                                 reads=[gat.b, self.cst_t.b, halo.b], writes=[halo.b])
                    self.transpose_out(uh, uh.b, 30, 2, io["conv1p"], stT)
                    self.transpose_out(usall, usall.b, NS, 0, io["conv1s"][14:30, :], stT)
                lsplits = [(0, 512), (512, 16)] if tile_i == 0 else [(0, 512)]
                for (c, nn) in lsplits:
                    pbS = self.bank()
                    pbQ = self.bank()
                    for ch in range(KC):
                        P.pe.op(lambda e, ch=ch, c=c, nn=nn, pbS=pbS: e.matmul(pbS[:, 0:nn], lhsT=self.ones[:], rhs=cvT[:, ch, c:c + nn],
                                                                              start=(ch == 0), stop=(ch == KC - 1)),
                                reads=[self.ones.b, cvT.b], writes=[pbS.b])
                    for ch in range(KC):
                        P.act.op(lambda e, ch=ch, c=c, nn=nn: e.activation(out=sq[:, 0:nn], in_=cvT[:, ch, c:c + nn], func=AF.Square),
                                 reads=[cvT.b], writes=[sq.b])
                        P.pe.op(lambda e, ch=ch, nn=nn, pbQ=pbQ: e.matmul(pbQ[:, 0:nn], lhsT=self.ones[:], rhs=sq[:, 0:nn],
                                                                         start=(ch == 0), stop=(ch == KC - 1)),
                                reads=[self.ones.b, sq.b], writes=[pbQ.b])
                    P.act.op(lambda e, nn=nn, pbS=pbS: e.mul(out=mu[:, 0:nn], in_=pbS[:, 0:nn], mul=1.0 / D), reads=[pbS.b], writes=[mu.b])
                    P.dve.op(lambda e, nn=nn: e.tensor_tensor(out=var[:, 0:nn], in0=mu[:, 0:nn], in1=mu[:, 0:nn], op=ALU.mult),
                             reads=[mu.b], writes=[var.b])
                    P.dve.op(lambda e, nn=nn, pbQ=pbQ: e.scalar_tensor_tensor(out=var[:, 0:nn], in0=pbQ[:, 0:nn], scalar=1.0 / D,
                                                                             in1=var[:, 0:nn], op0=ALU.mult, op1=ALU.subtract),
                             reads=[pbQ.b, var.b], writes=[var.b])
                    P.dve.op(lambda e, nn=nn: e.tensor_scalar(out=var[:, 0:nn], in0=var[:, 0:nn], scalar1=EPS, scalar2=None, op0=ALU.add),
                             reads=[var.b], writes=[var.b])
                    P.act.op(lambda e, nn=nn: e.activation(out=var[:, 0:nn], in_=var[:, 0:nn], func=AF.Sqrt), reads=[var.b], writes=[var.b])
                    P.dve.op(lambda e, nn=nn: e.reciprocal(out=rstd[:, 0:nn], in_=var[:, 0:nn]), reads=[var.b], writes=[rstd.b])
                    for ch in range(KC):
                        P.dve.op(lambda e, ch=ch, c=c, nn=nn: e.tensor_tensor(out=tv[:, 0:nn], in0=cvT[:, ch, c:c + nn], in1=mu[:, 0:nn],
                                                                              op=ALU.subtract), reads=[cvT.b, mu.b], writes=[tv.b])
                        P.dve.op(lambda e, nn=nn: e.tensor_tensor(out=tv[:, 0:nn], in0=tv[:, 0:nn], in1=rstd[:, 0:nn], op=ALU.mult),
                                 reads=[tv.b, rstd.b], writes=[tv.b])
                        P.act.op(lambda e, ch=ch, nn=nn: e.activation(out=tg[:, 0:nn], in_=tv[:, 0:nn], func=AF.Silu,
                                                                      bias=self.cs("lnb", ch, ch + 1), scale=self.cs("lng", ch, ch + 1)),
                                 reads=[tv.b, self.cst_t.b], writes=[tg.b])
                        P.dve.op(lambda e, ch=ch, c=c, nn=nn: e.tensor_tensor(out=sg[:, ch, c:c + nn], in0=tg[:, 0:nn],
                                                                              in1=sg[:, ch, c:c + nn], op=ALU.mult),
                                 reads=[tg.b, sg.b], writes=[sg.b])
                for blk in range(8):
                    wb = wbs[wi % 3]; wi += 1
                    self.load_wblock(wb, io["l1_w_out"], blk * 256, 256)

                    def cons_o(ps, m, c, nn, blk=blk, out0=out0):
                        P.dve.op(lambda e: e.tensor_tensor(out=self.xT[:, blk * 2 + m, out0 + c:out0 + c + nn],
                                                           in0=ps[:, 0:nn], in1=self.xT[:, blk * 2 + m, out0 + c:out0 + c + nn],
                                                           op=ALU.add),
                                 reads=[ps.b, self.xT.b], writes=[self.xT.b])
                    self.proj_F(wb, 0, 2, sg, sg.b, 0, nout, cons_o, splits=lsplits)
            P.barrier()


    def vap(self, ap, dims):
        return bass.AP(ap.tensor, ap.offset, [list(ap.ap[0])] + [list(d) for d in dims])

    def exchange(self, name, src_ap, src_b, ncols, dst, dt_=F32):
        P, nc = self.P, self.nc
        gin = nc.dram_tensor("gin_" + name, [128, ncols], dt_)
        gout = nc.dram_tensor("gout_" + name, [4 * 128, ncols], dt_)
        b1, b2 = Buf(), Buf()
        P.sp.op(lambda e: e.dma_start(out=gin[:], in_=src_ap), reads=[src_b], writes=[b1], dma=True)
        P.pool.op(lambda e: e.collective_compute("AllGather", ALU.bypass, replica_groups=GROUPS, ins=[gin[:]], outs=[gout[:]]),
                  reads=[b1], writes=[b2], cc=True)
        P.sp.op(lambda e: e.dma_start(out=dst[:, :, 0:ncols], in_=gout[:].rearrange("(r p) n -> p r n", p=128)),
                reads=[b2], writes=[dst.b], dma=True)

    def ssd_layer(self):
        P, io, nc = self.P, self.io, self.nc
        w_in = io["l2_w_in"]
        NH = NT + 4
        d_zs = nc.dram_tensor("d_zs", [NT, 4096], BF16)
        d_xs = nc.dram_tensor("d_xs", [NT, 4096], BF16)
        d_bs = nc.dram_tensor("d_bs", [NT, 1024], BF16)
        d_bf = nc.dram_tensor("d_bf", [128, 8 * NT], BF16)
        d_cf = nc.dram_tensor("d_cf", [128, 8 * NT], BF16)
        d_yn = nc.dram_tensor("d_yn", [128, 32 * NT], BF16)
        b_zs, b_xs, b_bs, b_bf, b_cf, b_yn = [Buf() for _ in range(6)]
        with ExitStack() as stL:
            U = self.sb(stL, "s2_U", [128, 128])
            SU = self.sb(stL, "s2_SU", [128, 128])
            aneg = self.sb(stL, "s2_aneg", [128, 64])
            dtT = self.sb(stL, "s2_dtT", [128, 9, 64])
            P.pool.op(lambda e: e.memset(U[:], 1.0), writes=[U.b])
            P.pool.op(lambda e: e.affine_select(out=U[:], in_=U[:], pattern=[[1, 128]], compare_op=ALU.is_ge, fill=0.0,
                                                base=0, channel_multiplier=-1), reads=[U.b], writes=[U.b])
            P.pool.op(lambda e: e.memset(SU[:], 1.0), writes=[SU.b])
            P.pool.op(lambda e: e.affine_select(out=SU[:], in_=SU[:], pattern=[[-1, 128]], compare_op=ALU.is_gt, fill=0.0,
                                                base=0, channel_multiplier=1), reads=[SU.b], writes=[SU.b])
            P.act.op(lambda e: e.activation(out=aneg[:], in_=self.cs("alog"), func=AF.Exp), reads=[self.cst_t.b], writes=[aneg.b])
            P.act.op(lambda e: e.mul(out=aneg[:], in_=aneg[:], mul=-1.0), reads=[aneg.b], writes=[aneg.b])
            with ExitStack() as st:
                xh = self.sb(st, "s2_xh", [128, KC, 4])
                gatx = self.sb(st, "s2_gatx", [128, 4, 64])
                hT = self.sb(st, "s2_hT", [128, KC, NH], BF16)
                sq = self.sb(st, "s2_sq", [128, 512])
                rstd = self.sb(st, "s2_rstd", [128, 512])
                wbs = [self.sb(st, "s2_wb%d" % i, [128, KC, 256], BF16) for i in range(3)]
                cbuf = [self.sb(st, "s2_cbuf%d" % i, [128, 4 + NP]) for i in range(2)]
                cbs = [self.sb(st, "s2_cbs%d" % i, [128, 4 + NS]) for i in range(2)]
                cvo = [self.sb(st, "s2_cvo%d" % i, [128, NT]) for i in range(2)]
                cvb = self.sb(st, "s2_cvb", [128, NT], BF16)
                stg = [self.sb(st, "s2_stg%d" % i, [128, 9, 512], BF16) for i in range(2)]
                zst = [self.sb(st, "s2_zst%d" % i, [128, 256], BF16) for i in range(3)]
                pre_p = self.sb(st, "s2_prep", [128, 48, 4])
                pre_s = self.sb(st, "s2_pres", [128, 48, 4])
                stF = self.sb(st, "s2_stF", [128, 48, 4])
                stT = self.sb(st, "s2_stT", [4, 2048])
                dtt = self.sb(st, "s2_dtt", [128, 4, 64])
                ginx = nc.dram_tensor("gin_xh", [128, 64], F32)
                goutx = nc.dram_tensor("gout_xh", [4 * 128, 64], F32)
                bx1, bx2 = Buf(), Buf()
                P.sp.op(lambda e: e.dma_start(out=ginx[:].rearrange("p (a b) -> p a b", a=16), in_=self.xT[:, :, NP - 4:NP]),
                        reads=[self.xT.b], writes=[bx1], dma=True)
                P.pool.op(lambda e: e.collective_compute("AllGather", ALU.bypass, replica_groups=GROUPS, ins=[ginx[:]], outs=[goutx[:]]),
                          reads=[bx1], writes=[bx2], cc=True)
                P.sp.op(lambda e: e.dma_start(out=gatx[:], in_=goutx[:].rearrange("(r p) n -> p r n", p=128)),
                        reads=[bx2], writes=[gatx.b], dma=True)
                xhf = xh[:].rearrange("p a b -> p (a b)")
                P.dve.op(lambda e: e.tensor_scalar(out=xhf, in0=gatx[:, 0, :], scalar1=self.cs("selp", 0, 1), scalar2=None, op0=ALU.mult),
                         reads=[gatx.b, self.cst_t.b], writes=[xh.b])
                for r in range(1, 4):
                    P.dve.op(lambda e, r=r: e.scalar_tensor_tensor(out=xhf, in0=gatx[:, r, :], scalar=self.cs("selp", r, r + 1), in1=xhf,
                                                                  op0=ALU.mult, op1=ALU.add),
                             reads=[gatx.b, self.cst_t.b, xh.b], writes=[xh.b])
                P.pool.op(lambda e: e.memset(stF[:], 0.0), writes=[stF.b])
                for pc3 in range(3):
                    P.sp.op(lambda e, pc3=pc3: e.dma_start(out=stT[0:3, :], in_=io["st2c"][:, pc3 * 2048:(pc3 + 1) * 2048]),
                            writes=[stT.b], dma=True)
                    for g in range(4):
                        pb = self.bank()
                        for q in range(4):
                            chl = g * 4 + q
                            P.pe.op(lambda e, pb=pb, q=q, chl=chl: e.transpose(pb[:, q * 4:q * 4 + 3], stT[0:3, chl * 128:(chl + 1) * 128],
                                                                             self.ident[0:3, 0:3]),
                                    reads=[stT.b, self.ident.b], writes=[pb.b])
                        gg = pc3 * 4 + g
                        P.act.op(lambda e, pb=pb, gg=gg: e.copy(out=stF[:, gg * 4:(gg + 1) * 4, 1:4],
                                                                in_=pb[:, 0:16].rearrange("p (a b) -> p a b", a=4)[:, :, 0:3]),
                                 reads=[pb.b], writes=[stF.b])
                self.rms_F(hT, 0, 4, "norm2", sq, rstd, off=0, src=xh)
                for (c, nn) in self.subtiles(0, NT):
                    self.rms_F(hT, c, nn, "norm2", sq, rstd, off=4 + c)
                tchunks = [(4 + i * 128, 128, i) for i in range(8)] + [(4 + NP, NS, 8)]
                wi = 0
                zc = 0
                for zb in range(16):
                    wb = wbs[wi % 3]; wi += 1
                    self.load_wblock(wb, w_in, zb * 256, 256)
                    for (c, rows, ti) in tchunks:
                        ps = self.bank()
                        for k in range(KC):
                            P.pe.op(lambda e, k=k, c=c, rows=rows, ps=ps, wb=wb: e.matmul(
                                ps[0:rows, 0:256], lhsT=hT[:, k, c:c + rows], rhs=wb[:, k, 0:256], start=(k == 0), stop=(k == KC - 1)),
                                reads=[hT.b, wb.b], writes=[ps.b])
                        zt = zst[zc % 3]; zc += 1
                        P.act.op(lambda e, rows=rows, ps=ps, zt=zt: e.activation(out=zt[0:rows, :], in_=ps[0:rows, 0:256], func=AF.Silu),
                                 reads=[ps.b], writes=[zt.b])
                        r0 = c - 4
                        P.sp.op(lambda e, rows=rows, r0=r0, zb=zb, zt=zt: e.dma_start(out=d_zs[r0:r0 + rows, zb * 256:(zb + 1) * 256],
                                                                                 in_=zt[0:rows, :]),
                                reads=[zt.b], writes=[b_zs], dma=True)
                wb = wbs[wi % 3]; wi += 1
                self.load_wblock(wb, w_in, 10240, 64)
                for (c, rows, ti) in tchunks:
                    ps = self.bank()
                    for k in range(KC):
                        P.pe.op(lambda e, k=k, c=c, rows=rows, ps=ps, wb=wb: e.matmul(
                            ps[0:rows, 0:64], lhsT=hT[:, k, c:c + rows], rhs=wb[:, k, 0:64], start=(k == 0), stop=(k == KC - 1)),
                            reads=[hT.b, wb.b], writes=[ps.b])
                    P.dve.op(lambda e, rows=rows, ps=ps: e.tensor_tensor(out=dtt[0:rows, 0, :], in0=ps[0:rows, 0:64],
                                                                         in1=self.cs("dtb")[0:rows, :], op=ALU.add),
                             reads=[ps.b, self.cst_t.b], writes=[dtt.b])
                    P.dve.op(lambda e, rows=rows: e.tensor_scalar(out=dtt[0:rows, 2, :], in0=dtt[0:rows, 0, :], scalar1=-1.0, scalar2=None,
                                                                  op0=ALU.mult), reads=[dtt.b], writes=[dtt.b])
                    P.dve.op(lambda e, rows=rows: e.tensor_tensor(out=dtt[0:rows, 1, :], in0=dtt[0:rows, 0, :], in1=dtt[0:rows, 2, :],
                                                                  op=ALU.max), reads=[dtt.b], writes=[dtt.b])
                    P.act.op(lambda e, rows=rows: e.activation(out=dtt[0:rows, 2, :], in_=dtt[0:rows, 1, :], func=AF.Exp, scale=-1.0),
                             reads=[dtt.b], writes=[dtt.b])
                    P.act.op(lambda e, rows=rows: e.activation(out=dtt[0:rows, 3, :], in_=dtt[0:rows, 2, :], func=AF.Ln, bias=1.0),
                             reads=[dtt.b], writes=[dtt.b])
                    P.dve.op(lambda e, rows=rows, ti=ti: e.scalar_tensor_tensor(out=dtT[0:rows, ti, :], in0=dtt[0:rows, 0, :], scalar=0.0,
                                                                               in1=dtt[0:rows, 3, :], op0=ALU.max, op1=ALU.add),
                             reads=[dtt.b], writes=[dtT.b])
                csplits = [(0, 4), (4, 512), (516, 512), (1028, NS)]
                for g24 in range(24):
                    wb = wbs[wi % 3]; wi += 1
                    self.load_wblock(wb, w_in, 4096 + g24 * 256, 256)
                    for m in range(2):
                        ch = g24 * 2 + m
                        cb_ = cbuf[ch % 2]
                        cs_ = cbs[ch % 2]
                        co = cvo[ch % 2]
                        P.act.op(lambda e, cs_=cs_, ch=ch: e.copy(out=cs_[:, 0:4], in_=stF[:, ch, :]), reads=[stF.b], writes=[cs_.b])
                        for (c, nn) in csplits:
                            ps = self.bank()
                            for k in range(KC):
                                P.pe.op(lambda e, k=k, m=m, c=c, nn=nn, ps=ps, wb=wb: e.matmul(
                                    ps[:, 0:nn], lhsT=wb[:, k, m * 128:(m + 1) * 128], rhs=hT[:, k, c:c + nn],
                                    start=(k == 0), stop=(k == KC - 1)), reads=[wb.b, hT.b], writes=[ps.b])
                            if c == 1028:
                                P.act.op(lambda e, ps=ps, cs_=cs_: e.copy(out=cs_[:, 4:4 + NS], in_=ps[:, 0:NS]), reads=[ps.b], writes=[cs_.b])
                            else:
                                P.act.op(lambda e, ps=ps, cb_=cb_, c=c, nn=nn: e.copy(out=cb_[:, c:c + nn], in_=ps[:, 0:nn]),
                                         reads=[ps.b], writes=[cb_.b])
                        P.pool.op(lambda e, cb_=cb_, ch=ch: e.tensor_copy(out=pre_p[:, ch, :], in_=cb_[:, NP:NP + 4]),
                                  reads=[cb_.b], writes=[pre_p.b])
                        P.pool.op(lambda e, cs_=cs_, ch=ch: e.tensor_copy(out=pre_s[:, ch, :], in_=cs_[:, NS:NS + 4]),
                                  reads=[cs_.b], writes=[pre_s.b])
                        for (buf, n, oc) in ((cb_, NP, 0), (cs_, NS, NP)):
                            dstc = co[:, oc:oc + n]
                            P.dve.op(lambda e, buf=buf, n=n, dstc=dstc, ch=ch: e.tensor_scalar(
                                out=dstc, in0=buf[:, 1:1 + n], scalar1=self.cs("cw2", ch * 4, ch * 4 + 1),
                                scalar2=self.cs("cb2", ch, ch + 1), op0=ALU.mult, op1=ALU.add),
                                reads=[buf.b, self.cst_t.b], writes=[co.b])
                            for k in range(1, 4):
                                P.dve.op(lambda e, buf=buf, n=n, dstc=dstc, ch=ch, k=k: e.scalar_tensor_tensor(
                                    out=dstc, in0=buf[:, 1 + k:1 + k + n], scalar=self.cs("cw2", ch * 4 + k, ch * 4 + k + 1),
                                    in1=dstc, op0=ALU.mult, op1=ALU.add), reads=[buf.b, self.cst_t.b, co.b], writes=[co.b])
                        if ch >= 32:
                            P.act.op(lambda e, co=co: e.activation(out=cvb[:], in_=co[:], func=AF.Silu), reads=[co.b], writes=[cvb.b])
                            gq = (ch - 32) % 8
                            dd, bb = (d_bf, b_bf) if ch < 40 else (d_cf, b_cf)
                            P.sp.op(lambda e, dd=dd, gq=gq: e.dma_start(out=dd[:, gq * NT:(gq + 1) * NT], in_=cvb[:]),
                                    reads=[cvb.b], writes=[bb], dma=True)
                        if ch < 40:
                            P.act.op(lambda e, co=co: e.activation(out=co[:], in_=co[:], func=AF.Silu), reads=[co.b], writes=[co.b])
                            sgi = (ch // 4) % 2
                            sg_ = stg[sgi]
                            q = ch % 4
                            for t3 in range(3):
                                pb = self.bank()
                                cnt = 4 if t3 < 2 else 1
                                for tq in range(cnt):
                                    tci = t3 * 4 + tq
                                    rows = 128 if tci < 8 else NS
                                    P.pe.op(lambda e, pb=pb, tq=tq, tci=tci, rows=rows, co=co: e.transpose(
                                        pb[0:rows, tq * 128:(tq + 1) * 128], co[:, tci * 128:tci * 128 + rows], self.ident[:]),
                                        reads=[co.b, self.ident.b], writes=[pb.b])
                                if t3 < 2:
                                    P.act.op(lambda e, pb=pb, sg_=sg_, t3=t3, q=q: e.copy(
                                        out=sg_[:, t3 * 4:(t3 + 1) * 4, q * 128:(q + 1) * 128],
                                        in_=pb[:].rearrange("p (a b) -> p a b", a=4)), reads=[pb.b], writes=[sg_.b])
                                else:
                                    P.act.op(lambda e, pb=pb, sg_=sg_, q=q: e.copy(out=sg_[0:NS, 8, q * 128:(q + 1) * 128], in_=pb[0:NS, 0:128]),
                                             reads=[pb.b], writes=[sg_.b])
                            if q == 3:
                                if ch < 32:
                                    dd, bb, c0 = d_xs, b_xs, (ch // 4) * 512
                                else:
                                    dd, bb, c0 = d_bs, b_bs, ((ch - 32) // 4) * 512
                                P.sp.op(lambda e, dd=dd, c0=c0, sg_=sg_: e.dma_start(
                                    out=dd[0:NP, c0:c0 + 512].rearrange("(t p) n -> p t n", p=128), in_=sg_[:, 0:8, :]),
                                    reads=[sg_.b], writes=[bb], dma=True)
                                P.sp.op(lambda e, dd=dd, c0=c0, sg_=sg_: e.dma_start(out=dd[NP:NT, c0:c0 + 512], in_=sg_[0:NS, 8, :]),
                                        reads=[sg_.b], writes=[bb], dma=True)
                for (pre, dst) in ((pre_p, io["conv2p"]), (pre_s, io["conv2s"])):
                    for pc3 in range(3):
                        for g in range(4):
                            pb = self.bank()
                            for q in range(4):
                                ch = pc3 * 16 + g * 4 + q
                                P.pe.op(lambda e, pb=pb, q=q, ch=ch, pre=pre: e.transpose(pb[0:4, q * 128:(q + 1) * 128], pre[:, ch, :], self.ident[:]),
                                        reads=[pre.b, self.ident.b], writes=[pb.b])
                            P.act.op(lambda e, pb=pb, g=g: e.copy(out=stT[0:4, g * 512:(g + 1) * 512], in_=pb[0:4, :]), reads=[pb.b], writes=[stT.b])
                        P.sp.op(lambda e, dst=dst, pc3=pc3: e.dma_start(out=dst[:, pc3 * 2048:(pc3 + 1) * 2048], in_=stT[1:4, :]),
                                reads=[stT.b], dma=True)
                P.barrier()
            if SSD_STAGE < 2:
                return
            with ExitStack() as st:
                hst = self.sb(st, "s2_hst", [128, 4096])
                hsb = self.sb(st, "s2_hsb", [128, 4096], BF16)
                dlog = self.sb(st, "s2_dlog", [128, 64])
                xs = self.sb(st, "s2_xs", [128, 4096], BF16)
                zs = self.sb(st, "s2_zs", [128, 4096], BF16)
                bs = self.sb(st, "s2_bs", [128, 1024], BF16)
                bf = self.sb(st, "s2_bf", [128, 8, 128], BF16)
                cf = self.sb(st, "s2_cf", [128, 8, 128], BF16)
                dtx = self.sb(st, "s2_dtx", [128, 4096], BF16)
                sm = self.sb(st, "s2_sm", [128, 8, 64])
                Xgs = [self.sb(st, "s2_Xg%d" % i, [128, 1024]) for i in range(2)]
                Dgs = [self.sb(st, "s2_Dg%d" % i, [128, 1024]) for i in range(2)]
                segs = [self.sb(st, "s2_seg%d" % i, [128, 1024], BF16) for i in range(2)]
                Mgs = [self.sb(st, "s2_Mg%d" % i, [128, 1024], BF16) for i in range(2)]
                cbms = [self.sb(st, "s2_cbm%d" % i, [128, 128], BF16) for i in range(2)]
                ygs = [self.sb(st, "s2_yg%d" % i, [128, 512]) for i in range(2)]
                yg2s = [self.sb(st, "s2_yg2%d" % i, [128, 512]) for i in range(2)]
                ssq = self.sb(st, "s2_ssq", [128, 8, 4])
                ynT_ = self.sb(st, "s2_ynT", [128, 4096])
                ynF = self.sb(st, "s2_ynF", [128, 32, 128], BF16)
                class _V2:
                    pass
                gS = _V2()
                gS.b = ynT_.b
                gS_v = ynT_[:, 0:2048].rearrange("p (r n) -> p r n", r=4)
                gD = self.sb(st, "s2_gD", [128, 4, 64])
                tS = self.sb(st, "s2_tS", [128, 512])
                class _V:
                    pass
                hio = _V()
                hio.b = ynT_.b
                hio_v = ynT_[:].rearrange("p (c n) -> p c n", c=32)

                def chunk(r0, L, ti, full, first_state):
                    P.sp.op(lambda e: e.dma_start(out=xs[0:L, :], in_=d_xs[r0:r0 + L, :]), reads=[b_xs], writes=[xs.b], dma=True)
                    P.sp.op(lambda e: e.dma_start(out=bs[0:L, :], in_=d_bs[r0:r0 + L, :]), reads=[b_bs], writes=[bs.b], dma=True)
                    if CHUNK_OPS < 2:
                        return
                    P.dve.op(lambda e: e.tensor_tensor(out=sm[0:L, 0, :], in0=dtT[0:L, ti, :], in1=aneg[0:L, :], op=ALU.mult),
                             reads=[dtT.b, aneg.b], writes=[sm.b])
                    p1 = self.bank()
                    P.pe.op(lambda e: e.matmul(p1[0:L, 0:64], lhsT=U[0:L, 0:L], rhs=sm[0:L, 0, :], start=True, stop=True),
                            reads=[U.b, sm.b], writes=[p1.b])
                    P.pe.op(lambda e: e.matmul(p1[0:L, 64:128], lhsT=SU[0:L, 0:L], rhs=sm[0:L, 0, :], start=True, stop=True),
                            reads=[SU.b, sm.b], writes=[p1.b])
                    P.pe.op(lambda e: e.matmul(p1[:, 128:192], lhsT=self.ones[0:L, :], rhs=sm[0:L, 0, :], start=True, stop=True),
                            reads=[self.ones.b, sm.b], writes=[p1.b])
                    P.act.op(lambda e: e.mul(out=sm[0:L, 1, :], in_=p1[0:L, 0:64], mul=-1.0), reads=[p1.b], writes=[sm.b])
                    P.act.op(lambda e: e.activation(out=sm[0:L, 2, :], in_=p1[0:L, 0:64], func=AF.Exp), reads=[p1.b], writes=[sm.b])
                    P.act.op(lambda e: e.activation(out=sm[0:L, 3, :], in_=p1[0:L, 64:128], func=AF.Exp), reads=[p1.b], writes=[sm.b])
                    P.act.op(lambda e: e.activation(out=sm[:, 4, :], in_=p1[:, 128:192], func=AF.Exp), reads=[p1.b], writes=[sm.b])
                    if not full:
                        P.dve.op(lambda e: e.tensor_tensor(out=dlog[:], in0=dlog[:], in1=sm[:, 4, :], op=ALU.mult),
                                 reads=[sm.b, dlog.b], writes=[dlog.b])
                    P.dve.op(lambda e: e.tensor_tensor(out=sm[0:L, 5, :], in0=dtT[0:L, ti, :], in1=sm[0:L, 3, :], op=ALU.mult),
                             reads=[dtT.b, sm.b], writes=[sm.b])
                    if CHUNK_OPS < 3:
                        return
                    xs3 = xs[0:L, :].rearrange("p (h q) -> p h q", h=64)
                    if not full:
                        P.dve.op(lambda e: e.tensor_tensor(out=dtx[0:L, :].rearrange("p (h q) -> p h q", h=64), in0=xs3,
                                                           in1=self.vap(sm[0:L, 5, :], [[1, 64], [0, 64]]), op=ALU.mult),
                                 reads=[xs.b, sm.b], writes=[dtx.b])
                    if full:
                        P.sp.op(lambda e: e.dma_start(out=zs[0:L, :], in_=d_zs[r0:r0 + L, :]), reads=[b_zs], writes=[zs.b], dma=True)
                        for (dd, bb, tt_) in ((d_bf, b_bf, bf), (d_cf, b_cf, cf)):
                            P.sp.op(lambda e, dd=dd, tt_=tt_: e.dma_start(
                                out=tt_[:, :, 0:L], in_=dd[:].rearrange("p (g t) -> p g t", g=8)[:, :, r0:r0 + L]),
                                reads=[bb], writes=[tt_.b], dma=True)
                        P.dve.op(lambda e: e.tensor_tensor(out=dtx[0:L, :].rearrange("p (h q) -> p h q", h=64), in0=xs3,
                                                           in1=self.vap(dtT[0:L, ti, :], [[1, 64], [0, 64]]), op=ALU.mult),
                                 reads=[xs.b, dtT.b], writes=[dtx.b])
                        for g in range(8):
                            Xg, Dg, segb, Mg, cbm, yg, yg2 = Xgs[g % 2], Dgs[g % 2], segs[g % 2], Mgs[g % 2], cbms[g % 2], ygs[g % 2], yg2s[g % 2]
                            junk = yg2
                            P.dve.op(lambda e, g=g, Xg=Xg: e.tensor_tensor(
                                out=Xg[0:L, 0:8 * L].rearrange("p (a b) -> p a b", a=8),
                                in0=self.vap(sm[0:L, 0, g * 8:(g + 1) * 8], [[1, 8], [0, L]]),
                                in1=self.vap(U[0:L, 0:L], [[0, 8], [1, L]]), op=ALU.mult),
                                reads=[sm.b, U.b], writes=[Xg.b])
                            pa = self.bank()
                            pa2 = self.bank()
                            half = 4 * L
                            P.pe.op(lambda e, pa=pa, Xg=Xg: e.matmul(pa[0:L, 0:half], lhsT=self.ones[0:L, 0:L], rhs=Xg[0:L, 0:half], start=True, stop=True),
                                    reads=[self.ones.b, Xg.b], writes=[pa.b])
                            P.pe.op(lambda e, pa2=pa2, Xg=Xg: e.matmul(pa2[0:L, 0:half], lhsT=self.ones[0:L, 0:L], rhs=Xg[0:L, half:2 * half],
                                                               start=True, stop=True), reads=[self.ones.b, Xg.b], writes=[pa2.b])
                            for e8 in range(8):
                                pp = pa if e8 < 4 else pa2
                                o = (e8 % 4) * L
                                P.dve.op(lambda e, g=g, e8=e8, pp=pp, o=o, Dg=Dg: e.tensor_scalar(
                                    out=Dg[0:L, e8 * L:(e8 + 1) * L], in0=pp[0:L, o:o + L], scalar1=sm[0:L, 1, g * 8 + e8:g * 8 + e8 + 1],
                                    scalar2=0.0, op0=ALU.add, op1=ALU.min), reads=[pp.b, sm.b], writes=[Dg.b])
                            P.act.op(lambda e, segb=segb, Dg=Dg: e.activation(out=segb[0:L, 0:8 * L], in_=Dg[0:L, 0:8 * L], func=AF.Exp),
                                     reads=[Dg.b], writes=[segb.b])
                            pc = self.bank()
                            P.pe.op(lambda e, g=g, pc=pc: e.matmul(pc[0:L, 0:L], lhsT=bf[:, g, 0:L], rhs=cf[:, g, 0:L], start=True, stop=True),
                                    reads=[bf.b, cf.b], writes=[pc.b])
                            P.dve.op(lambda e, pc=pc, cbm=cbm: e.tensor_tensor(out=cbm[0:L, 0:L], in0=pc[0:L, 0:L], in1=U[0:L, 0:L], op=ALU.mult),
                                     reads=[pc.b, U.b], writes=[cbm.b])
                            P.dve.op(lambda e, Mg=Mg, segb=segb, cbm=cbm: e.tensor_tensor(out=Mg[0:L, 0:8 * L].rearrange("p (a b) -> p a b", a=8),
                                                               in0=segb[0:L, 0:8 * L].rearrange("p (a b) -> p a b", a=8),
                                                               in1=self.vap(cbm[0:L, 0:L], [[0, 8], [1, L]]), op=ALU.mult),
                                     reads=[segb.b, cbm.b], writes=[Mg.b])
                            pd = self.bank()
                            for e8 in range(8):
                                h = g * 8 + e8
                                P.pe.op(lambda e, e8=e8, h=h, pd=pd, Mg=Mg: e.matmul(pd[0:L, e8 * 64:(e8 + 1) * 64], lhsT=Mg[0:L, e8 * L:(e8 + 1) * L],
                                                                             rhs=dtx[0:L, h * 64:(h + 1) * 64], start=True, stop=True),
                                        reads=[Mg.b, dtx.b], writes=[pd.b])
                            po = self.bank()
                            P.pe.op(lambda e, g=g, po=po: e.matmul(po[0:L, :], lhsT=cf[:, g, 0:L], rhs=hsb[:, g * 512:(g + 1) * 512],
                                                                  start=True, stop=True), reads=[cf.b, hsb.b], writes=[po.b])
                            gsl = slice(g * 512, (g + 1) * 512)
                            P.dve.op(lambda e, g=g, po=po, yg=yg: e.tensor_tensor(
                                out=yg[0:L, :].rearrange("p (a b) -> p a b", a=8), in0=po[0:L, :].rearrange("p (a b) -> p a b", a=8),
                                in1=self.vap(sm[0:L, 2, g * 8:(g + 1) * 8], [[1, 8], [0, 64]]), op=ALU.mult),
                                reads=[po.b, sm.b], writes=[yg.b])
                            P.dve.op(lambda e, pd=pd, yg=yg: e.tensor_tensor(out=yg[0:L, :], in0=pd[0:L, :], in1=yg[0:L, :], op=ALU.add),
                                     reads=[pd.b, yg.b], writes=[yg.b])
                            P.dve.op(lambda e, g=g, gsl=gsl, yg2=yg2: e.tensor_tensor(
                                out=yg2[0:L, :].rearrange("p (a b) -> p a b", a=8), in0=xs[0:L, gsl].rearrange("p (a b) -> p a b", a=8),
                                in1=self.vap(self.cs("dskip")[0:L, g * 8:(g + 1) * 8], [[1, 8], [0, 64]]), op=ALU.mult),
                                reads=[xs.b, self.cst_t.b], writes=[yg2.b])
                            P.dve.op(lambda e, yg=yg, yg2=yg2: e.tensor_tensor(out=yg[0:L, :], in0=yg[0:L, :], in1=yg2[0:L, :], op=ALU.add),
                                     reads=[yg.b, yg2.b], writes=[yg.b])
                            P.dve.op(lambda e, gsl=gsl, yg=yg: e.tensor_tensor(out=yg[0:L, :], in0=yg[0:L, :], in1=zs[0:L, gsl], op=ALU.mult),
                                     reads=[yg.b, zs.b], writes=[yg.b])
                            P.act.op(lambda e, g=g, junk=junk, yg=yg: e.activation(out=junk[0:L, :], in_=yg[0:L, :], func=AF.Square, accum_out=ssq[0:L, g, 0:1]),
                                     reads=[yg.b], writes=[junk.b, ssq.b])
                            P.dve.op(lambda e, g=g: e.tensor_scalar(out=ssq[0:L, g, 1:2], in0=ssq[0:L, g, 0:1], scalar1=1.0 / 512, scalar2=EPS,
                                                                    op0=ALU.mult, op1=ALU.add), reads=[ssq.b], writes=[ssq.b])
                            P.act.op(lambda e, g=g: e.activation(out=ssq[0:L, g, 2:3], in_=ssq[0:L, g, 1:2], func=AF.Sqrt), reads=[ssq.b], writes=[ssq.b])
                            P.dve.op(lambda e, g=g: e.reciprocal(out=ssq[0:L, g, 3:4], in_=ssq[0:L, g, 2:3]), reads=[ssq.b], writes=[ssq.b])
                            P.dve.op(lambda e, g=g, gsl=gsl, yg=yg: e.tensor_scalar(out=ynT_[0:L, gsl], in0=yg[0:L, :], scalar1=ssq[0:L, g, 3:4], scalar2=None,
                                                                             op0=ALU.mult), reads=[yg.b, ssq.b], writes=[ynT_.b])
                        P.dve.op(lambda e: e.tensor_tensor(out=dtx[0:L, :].rearrange("p (h q) -> p h q", h=64),
                                                           in0=dtx[0:L, :].rearrange("p (h q) -> p h q", h=64),
                                                           in1=self.vap(sm[0:L, 3, :], [[1, 64], [0, 64]]), op=ALU.mult),
                                 reads=[dtx.b, sm.b], writes=[dtx.b])
                        for c4 in range(8):
                            pb = self.bank()
                            for q in range(4):
                                cc = c4 * 4 + q
                                P.pe.op(lambda e, pb=pb, q=q, cc=cc: e.transpose(pb[:, q * 128:q * 128 + L], ynT_[0:L, cc * 128:(cc + 1) * 128],
                                                                               self.ident[0:L, 0:L]), reads=[ynT_.b, self.ident.b], writes=[pb.b])
                            for q in range(4):
                                cc = c4 * 4 + q
                                P.act.op(lambda e, pb=pb, q=q, cc=cc: e.activation(out=ynF[:, cc, 0:L], in_=pb[:, q * 128:q * 128 + L], func=AF.Copy,
                                                                                  scale=self.cs("gn2", cc, cc + 1)),
                                         reads=[pb.b, self.cst_t.b], writes=[ynF.b])
                        P.sp.op(lambda e: e.dma_start(out=d_yn[:].rearrange("p (c t) -> p c t", c=32)[:, :, r0:r0 + L], in_=ynF[:, :, 0:L]),
                                reads=[ynF.b], writes=[b_yn], dma=True)
                    if CHUNK_OPS < 4:
                        return
                    for g in range(8):
                        pcs = self.bank()
                        P.pe.op(lambda e, g=g, pcs=pcs: e.matmul(pcs[:, :], lhsT=bs[0:L, g * 128:(g + 1) * 128], rhs=dtx[0:L, g * 512:(g + 1) * 512],
                                                                start=True, stop=True), reads=[bs.b, dtx.b], writes=[pcs.b])
                        gsl = slice(g * 512, (g + 1) * 512)
                        if first_state and False:
                            pass
                        P.dve.op(lambda e, g=g, gsl=gsl: e.tensor_tensor(
                            out=hst[:, gsl].rearrange("p (a b) -> p a b", a=8), in0=hst[:, gsl].rearrange("p (a b) -> p a b", a=8),
                            in1=self.vap(sm[:, 4, g * 8:(g + 1) * 8], [[1, 8], [0, 64]]), op=ALU.mult),
                            reads=[hst.b, sm.b], writes=[hst.b])
                        P.dve.op(lambda e, gsl=gsl, pcs=pcs: e.tensor_tensor(out=hst[:, gsl], in0=pcs[:, :], in1=hst[:, gsl], op=ALU.add),
                                 reads=[hst.b, pcs.b], writes=[hst.b])
                        if full:
                            P.act.op(lambda e, gsl=gsl: e.copy(out=hsb[:, gsl], in_=hst[:, gsl]), reads=[hst.b], writes=[hsb.b])

                def state_out(dst):
                    for c4 in range(8):
                        pb = self.bank()
                        for q in range(4):
                            cc = c4 * 4 + q
                            P.pe.op(lambda e, pb=pb, q=q, cc=cc: e.transpose(pb[:, q * 128:(q + 1) * 128], hst[:, cc * 128:(cc + 1) * 128], self.ident[:]),
                                    reads=[hst.b, self.ident.b], writes=[pb.b])
                        P.act.op(lambda e, pb=pb, c4=c4: e.copy(out=hio_v[:, c4 * 4:(c4 + 1) * 4, :], in_=pb[:].rearrange("p (a b) -> p a b", a=4)),
                                 reads=[pb.b], writes=[hio.b])
                    P.sp.op(lambda e: e.dma_start(out=dst.rearrange("(c p) n -> p c n", p=128), in_=hio_v), reads=[hio.b], dma=True)

                P.dve.op(lambda e: e.memset(hst[:], 0.0), writes=[hst.b])
                P.dve.op(lambda e: e.memset(dlog[:], 1.0), writes=[dlog.b])
                if SCAN_STEP < 1:
                    return
                for ci in range(8 if SCAN_STEP != 1 else 1):
                    chunk(ci * 128, 128, ci, False, ci == 0)
                if SCAN_STEP < 2:
                    return
                ginD = nc.dram_tensor("gin_sd", [128, 64], F32)
                goutD = nc.dram_tensor("gout_sd", [4 * 128, 64], F32)
                bd1, bd2 = Buf(), Buf()
                P.sp.op(lambda e: e.dma_start(out=ginD[:], in_=dlog[:]), reads=[dlog.b], writes=[bd1], dma=True)
                P.pool.op(lambda e: e.collective_compute("AllGather", ALU.bypass, replica_groups=GROUPS, ins=[ginD[:]], outs=[goutD[:]]),
                          reads=[bd1], writes=[bd2], cc=True)
                P.sp.op(lambda e: e.dma_start(out=gD[:], in_=goutD[:].rearrange("(r p) n -> p r n", p=128)), reads=[bd2], writes=[gD.b], dma=True)
                for r in range(1, 3):
                    P.dve.op(lambda e, r=r: e.tensor_scalar(out=gD[:, r, :], in0=gD[:, r, :], scalar1=self.cs("valid", r, r + 1),
                                                            scalar2=self.cs("nvalid", r, r + 1), op0=ALU.mult, op1=ALU.add),
                             reads=[gD.b, self.cst_t.b], writes=[gD.b])
                for sl in range(8):
                    ginS = nc.dram_tensor("gin_ss%d" % sl, [128, 512], F32)
                    goutS = nc.dram_tensor("gout_ss%d" % sl, [4 * 128, 512], F32)
                    bs1, bs2 = Buf(), Buf()
                    ssl = slice(sl * 512, (sl + 1) * 512)
                    P.sp.op(lambda e, ginS=ginS, ssl=ssl: e.dma_start(out=ginS[:], in_=hst[:, ssl]), reads=[hst.b], writes=[bs1], dma=True)
                    P.pool.op(lambda e, ginS=ginS, goutS=goutS: e.collective_compute("AllGather", ALU.bypass, replica_groups=GROUPS,
                                                                                   ins=[ginS[:]], outs=[goutS[:]]),
                              reads=[bs1], writes=[bs2], cc=True)
                    P.sp.op(lambda e, goutS=goutS: e.dma_start(out=gS_v, in_=goutS[:].rearrange("(r p) n -> p r n", p=128)),
                            reads=[bs2], writes=[gS.b], dma=True)
                    P.dve.op(lambda e: e.tensor_scalar(out=tS[:], in0=gS_v[:, 0, :], scalar1=self.cs("valid", 0, 1), scalar2=None, op0=ALU.mult),
                             reads=[gS.b, self.cst_t.b], writes=[tS.b])
                    for r in range(1, 3):
                        P.dve.op(lambda e, r=r, sl=sl: e.tensor_tensor(
                            out=tS[:].rearrange("p (a b) -> p a b", a=8), in0=tS[:].rearrange("p (a b) -> p a b", a=8),
                            in1=self.vap(gD[:, r, sl * 8:(sl + 1) * 8], [[1, 8], [0, 64]]), op=ALU.mult),
                            reads=[tS.b, gD.b], writes=[tS.b])
                        P.dve.op(lambda e, r=r: e.scalar_tensor_tensor(out=tS[:], in0=gS_v[:, r, :], scalar=self.cs("valid", r, r + 1), in1=tS[:],
                                                                      op0=ALU.mult, op1=ALU.add),
                                 reads=[gS.b, self.cst_t.b, tS.b], writes=[tS.b])
                    P.dve.op(lambda e, ssl=ssl: e.tensor_copy(out=hst[:, ssl], in_=tS[:]), reads=[tS.b], writes=[hst.b])
                    P.act.op(lambda e, ssl=ssl: e.copy(out=hsb[:, ssl], in_=tS[:]), reads=[tS.b], writes=[hsb.b])
                if SCAN_STEP < 3:
                    return
                for ci in range(8 if SCAN_STEP >= 4 else 1):
                    chunk(ci * 128, 128, ci, True, False)
                if SCAN_STEP < 5:
                    return
                state_out(io["ssm2p"])
                if SCAN_STEP < 6:
                    return
                P.sp.op(lambda e: e.dma_start(out=hio_v, in_=io["st2s"].rearrange("(c p) n -> p c n", p=128)), writes=[hio.b], dma=True)
                for c4 in range(8):
                    pb = self.bank()
                    for q in range(4):
                        cc = c4 * 4 + q
                        P.pe.op(lambda e, pb=pb, q=q, cc=cc: e.transpose(pb[:, q * 128:(q + 1) * 128], hio_v[:, cc, :], self.ident[:]),
                                reads=[hio.b, self.ident.b], writes=[pb.b])
                    P.dve.op(lambda e, pb=pb, c4=c4: e.tensor_copy(out=hst[:, c4 * 512:(c4 + 1) * 512], in_=pb[:]), reads=[pb.b], writes=[hst.b])
                    P.act.op(lambda e, c4=c4: e.copy(out=hsb[:, c4 * 512:(c4 + 1) * 512], in_=hst[:, c4 * 512:(c4 + 1) * 512]),
                             reads=[hst.b], writes=[hsb.b])
                chunk(NP, NS, 8, True, False)
                state_out(io["ssm2s"])
                P.barrier()
            if SSD_STAGE < 3:
                return
            with ExitStack() as st:
                ynA = self.sb(st, "s2_ynA", [128, 32, NT], BF16)
                wbs = [self.sb(st, "s2_wo%d" % i, [128, 32, 256], BF16) for i in range(3)]
                P.sp.op(lambda e: e.dma_start(out=ynA[:], in_=d_yn[:].rearrange("p (c t) -> p c t", c=32)), reads=[b_yn], writes=[ynA.b], dma=True)
                for blk in range(8):
                    wb = wbs[blk % 3]
                    self.load_wblock(wb, io["l2_w_out"], blk * 256, 256, kc=32)

                    def cons_o(ps, m, c, nn, blk=blk):
                        P.dve.op(lambda e: e.tensor_tensor(out=self.xT[:, blk * 2 + m, c:c + nn], in0=ps[:, 0:nn],
                                                           in1=self.xT[:, blk * 2 + m, c:c + nn], op=ALU.add),
                                 reads=[ps.b, self.xT.b], writes=[self.xT.b])
                    self.proj_F(wb, 0, 2, ynA, ynA.b, 0, NT, cons_o, kc=32)
                P.barrier()

    def placeholders(self):
        P, io = self.P, self.io
        if not self.todo:
            return
        with ExitStack() as st:
            z = self.sb(st, "zeros", [128, 6144])
            P.pool.op(lambda e: e.memset(z[:], 0.0), writes=[z.b])
            for nm, shp in self.todo:
                r, c = shp
                for r0 in range(0, r, 128):
                    rr = min(128, r - r0)
                    P.sp.op(lambda e, nm=nm, r0=r0, rr=rr, c=c: e.dma_start(out=io[nm][r0:r0 + rr, :], in_=z[0:rr, 0:c]),
                            reads=[z.b], dma=True)
            P.barrier()

    def final_norm(self):
        P, io = self.P, self.io
        with ExitStack() as st:
            sq = self.sb(st, "f_sq", [128, 512])
            rstd = self.sb(st, "f_rstd", [128, 512])
            yF = self.sb(st, "f_yF", [128, KC, 128])
            yTs = [self.sb(st, "f_yT%d" % i, [128, D]) for i in range(2)]
            for t in range(9):
                rows = 128 if t < 8 else NS
                c0 = t * 128
                self.rms_F(None, c0, rows, None, sq, rstd)
                for k in range(KC):
                    eng = P.dve
                    eng.op(lambda e, k=k, c0=c0, rows=rows: e.scalar_tensor_tensor(
                        out=yF[:, k, 0:rows], in0=self.xT[:, k, c0:c0 + rows], scalar=self.cs("fnorm", k, k + 1),
                        in1=rstd[:, 0:rows], op0=ALU.mult, op1=ALU.mult),
                        reads=[self.xT.b, rstd.b, self.cst_t.b], writes=[yF.b])
                yT = yTs[t % 2]
                for k4 in range(4):
                    pb = self.bank()
                    for kk in range(4):
                        k = k4 * 4 + kk
                        P.pe.op(lambda e, pb=pb, k=k, kk=kk, rows=rows: e.transpose(
                            pb[0:rows, kk * 128:(kk + 1) * 128], yF[:, k, 0:rows], self.ident[:]),
                            reads=[yF.b, self.ident.b], writes=[pb.b])
                    if k4 % 2 == 0:
                        P.dve.op(lambda e, pb=pb, yT=yT, k4=k4, rows=rows: e.tensor_copy(
                            out=yT[0:rows, k4 * 512:(k4 + 1) * 512], in_=pb[0:rows, :]),
                            reads=[pb.b], writes=[yT.b])
                    else:
                        P.act.op(lambda e, pb=pb, yT=yT, k4=k4, rows=rows: e.copy(
                            out=yT[0:rows, k4 * 512:(k4 + 1) * 512], in_=pb[0:rows, :]),
                            reads=[pb.b], writes=[yT.b])
                dst = io["yp"][t * 128:(t + 1) * 128, :] if t < 8 else io["ys"]
                P.sp.op(lambda e, dst=dst, yT=yT, rows=rows: e.dma_start(out=dst, in_=yT[0:rows, :]),
                        reads=[yT.b], dma=True)
            P.barrier()


    def load_wblock(self, wb, w_ap, col0, ncols, kc=KC):
        src = w_ap[:, col0:col0 + ncols].rearrange("(k p) n -> p k n", p=128)
        self.P.pool.op(lambda e: e.dma_start(out=wb[:, 0:kc, 0:ncols], in_=src), writes=[wb.b], dma=True)

    def subtiles(self, c0, n):
        out = []
        c = c0
        while c < c0 + n:
            m = min(512 - (c % 512), c0 + n - c)
            out.append((c, m))
            c += m
        return out

    def proj_F(self, wb, wcol0, nchunks, hT, hb, hoff, n, consumer, kc=KC, splits=None):
        P = self.P
        for m in range(nchunks):
            for (c, nn) in (splits if splits is not None else self.subtiles(hoff, n)):
                ps = self.bank()
                for k in range(kc):
                    P.pe.op(lambda e, ps=ps, k=k, m=m, c=c, nn=nn: e.matmul(
                        ps[:, 0:nn], lhsT=wb[:, k, wcol0 + m * 128:wcol0 + (m + 1) * 128], rhs=hT[:, k, c:c + nn],
                        start=(k == 0), stop=(k == kc - 1)),
                        reads=[wb.b, hb], writes=[ps.b])
                consumer(ps, m, c - hoff, nn)

    def mla_layer(self, L):
        P, io, nc = self.P, self.io, self.nc
        w_in = io["l%d_w_in" % L]
        with ExitStack() as stL:
            qanT = self.sb(stL, "m%d_qanT" % L, [128, 4, NT], BF16)
            lat = self.sb(stL, "m%d_lat" % L, [128, 5, NT], BF16)
            sgT = self.sb(stL, "m%d_sgT" % L, [128, KC, NT], BF16)
            self.ckvn_s = self.sb(stL, "m%d_ckvns" % L, [NS, 512], BF16)
            P.pool.op(lambda e: e.memset(lat[:, 4, :], 0.0), writes=[lat.b])
            cstm_t = self.sb(stL, "m%d_cstm" % L, [128, self.nmst])
            self.cstm = cstm_t.t
            P.sp.op(lambda e: e.dma_start(out=cstm_t[:], in_=io["cstm"]), writes=[self.cst_t.b], dma=True)
            with ExitStack() as st:
                wbs = [self.sb(st, "m%d_wb%d" % (L, i), [128, KC, 576], BF16) for i in range(2)]
                hT = self.sb(st, "m%d_hT" % L, [128, KC, 528], BF16)
                sq = self.sb(st, "m%d_sq" % L, [128, 512])
                rstd = self.sb(st, "m%d_rstd" % L, [128, 512])
                qa = self.sb(st, "m%d_qa" % L, [128, 4, 528])
                ckvn = self.sb(st, "m%d_ckvn" % L, [128, 512])
                junk = self.sb(st, "m%d_junk" % L, [128, 512])
                ss = self.sb(st, "m%d_ss" % L, [128, 4])
                kp = self.sb(st, "m%d_kp" % L, [128, 64])
                kr = self.sb(st, "m%d_kr" % L, [128, 4, 32])
                kpo = self.sb(st, "m%d_kpo" % L, [128, 64])
                wi = 0
                for tt in range(2):
                    tc0 = tt * 512
                    n = 512 if tt == 0 else 528
                    for (c, nn) in self.subtiles(tc0, n):
                        self.rms_F(hT, c, nn, "norm%d" % L, sq, rstd, off=c - tc0)
                    wb = wbs[wi % 2]; wi += 1
                    self.load_wblock(wb, w_in, 0, 512)

                    def cons_qa(ps, m, c, nn):
                        P.act.op(lambda e: e.copy(out=qa[:, m, c:c + nn], in_=ps[:, 0:nn]),
                                 reads=[ps.b], writes=[qa.b])
                    self.proj_F(wb, 0, 4, hT, hT.b, 0, n, cons_qa)
                    for (c, nn) in self.subtiles(0, n):
                        pb = self.bank()
                        for m in range(4):
                            P.act.op(lambda e, m=m, c=c, nn=nn: e.activation(out=sq[:, 0:nn], in_=qa[:, m, c:c + nn],
                                                                             func=AF.Square),
                                     reads=[qa.b], writes=[sq.b])
                            P.pe.op(lambda e, m=m, pb=pb, nn=nn: e.matmul(pb[:, 0:nn], lhsT=self.ones[:], rhs=sq[:, 0:nn],
                                                                         start=(m == 0), stop=(m == 3)),
                                    reads=[sq.b, self.ones.b], writes=[pb.b])
                        P.dve.op(lambda e, pb=pb, nn=nn: e.tensor_scalar(out=rstd[:, 0:nn], in0=pb[:, 0:nn],
                                                                        scalar1=1.0 / 512, scalar2=EPS,
                                                                        op0=ALU.mult, op1=ALU.add),
                                 reads=[pb.b], writes=[rstd.b])
                        P.act.op(lambda e, nn=nn: e.activation(out=rstd[:, 0:nn], in_=rstd[:, 0:nn], func=AF.Sqrt),
                                 reads=[rstd.b], writes=[rstd.b])
                        P.dve.op(lambda e, nn=nn: e.reciprocal(out=rstd[:, 0:nn], in_=rstd[:, 0:nn]),
                                 reads=[rstd.b], writes=[rstd.b])
                        for m in range(4):
                            P.dve.op(lambda e, m=m, c=c, nn=nn, tc0=tc0: e.scalar_tensor_tensor(
                                out=qanT[:, m, tc0 + c:tc0 + c + nn], in0=qa[:, m, c:c + nn],
                                scalar=self.cs("qnorm%d" % L, m, m + 1), in1=rstd[:, 0:nn],
                                op0=ALU.mult, op1=ALU.mult),
                                reads=[qa.b, rstd.b, self.cst_t.b], writes=[qanT.b])
                    wb = wbs[wi % 2]; wi += 1
                    self.load_wblock(wb, w_in, 512, 576)
                    chunks = [(tc0 + i * 128, 128) for i in range(4)]
                    if tt == 1:
                        chunks.append((NP, NS))
                    for (c, rows) in chunks:
                        t = c // 128
                        ps1 = self.bank()
                        ps2 = self.bank()
                        for k in range(KC):
                            P.pe.op(lambda e, k=k, c=c, rows=rows, ps1=ps1, wb=wb, tc0=tc0: e.matmul(
                                ps1[0:rows, :], lhsT=hT[:, k, c - tc0:c - tc0 + rows], rhs=wb[:, k, 0:512],
                                start=(k == 0), stop=(k == KC - 1)), reads=[hT.b, wb.b], writes=[ps1.b])
                        for k in range(KC):
                            P.pe.op(lambda e, k=k, c=c, rows=rows, ps2=ps2, wb=wb, tc0=tc0: e.matmul(
                                ps2[0:rows, 0:64], lhsT=hT[:, k, c - tc0:c - tc0 + rows], rhs=wb[:, k, 512:576],
                                start=(k == 0), stop=(k == KC - 1)), reads=[hT.b, wb.b], writes=[ps2.b])
                        P.act.op(lambda e, rows=rows, ps1=ps1: e.activation(out=junk[0:rows, :], in_=ps1[0:rows, :],
                                                                            func=AF.Square, accum_out=ss[0:rows, 0:1]),
                                 reads=[ps1.b], writes=[junk.b, ss.b])
                        P.dve.op(lambda e, rows=rows: e.tensor_scalar(out=ss[0:rows, 1:2], in0=ss[0:rows, 0:1],
                                                                      scalar1=1.0 / 512, scalar2=EPS,
                                                                      op0=ALU.mult, op1=ALU.add),
                                 reads=[ss.b], writes=[ss.b])
                        P.act.op(lambda e, rows=rows: e.activation(out=ss[0:rows, 2:3], in_=ss[0:rows, 1:2], func=AF.Sqrt),
                                 reads=[ss.b], writes=[ss.b])
                        P.dve.op(lambda e, rows=rows: e.reciprocal(out=ss[0:rows, 3:4], in_=ss[0:rows, 2:3]),
                                 reads=[ss.b], writes=[ss.b])
                        P.dve.op(lambda e, rows=rows, ps1=ps1: e.scalar_tensor_tensor(
                            out=ckvn[0:rows, :], in0=ps1[0:rows, :], scalar=ss[0:rows, 3:4],
                            in1=self.cs("kvnorm%d" % L)[0:rows, :], op0=ALU.mult, op1=ALU.mult),
                            reads=[ps1.b, ss.b, self.cst_t.b], writes=[ckvn.b])
                        if c >= NP:
                            P.pool.op(lambda e, rows=rows: e.tensor_copy(out=self.ckvn_s[0:rows, :], in_=ckvn[0:rows, :]),
                                      reads=[ckvn.b], writes=[self.ckvn_s.b])
                        cosv = self.cs("cosT", t * 32, t * 32 + 32)
                        sinv = self.cs("sinT", t * 32, t * 32 + 32)
                        P.act.op(lambda e, rows=rows, ps2=ps2: e.copy(out=kp[0:rows, :], in_=ps2[0:rows, 0:64]),
                                 reads=[ps2.b], writes=[kp.b])
                        for qi, (xa, tb) in enumerate(((0, cosv), (32, sinv), (0, sinv), (32, cosv))):
                            P.dve.op(lambda e, rows=rows, qi=qi, xa=xa, tb=tb: e.tensor_tensor(
                                out=kr[0:rows, qi, :], in0=kp[0:rows, xa:xa + 32], in1=tb[0:rows, :], op=ALU.mult),
                                reads=[kp.b, self.cst_t.b], writes=[kr.b])
                        P.dve.op(lambda e, rows=rows: e.tensor_tensor(out=kpo[0:rows, 0:32], in0=kr[0:rows, 0, :],
                                                                      in1=kr[0:rows, 1, :], op=ALU.subtract),
                                 reads=[kr.b], writes=[kpo.b])
                        P.dve.op(lambda e, rows=rows: e.tensor_tensor(out=kpo[0:rows, 32:64], in0=kr[0:rows, 2, :],
                                                                      in1=kr[0:rows, 3, :], op=ALU.add),
                                 reads=[kr.b], writes=[kpo.b])
                        if c < NP:
                            d1 = io["o%d_ckv_p" % L][c:c + rows, :]
                            d2 = io["o%d_kpe_p" % L][c:c + rows, :]
                        else:
                            d1 = io["o%d_ckv_s" % L]
                            d2 = io["o%d_kpe_s" % L]
                        P.sp.op(lambda e, d1=d1, rows=rows: e.dma_start(out=d1, in_=ckvn[0:rows, :]),
                                reads=[ckvn.b], dma=True)
                        P.sp.op(lambda e, d2=d2, rows=rows: e.dma_start(out=d2, in_=kpo[0:rows, :]),
                                reads=[kpo.b], dma=True)
                        pt = self.bank()
                        for m in range(4):
                            P.pe.op(lambda e, m=m, rows=rows, pt=pt: e.transpose(
                                pt[:, m * 128:m * 128 + rows], ckvn[0:rows, m * 128:(m + 1) * 128],
                                self.ident[0:rows, 0:rows]), reads=[ckvn.b, self.ident.b], writes=[pt.b])
                        P.act.op(lambda e, rows=rows, c=c, pt=pt: e.copy(
                            out=lat[:, 0:4, c:c + rows],
                            in_=pt[:].rearrange("p (a b) -> p a b", a=4)[:, :, 0:rows]),
                            reads=[pt.b], writes=[lat.b])
                        pt2 = self.bank()
                        P.pe.op(lambda e, rows=rows, pt2=pt2: e.transpose(pt2[0:64, 0:rows], kpo[0:rows, :],
                                                                          self.ident[0:rows, 0:rows]),
                                reads=[kpo.b, self.ident.b], writes=[pt2.b])
                        P.act.op(lambda e, rows=rows, c=c, pt2=pt2: e.copy(out=lat[0:64, 4, c:c + rows],
                                                                           in_=pt2[0:64, 0:rows]),
                                 reads=[pt2.b], writes=[lat.b])
                    for gb in range(4):
                        wb = wbs[wi % 2]; wi += 1
                        self.load_wblock(wb, w_in, 1088 + gb * 512, 512)

                        def cons_g(ps, m, c, nn, gb=gb, tc0=tc0):
                            P.act.op(lambda e: e.activation(out=sgT[:, gb * 4 + m, tc0 + c:tc0 + c + nn],
                                                            in_=ps[:, 0:nn], func=AF.Silu),
                                     reads=[ps.b], writes=[sgT.b])
                        self.proj_F(wb, 0, 4, hT, hT.b, 0, n, cons_g)
                P.barrier()
            if ATTN:
                self.mla_attention(L, qanT, lat, sgT)
            with ExitStack() as st:
                wbs = [self.sb(st, "m%d_wo%d" % (L, i), [128, KC, 512], BF16) for i in range(2)]
                for blk in range(4):
                    wb = wbs[blk % 2]
                    self.load_wblock(wb, io["l%d_w_out" % L], blk * 512, 512)

                    def cons_o(ps, m, c, nn, blk=blk):
                        P.dve.op(lambda e: e.tensor_tensor(out=self.xT[:, blk * 4 + m, c:c + nn],
                                                           in0=ps[:, 0:nn], in1=self.xT[:, blk * 4 + m, c:c + nn],
                                                           op=ALU.add),
                                 reads=[ps.b, self.xT.b], writes=[self.xT.b])
                    self.proj_F(wb, 0, 4, sgT, sgT.b, 0, NT, cons_o)
                P.barrier()


LAYERS = [0, 1, 2, 3]
NOFF = 8
import os
SSD_STAGE = int(os.environ.get('SSD_STAGE', '3'))
SCAN_STEP = int(os.environ.get('SCAN_STEP', '9'))
CHUNK_OPS = int(os.environ.get('CHUNK_OPS', '9'))
ATTN = True


def build_program(coff, ncst, moff, nmst):
    nc = bass.Bass("TRN2", target_bir_lowering=False)
    k = K(nc, coff, ncst, moff, nmst)
    k.build()
    return nc


def kernel(**inp):
    inp = {k: np.asarray(v) for k, v in inp.items()}
    consts = [build_consts(c, inp) for c in range(NCORES)]
    coff = consts[0][0].off
    ncst = consts[0][0].n
    moff = consts[0][1].off
    nmst = consts[0][1].n
    nc = build_program(coff, ncst, moff, nmst)
    in_maps = []
    for c in range(NCORES):
        b, j = c // 4, c % 4
        m = {
            "xp": np.ascontiguousarray(inp["x_prompt"][b, j * NP:(j + 1) * NP]),
            "xs": np.ascontiguousarray(inp["x_sample"][c]),
            "cst": consts[c][0].build(),
            "cstm": consts[c][1].build(),
        }
        for L in (0, 3):
            if L not in LAYERS:
                continue
            m["c%d_ckv" % L] = np.ascontiguousarray(inp["cache_l%d_ckv" % L][c])
            m["c%d_kpe" % L] = np.ascontiguousarray(inp["cache_l%d_kpe" % L][c])
            m["l%d_w_in" % L] = inp["l%d_w_in" % L]
            m["l%d_w_qb" % L] = inp["l%d_w_qb" % L].reshape(512, 16 * 192)
            m["l%d_w_kvb" % L] = inp["l%d_w_kvb" % L].reshape(512, 16 * 256)
            m["l%d_w_out" % L] = inp["l%d_w_out" % L]
        if 2 in LAYERS:
            m["st2c"] = np.ascontiguousarray(inp["state_l2_conv"][c])
            m["st2s"] = np.ascontiguousarray(inp["state_l2_ssm"][c]).reshape(4096, 128)
            m["l2_w_in"] = inp["l2_w_in"]
            m["l2_w_out"] = inp["l2_w_out"]
        if 1 in LAYERS:
            m["st1"] = np.ascontiguousarray(inp["state_l1_conv"][c])
            m["l1_w_in"] = inp["l1_w_in"]
            m["l1_w_out"] = inp["l1_w_out"]
        in_maps.append(m)
    res = run_bass_kernel_spmd(nc, in_maps, core_ids=list(range(NCORES)))
    R = res.results

    def cat_prompt(name):
        return np.stack([np.concatenate([R[b * 4 + j][name] for j in range(4)], axis=0) for b in range(2)], axis=0)

    def cat_sample(name):
        return np.stack([R[c][name] for c in range(NCORES)], axis=0)

    def last_core(name):
        return np.stack([R[b * 4 + 3][name] for b in range(2)], axis=0)

    outs = {}
    outs["y_prompt"] = cat_prompt("yp")
    outs["y_sample"] = cat_sample("ys")
    for L in (0, 3):
        outs["l%d_ckv_p" % L] = cat_prompt("o%d_ckv_p" % L)
        outs["l%d_kpe_p" % L] = cat_prompt("o%d_kpe_p" % L)
        outs["l%d_ckv_s" % L] = cat_sample("o%d_ckv_s" % L)
        outs["l%d_kpe_s" % L] = cat_sample("o%d_kpe_s" % L)
    outs["l1_conv_p"] = last_core("conv1p")
    outs["l1_conv_s"] = cat_sample("conv1s")
    outs["l2_conv_p"] = last_core("conv2p")
    outs["l2_ssm_p"] = last_core("ssm2p").reshape(2, 64, 64, 128)
    outs["l2_conv_s"] = cat_sample("conv2s")
    outs["l2_ssm_s"] = cat_sample("ssm2s").reshape(NCORES, 64, 64, 128)
    order = ["y_prompt", "y_sample", "l0_ckv_p", "l0_kpe_p", "l0_ckv_s", "l0_kpe_s", "l1_conv_p", "l1_conv_s",
             "l2_conv_p", "l2_ssm_p", "l2_conv_s", "l2_ssm_s", "l3_ckv_p", "l3_kpe_p", "l3_ckv_s", "l3_kpe_s"]
    return tuple(np.ascontiguousarray(outs[k], dtype=np.float32) for k in order)
```

```python
import numpy as np
from contextlib import ExitStack
import concourse.bass as bass
import concourse.mybir as mybir
from concourse.bass_utils import run_bass_kernel_spmd

F32 = mybir.dt.float32
BF16 = mybir.dt.bfloat16
ALU = mybir.AluOpType
AF = mybir.ActivationFunctionType
AX = mybir.AxisListType

NCORES = 8
D = 2048
KC = 16
NP = 1024
NS = 16
NT = NP + NS
PAST = 4096
EPS = 1e-6
NEG = -30000.0
MLA_SCALE = 192 ** -0.5
GROUPS = [[0, 1, 2, 3], [4, 5, 6, 7]]


class Buf:
    __slots__ = ("name", "w", "r")

    def __init__(self, name=""):
        self.name = name
        self.w = None
        self.r = []


class Eng:
    def __init__(self, prog, name):
        self.prog = prog
        self.name = name
        self.ops = []
        self.count = 0
        self.waited = {}
        self.dma_n = 0
        self.dma_vals = {}
        self.pending = []

    def _need(self, tok, waits, same_ok):
        if tok is None:
            return
        eng, key, val = tok
        if eng is self and same_ok:
            return
        if self.waited.get(key, 0) >= val:
            return
        self.waited[key] = val
        waits.append((key, val))

    def op(self, fn, reads=(), writes=(), dma=False, cc=False):
        waits = []
        for key, val in self.pending:
            if self.waited.get(key, 0) < val:
                self.waited[key] = val
                waits.append((key, val))
        self.pending = []
        is_pe = self.name == "tensor"
        asyncop = dma or cc
        for b in reads:
            self._need(b.w, waits, same_ok=(is_pe and not asyncop))
        for b in writes:
            self._need(b.w, waits, same_ok=(not asyncop))
            for t in b.r:
                self._need(t, waits, same_ok=(not asyncop))
        if cc:
            key = ("cc", self.name)
            prev = self.dma_vals.get(key, 0)
            if prev and self.waited.get(key, 0) < prev:
                self.waited[key] = prev
                waits.append((key, prev))
            val = prev + 1
            self.dma_vals[key] = val
            tok = (None, key, val)
            inc = (key, 1)
        elif dma:
            k = self.dma_n % self.prog.ndma_sems
            self.dma_n += 1
            key = ("dma", self.name, k)
            prev = self.dma_vals.get(key, 0)
            if prev and self.waited.get(key, 0) < prev:
                self.waited[key] = prev
                waits.append((key, prev))
            val = prev + 16
            self.dma_vals[key] = val
            tok = (None, key, val)
            inc = (key, 16)
        else:
            self.count += 1
            key = ("eng", self.name)
            tok = (self, key, self.count)
            inc = (key, 1)
        self.ops.append((fn, waits, inc))
        for b in reads:
            b.r.append(tok)
        for b in writes:
            b.w = tok
            b.r = []
        return tok


class Prog:
    def __init__(self, nc, ndma_sems=8):
        self.nc = nc
        self.ndma_sems = ndma_sems
        self.pe = Eng(self, "tensor")
        self.act = Eng(self, "scalar")
        self.dve = Eng(self, "vector")
        self.pool = Eng(self, "gpsimd")
        self.sp = Eng(self, "sync")
        self.engs = [self.pe, self.act, self.dve, self.pool, self.sp]

    def all_tokens(self):
        toks = []
        for e in self.engs:
            if e.count:
                toks.append((("eng", e.name), e.count))
            for key, val in e.dma_vals.items():
                toks.append((key, val))
        return toks

    def barrier(self):
        toks = self.all_tokens()
        for e in self.engs:
            e.pending = list(toks)

    def emit(self, stack):
        nc = self.nc
        sems = {}
        for e in self.engs:
            sems[("eng", e.name)] = stack.enter_context(nc.semaphore("s_" + e.name))
            sems[("cc", e.name)] = stack.enter_context(nc.semaphore("c_" + e.name))
            for k in range(self.ndma_sems):
                sems[("dma", e.name, k)] = stack.enter_context(nc.semaphore("d_%s_%d" % (e.name, k)))
        fw = []
        for key, val in self.all_tokens():
            if self.sp.waited.get(key, 0) < val:
                self.sp.waited[key] = val
                fw.append((key, val))
        self.sp.ops.append((None, fw, None))
        block = stack.enter_context(nc.Block())

        def mk(e):
            def body(eng):
                for fn, waits, inc in e.ops:
                    for key, val in waits:
                        eng.wait_ge(sems[key], val)
                    if fn is None:
                        continue
                    ins = fn(eng)
                    ins.then_inc(sems[inc[0]], inc[1])
            return body

        block.tensor(mk(self.pe))
        block.scalar(mk(self.act))
        block.vector(mk(self.dve))
        block.gpsimd(mk(self.pool))
        block.sync(mk(self.sp))


class T:
    def __init__(self, t, name=""):
        self.t = t
        self.b = Buf(name)

    def __getitem__(self, idx):
        return self.t[idx]


class Cols:
    def __init__(self):
        self.off = {}
        self.n = 0
        self.parts = []

    def add(self, name, arr):
        arr = np.ascontiguousarray(arr, dtype=np.float32)
        assert arr.shape[0] == 128, (name, arr.shape)
        arr = arr.reshape(128, -1)
        self.off[name] = (self.n, arr.shape[1])
        self.n += arr.shape[1]
        self.parts.append(arr)

    def build(self):
        return np.ascontiguousarray(np.concatenate(self.parts, axis=1))


def fvec(v):
    v = np.asarray(v, np.float32)
    return np.ascontiguousarray(v.reshape(-1, 128).T)


def brow(v):
    v = np.asarray(v, np.float32).reshape(1, -1)
    return np.ascontiguousarray(np.broadcast_to(v, (128, v.shape[1])))


def rope_tables(pos):
    half = 32
    freqs = (10000.0 ** (-np.arange(half, dtype=np.float32) / np.float32(half))).astype(np.float32)
    ang = pos.astype(np.float32)[:, None] * freqs[None, :]
    return np.cos(ang).astype(np.float32), np.sin(ang).astype(np.float32)


def build_consts(core, inp):
    j = core % 4
    C = Cols()
    M = Cols()
    for L in (0, 3):
        C.add("norm%d" % L, fvec(inp["l%d_norm" % L]))
        C.add("qnorm%d" % L, fvec(inp["l%d_q_norm" % L]))
        M.add("kvnorm%d" % L, brow(inp["l%d_kv_norm" % L]))
    C.add("fnorm", fvec(inp["final_norm"]))
    C.add("norm1", fvec(inp["l1_norm"]))
    dww = np.asarray(inp["l1_dw_w"], np.float32)
    C.add("dww", np.ascontiguousarray(dww.reshape(31, 16, 128).transpose(2, 1, 0)))
    C.add("dwb", fvec(inp["l1_dw_b"]))
    C.add("lng", fvec(inp["l1_ln_g"]))
    C.add("lnb", fvec(inp["l1_ln_b"]))
    C.add("norm2", fvec(inp["l2_norm"]))
    cw = np.asarray(inp["l2_conv_w"], np.float32)
    C.add("cw2", np.ascontiguousarray(cw.reshape(4, 48, 128).transpose(2, 1, 0)))
    C.add("cb2", fvec(inp["l2_conv_b"]))
    C.add("dtb", brow(inp["l2_dt_bias"]))
    C.add("alog", brow(inp["l2_a_log"]))
    C.add("dskip", brow(inp["l2_d_skip"]))
    C.add("gn2", fvec(inp["l2_gnorm"]))
    selp = np.zeros((128, 4), np.float32)
    if j > 0:
        selp[:, j - 1] = 1.0
    C.add("selp", selp)
    val = np.zeros((128, 3), np.float32)
    for r in range(3):
        val[:, r] = 1.0 if r < j else 0.0
    C.add("valid", val)
    C.add("nvalid", 1.0 - val)
    pos = np.concatenate([j * NP + np.arange(NP), PAST + np.arange(NS)])
    cos, sin = rope_tables(pos)
    cosT = np.zeros((128, 9, 32), np.float32)
    sinT = np.zeros((128, 9, 32), np.float32)
    for t in range(8):
        cosT[:, t] = cos[t * 128:(t + 1) * 128]
        sinT[:, t] = sin[t * 128:(t + 1) * 128]
    cosT[:NS, 8] = cos[NP:]
    sinT[:NS, 8] = sin[NP:]
    M.add("cosT", cosT)
    M.add("sinT", sinT)
    cosF = np.zeros((128, NT), np.float32)
    sinF = np.zeros((128, NT), np.float32)
    cosF[0:32] = cos.T
    cosF[32:64] = cos.T
    sinF[0:32] = sin.T
    sinF[32:64] = sin.T
    M.add("cosF", cosF)
    M.add("sinF", sinF)
    ab = np.zeros((128, 3), np.float32)
    for r in range(3):
        ab[:, r] = 0.0 if r < j else NEG
    M.add("abias", ab)
    return C, M


class K:
    def __init__(self, nc, coff, ncst, moff, nmst):
        self.nc = nc
        self.P = Prog(nc)
        self.coff = coff
        self.ncst = ncst
        self.moff = moff
        self.nmst = nmst
        self.psn = 0

    def sb(self, st, name, shape, dt=F32):
        return T(st.enter_context(self.nc.sbuf_tensor("sb_" + name, shape, dt)), name)

    def cs(self, name, k0=0, k1=None):
        if name in self.coff:
            o, n = self.coff[name]
            t = self.cst
        else:
            o, n = self.moff[name]
            t = self.cstm
        if k1 is None:
            k1 = n
        return t[:, o + k0:o + k1]

    def bank(self):
        b = self.banks[self.psn % 6]
        self.psn += 1
        return b

    def build(self):
        nc = self.nc
        P = self.P
        dt = nc.dram_tensor
        io = {}

        def din(name, shape):
            io[name] = dt(name, shape, F32, kind="ExternalInput").ap()

        def dout(name, shape):
            io[name] = dt(name, shape, F32, kind="ExternalOutput").ap()

        din("xp", [NP, D])
        din("xs", [NS, D])
        din("cst", [128, self.ncst])
        din("cstm", [128, self.nmst])
        for L in (0, 3):
            if L not in LAYERS:
                continue
            din("c%d_ckv" % L, [PAST, 512])
            din("c%d_kpe" % L, [PAST, 64])
            din("l%d_w_in" % L, [D, 3136])
            din("l%d_w_qb" % L, [512, 16 * 192])
            din("l%d_w_kvb" % L, [512, 16 * 256])
            din("l%d_w_out" % L, [D, D])
            dout("o%d_ckv_p" % L, [NP, 512])
            dout("o%d_kpe_p" % L, [NP, 64])
            dout("o%d_ckv_s" % L, [NS, 512])
            dout("o%d_kpe_s" % L, [NS, 64])
        dout("yp", [NP, D])
        dout("ys", [NS, D])
        if 1 in LAYERS:
            din("st1", [30, D])
            din("l1_w_in", [D, 6144])
            din("l1_w_out", [D, D])
            dout("conv1p", [30, D])
            dout("conv1s", [30, D])
        if 2 in LAYERS:
            din("st2c", [3, 6144])
            din("st2s", [4096, 128])
            din("l2_w_in", [D, 10304])
            din("l2_w_out", [4096, D])
            dout("conv2p", [3, 6144])
            dout("conv2s", [3, 6144])
            dout("ssm2p", [4096, 128])
            dout("ssm2s", [4096, 128])
        self.todo = []
        for L in (0, 3):
            if L not in LAYERS:
                for nm, shp in (("o%d_ckv_p" % L, [NP, 512]), ("o%d_kpe_p" % L, [NP, 64]),
                                ("o%d_ckv_s" % L, [NS, 512]), ("o%d_kpe_s" % L, [NS, 64])):
                    dout(nm, shp)
                    self.todo.append((nm, shp))
        if 1 not in LAYERS:
            for nm, shp in (("conv1p", [30, D]), ("conv1s", [30, D])):
                dout(nm, shp)
                self.todo.append((nm, shp))
        if 2 not in LAYERS:
            for nm, shp in (("conv2p", [3, 6144]), ("conv2s", [3, 6144]), ("ssm2p", [4096, 128]), ("ssm2s", [4096, 128])):
                dout(nm, shp)
                self.todo.append((nm, shp))
        self.io = io
        self.gin = dt("gin", [128, 5120], BF16)
        self.gout = dt("gout", [4 * 128, 5120], BF16)

        with ExitStack() as st:
            self.st = st
            self.cst_t = self.sb(st, "cst", [128, self.ncst])
            self.cst = self.cst_t.t
            self.xT = self.sb(st, "xT", [128, KC, NT])
            self.ident = self.sb(st, "ident", [128, 128])
            self.identb = self.sb(st, "identb", [128, 128], BF16)
            self.ones = self.sb(st, "ones", [128, 128])
            self.onesb = self.sb(st, "onesb", [128, 128], BF16)
            self.banks = [T(st.enter_context(nc.psum_tensor("ps%d" % i, [128, 512], F32)), "ps%d" % i)
                          for i in range(8)]
            P.sp.op(lambda e: e.dma_start(out=self.cst[:], in_=io["cst"]), writes=[self.cst_t.b], dma=True)
            P.pool.op(lambda e: e.memset(self.ident[:], 0.0), writes=[self.ident.b])
            P.pool.op(lambda e: e.affine_select(out=self.ident[:], in_=self.ident[:], pattern=[[-1, 128]],
                                                compare_op=ALU.not_equal, fill=1.0, base=0,
                                                channel_multiplier=1),
                      reads=[self.ident.b], writes=[self.ident.b])
            P.pool.op(lambda e: e.tensor_copy(out=self.identb[:], in_=self.ident[:]),
                      reads=[self.ident.b], writes=[self.identb.b])
            P.pool.op(lambda e: e.memset(self.ones[:], 1.0), writes=[self.ones.b])
            P.pool.op(lambda e: e.memset(self.onesb[:], 1.0), writes=[self.onesb.b])
            self.load_x()
            for L in LAYERS:
                if L in (0, 3):
                    self.mla_layer(L)
                elif L == 1:
                    self.conv_layer()
                elif L == 2:
                    self.ssd_layer()
            self.final_norm()
            self.placeholders()
            P.emit(st)

    def load_x(self):
        P, io = self.P, self.io
        with ExitStack() as st:
            xin = [self.sb(st, "xin%d" % i, [128, D]) for i in range(2)]
            for t in range(9):
                rows = 128 if t < 8 else NS
                src = io["xp"][t * 128:(t + 1) * 128, :] if t < 8 else io["xs"]
                xi = xin[t % 2]
                P.sp.op(lambda e, xi=xi, src=src, rows=rows: e.dma_start(out=xi[0:rows, :], in_=src),
                        writes=[xi.b], dma=True)
                for k4 in range(4):
                    pb = self.bank()
                    for kk in range(4):
                        k = k4 * 4 + kk
                        P.pe.op(lambda e, pb=pb, xi=xi, k=k, kk=kk, rows=rows: e.transpose(
                            pb[:, kk * 128:kk * 128 + rows], xi[0:rows, k * 128:(k + 1) * 128],
                            self.ident[0:rows, 0:rows]),
                            reads=[xi.b, self.ident.b], writes=[pb.b])
                    eng = P.dve if k4 % 2 == 0 else P.act
                    dst = self.xT[:, k4 * 4:(k4 + 1) * 4, t * 128:t * 128 + rows]
                    srcp = pb[:].rearrange("p (a b) -> p a b", a=4)[:, :, 0:rows]
                    if eng is P.dve:
                        eng.op(lambda e, dst=dst, srcp=srcp: e.tensor_copy(out=dst, in_=srcp),
                               reads=[pb.b], writes=[self.xT.b])
                    else:
                        eng.op(lambda e, dst=dst, srcp=srcp: e.copy(out=dst, in_=srcp),
                               reads=[pb.b], writes=[self.xT.b])
            P.barrier()

    def rms_F(self, hT, c0, n, gname, sq, rstd, off=0, src=None):
        src = self.xT if src is None else src
        return self._rms_F(hT, c0, n, gname, sq, rstd, off, src)

    def _rms_F(self, hT, c0, n, gname, sq, rstd, off, src):
        P = self.P
        pb = self.bank()
        for k in range(KC):
            P.act.op(lambda e, k=k: e.activation(out=sq[:, 0:n], in_=src[:, k, c0:c0 + n], func=AF.Square),
                     reads=[src.b], writes=[sq.b])
            P.pe.op(lambda e, k=k: e.matmul(pb[:, 0:n], lhsT=self.ones[:], rhs=sq[:, 0:n],
                                           start=(k == 0), stop=(k == KC - 1)),
                    reads=[sq.b, self.ones.b], writes=[pb.b])
        P.dve.op(lambda e: e.tensor_scalar(out=rstd[:, 0:n], in0=pb[:, 0:n], scalar1=1.0 / D, scalar2=EPS,
                                           op0=ALU.mult, op1=ALU.add),
                 reads=[pb.b], writes=[rstd.b])
        P.act.op(lambda e: e.activation(out=rstd[:, 0:n], in_=rstd[:, 0:n], func=AF.Sqrt),
                 reads=[rstd.b], writes=[rstd.b])
        P.dve.op(lambda e: e.reciprocal(out=rstd[:, 0:n], in_=rstd[:, 0:n]), reads=[rstd.b], writes=[rstd.b])
        if hT is None:
            return
        for k in range(KC):
            eng = P.dve
            eng.op(lambda e, k=k: e.scalar_tensor_tensor(out=hT[:, k, off:off + n], in0=src[:, k, c0:c0 + n],
                                                          scalar=self.cs(gname, k, k + 1), in1=rstd[:, 0:n],
                                                          op0=ALU.mult, op1=ALU.mult),
                   reads=[src.b, rstd.b, self.cst_t.b], writes=[hT.b])


    def kv_from_lat(self, wkv, latm, src_b, kT, vS, ntok=1024):
        P = self.P
        for (c, nn) in self.subtiles(0, ntok):
            ps = self.bank()
            for m in range(4):
                P.pe.op(lambda e, m=m, c=c, nn=nn, ps=ps: e.matmul(ps[:, 0:nn], lhsT=wkv[:, m, 0:128], rhs=latm(m, c, c + nn),
                                                                  start=(m == 0), stop=(m == 3)),
                        reads=[wkv.b, src_b], writes=[ps.b])
            P.act.op(lambda e, c=c, nn=nn, ps=ps: e.copy(out=kT[:, c:c + nn], in_=ps[:, 0:nn]), reads=[ps.b], writes=[kT.b])
        ntc = (ntok + 127) // 128
        for tc4 in range((ntc + 3) // 4):
            ps = self.bank()
            cnt = min(4, ntc - tc4 * 4)
            rows = min(128, ntok)
            for tq in range(cnt):
                tc = tc4 * 4 + tq
                for m in range(4):
                    P.pe.op(lambda e, m=m, tc=tc, tq=tq, ps=ps, rows=rows: e.matmul(
                        ps[0:rows, tq * 128:(tq + 1) * 128], lhsT=latm(m, tc * 128, tc * 128 + rows), rhs=wkv[:, m, 128:256],
                        start=(m == 0), stop=(m == 3)), reads=[wkv.b, src_b], writes=[ps.b])
            P.dve.op(lambda e, tc4=tc4, cnt=cnt, ps=ps, rows=rows: e.tensor_copy(
                out=vS[0:rows, tc4 * 4:tc4 * 4 + cnt, :],
                in_=ps[0:rows, 0:cnt * 128].rearrange("p (a b) -> p a b", a=cnt)),
                reads=[ps.b], writes=[vS.b])

    def mla_attention(self, L, qanT, lat, sgT):
        P, io, nc = self.P, self.io, self.nc
        gins = [nc.dram_tensor("gin%d_%d" % (L, a), [128, NP], BF16) for a in range(5)]
        gouts = [nc.dram_tensor("gout%d_%d" % (L, a), [4 * 128, NP], BF16) for a in range(5)]
        b_gin = [Buf() for a in range(5)]
        b_gout = [Buf() for a in range(5)]
        for a in range(5):
            P.sp.op(lambda e, a=a: e.dma_start(out=gins[a][:], in_=lat[:, a, 0:NP]),
                    reads=[lat.b], writes=[b_gin[a]], dma=True)
            P.pool.op(lambda e, a=a: e.collective_compute("AllGather", ALU.bypass, replica_groups=GROUPS,
                                                          ins=[gins[a][:]], outs=[gouts[a][:]]),
                      reads=[b_gin[a]], writes=[b_gout[a]], cc=True)
        w_qb = io["l%d_w_qb" % L]
        w_kvb = io["l%d_w_kvb" % L]
        obank = [self.banks[6], self.banks[7]]
        with ExitStack() as stA:
            hb = []
            for i in range(2):
                hb.append(dict(
                    wq=self.sb(stA, "a%d_wq%d" % (L, i), [128, 4, 192], BF16),
                    wkv=self.sb(stA, "a%d_wkv%d" % (L, i), [128, 4, 256], BF16),
                    wqr=self.sb(stA, "a%d_wqr%d" % (L, i), [128, 4, 64], BF16),
                    qn=self.sb(stA, "a%d_qn%d" % (L, i), [128, NT], BF16),
                    qpe=self.sb(stA, "a%d_qpe%d" % (L, i), [128, NT], BF16)))
            t1 = self.sb(stA, "a%d_t1" % L, [128, 512])
            t2 = self.sb(stA, "a%d_t2" % L, [128, 512])
            segb = [(self.sb(stA, "a%d_kT%d" % (L, i), [128, 1024], BF16),
                     self.sb(stA, "a%d_vS%d" % (L, i), [128, 8, 128], BF16)) for i in range(2)]
            pTs = [self.sb(stA, "a%d_pT%d" % (L, i), [128, 512], BF16) for i in range(3)]
            acc = [self.sb(stA, "a%d_acc%d" % (L, i), [128, 512]) for i in range(2)]
            rinv = self.sb(stA, "a%d_rinv" % L, [128, 512])
            otmp = self.sb(stA, "a%d_otmp" % L, [128, 512])
            qs_n = self.sb(stA, "a%d_qsn" % L, [128, 16, NS], BF16)
            qs_pe = self.sb(stA, "a%d_qspe" % L, [128, 16, NS], BF16)
            cosF = self.cs("cosF")
            sinF = self.cs("sinF")
            pcount = 0
            with ExitStack() as stG:
                gat = self.sb(stG, "a%d_gat" % L, [128, 3, 5, NP], BF16)
                for r in range(3):
                    for a in range(5):
                        P.sp.op(lambda e, r=r, a=a: e.dma_start(out=gat[:, r, a, :], in_=gouts[a][r * 128:(r + 1) * 128, :]),
                                reads=[b_gout[a]], writes=[gat.b], dma=True)
                for h in range(16):
                    B = hb[h % 2]
                    wq, wkv, wqr, qn, qpe = B["wq"], B["wkv"], B["wqr"], B["qn"], B["qpe"]
                    P.pool.op(lambda e, h=h, wq=wq: e.dma_start(
                        out=wq[:], in_=w_qb[:, h * 192:(h + 1) * 192].rearrange("(k p) n -> p k n", p=128)),
                        writes=[wq.b], dma=True)
                    P.pool.op(lambda e, h=h, wkv=wkv: e.dma_start(
                        out=wkv[:], in_=w_kvb[:, h * 256:(h + 1) * 256].rearrange("(k p) n -> p k n", p=128)),
                        writes=[wkv.b], dma=True)
                    P.act.op(lambda e, wq=wq, wqr=wqr: e.mul(out=wqr[:, :, 0:32], in_=wq[:, :, 160:192], mul=-1.0),
                             reads=[wq.b], writes=[wqr.b])
                    P.dve.op(lambda e, wq=wq, wqr=wqr: e.tensor_copy(out=wqr[:, :, 32:64], in_=wq[:, :, 128:160]),
                             reads=[wq.b], writes=[wqr.b])
                    for (c, nn) in self.subtiles(0, NT):
                        ps = self.bank()
                        ps2 = self.bank()
                        ps3 = self.bank()
                        for m in range(4):
                            P.pe.op(lambda e, m=m, c=c, nn=nn, ps=ps, wq=wq: e.matmul(
                                ps[:, 0:nn], lhsT=wq[:, m, 0:128], rhs=qanT[:, m, c:c + nn], start=(m == 0), stop=(m == 3)),
                                reads=[wq.b, qanT.b], writes=[ps.b])
                        for m in range(4):
                            P.pe.op(lambda e, m=m, c=c, nn=nn, ps2=ps2, wq=wq: e.matmul(
                                ps2[0:64, 0:nn], lhsT=wq[:, m, 128:192], rhs=qanT[:, m, c:c + nn], start=(m == 0), stop=(m == 3)),
                                reads=[wq.b, qanT.b], writes=[ps2.b])
                        for m in range(4):
                            P.pe.op(lambda e, m=m, c=c, nn=nn, ps3=ps3, wqr=wqr: e.matmul(
                                ps3[0:64, 0:nn], lhsT=wqr[:, m, :], rhs=qanT[:, m, c:c + nn], start=(m == 0), stop=(m == 3)),
                                reads=[wqr.b, qanT.b], writes=[ps3.b])
                        P.act.op(lambda e, c=c, nn=nn, ps=ps, qn=qn: e.copy(out=qn[:, c:c + nn], in_=ps[:, 0:nn]),
                                 reads=[ps.b], writes=[qn.b])
                        P.dve.op(lambda e, c=c, nn=nn, ps2=ps2: e.tensor_tensor(out=t1[0:64, 0:nn], in0=ps2[0:64, 0:nn],
                                                                                in1=cosF[0:64, c:c + nn], op=ALU.mult),
                                 reads=[ps2.b, self.cst_t.b], writes=[t1.b])
                        P.dve.op(lambda e, c=c, nn=nn, ps3=ps3: e.tensor_tensor(out=t2[0:64, 0:nn], in0=ps3[0:64, 0:nn],
                                                                                in1=sinF[0:64, c:c + nn], op=ALU.mult),
                                 reads=[ps3.b, self.cst_t.b], writes=[t2.b])
                        P.dve.op(lambda e, c=c, nn=nn, qpe=qpe: e.tensor_tensor(out=qpe[0:64, c:c + nn], in0=t1[0:64, 0:nn],
                                                                                in1=t2[0:64, 0:nn], op=ALU.add),
                                 reads=[t1.b, t2.b], writes=[qpe.b])
                    P.act.op(lambda e, h=h, qn=qn: e.copy(out=qs_n[:, h, :], in_=qn[:, NP:NT]), reads=[qn.b], writes=[qs_n.b])
                    P.act.op(lambda e, h=h, qpe=qpe: e.copy(out=qs_pe[0:64, h, :], in_=qpe[0:64, NP:NT]),
                             reads=[qpe.b], writes=[qs_pe.b])
                    pend = None
                    for si in range(4):
                        own = si == 3
                        r = si
                        kT, vS = segb[(h * 4 + si) % 2]
                        if own:
                            latm = lambda m, a, b: lat[:, m, a:b]
                            kpes = lambda a, b: lat[0:64, 4, a:b]
                            src_b = lat.b
                        else:
                            latm = lambda m, a, b, r=r: gat[:, r, m, a:b]
                            kpes = lambda a, b, r=r: gat[0:64, r, 4, a:b]
                            src_b = gat.b
                        self.kv_from_lat(wkv, latm, src_b, kT, vS)
                        for qt in range(2):
                            for kt in range(8):
                                c_lo = 0
                                partial = False
                                if own:
                                    if kt >= 4 * (qt + 1):
                                        continue
                                    bq = kt - 4 * qt
                                    if bq >= 0:
                                        c_lo = 128 * bq
                                        partial = True
                                ps_s = self.bank()
                                q0 = qt * 512 + c_lo
                                q1 = qt * 512 + 512
                                P.pe.op(lambda e, ps_s=ps_s, c_lo=c_lo, kt=kt, q0=q0, q1=q1, kT=kT, qn=qn: e.matmul(
                                    ps_s[:, c_lo:512], lhsT=kT[:, kt * 128:(kt + 1) * 128], rhs=qn[:, q0:q1],
                                    start=True, stop=False), reads=[kT.b, qn.b], writes=[ps_s.b])
                                P.pe.op(lambda e, ps_s=ps_s, c_lo=c_lo, kt=kt, q0=q0, q1=q1, kpes=kpes, qpe=qpe: e.matmul(
                                    ps_s[:, c_lo:512], lhsT=kpes(kt * 128, (kt + 1) * 128), rhs=qpe[0:64, q0:q1],
                                    start=False, stop=True), reads=[src_b, qpe.b], writes=[ps_s.b])
                                pT = pTs[pcount % 3]
                                pcount += 1
                                bias = 0.0 if own else self.cs("abias", r, r + 1)
                                P.act.op(lambda e, ps_s=ps_s, c_lo=c_lo, pT=pT, bias=bias: e.activation(
                                    out=pT[:, c_lo:512], in_=ps_s[:, c_lo:512], func=AF.Exp, bias=bias, scale=MLA_SCALE),
                                    reads=[ps_s.b, self.cst_t.b], writes=[pT.b])
                                if partial:
                                    P.pool.op(lambda e, pT=pT, c_lo=c_lo: e.memset(pT[64:128, c_lo:c_lo + 64], 0.0),
                                              writes=[pT.b])
                                first = (si == 0 and kt == 0)
                                last = (own and kt == 4 * qt + 3)
                                ob = obank[qt]
                                ac = acc[qt]
                                use_pool = False

                                def pv_step(ob=ob, c_lo=c_lo, vS=vS, kt=kt, pT=pT, first=first, last=last, ac=ac, use_pool=use_pool):
                                    P.pe.op(lambda e: e.matmul(ob[:, c_lo:512], lhsT=vS[:, kt, :], rhs=pT[:, c_lo:512], start=first, stop=last),
                                            reads=[vS.b, pT.b], writes=[ob.b])
                                    if first:
                                        P.dve.op(lambda e: e.tensor_copy(out=ac[:], in_=pT[:]), reads=[pT.b], writes=[ac.b])
                                    else:
                                        eng = P.pool if use_pool else P.dve
                                        eng.op(lambda e: e.tensor_tensor(out=ac[:, c_lo:512], in0=ac[:, c_lo:512], in1=pT[:, c_lo:512], op=ALU.add),
                                               reads=[pT.b, ac.b], writes=[ac.b])
                                if pend is not None:
                                    pend()
                                pend = pv_step
                    if pend is not None:
                        pend()
                        pend = None
                    for qt in range(2):
                        ps = self.bank()
                        P.pe.op(lambda e, ps=ps, qt=qt: e.matmul(ps[:], lhsT=self.ones[:], rhs=acc[qt][:], start=True, stop=True),
                                reads=[self.ones.b, acc[qt].b], writes=[ps.b])
                        P.dve.op(lambda e, ps=ps: e.reciprocal(out=rinv[:], in_=ps[:]), reads=[ps.b], writes=[rinv.b])
                        P.dve.op(lambda e, qt=qt: e.tensor_tensor(out=otmp[:], in0=obank[qt][:], in1=rinv[:], op=ALU.mult),
                                 reads=[obank[qt].b, rinv.b], writes=[otmp.b])
                        P.dve.op(lambda e, qt=qt, h=h: e.tensor_tensor(out=sgT[:, h, qt * 512:(qt + 1) * 512], in0=otmp[:],
                                                                       in1=sgT[:, h, qt * 512:(qt + 1) * 512], op=ALU.mult),
                                 reads=[otmp.b, sgT.b], writes=[sgT.b])
                P.barrier()
            with ExitStack() as stS:
                cts = [self.sb(stS, "a%d_ct%d" % (L, i), [128, 576]) for i in range(2)]
                ctbs = [self.sb(stS, "a%d_ctb%d" % (L, i), [128, 512], BF16) for i in range(2)]
                lTs = [self.sb(stS, "a%d_lT%d" % (L, i), [128, 5, 128], BF16) for i in range(2)]
                wuks = [self.sb(stS, "a%d_wuk%d" % (L, i), [128, 4, 128]) for i in range(2)]
                wukT = self.sb(stS, "a%d_wukT" % L, [128, 512], BF16)
                qlT = self.sb(stS, "a%d_qlT" % L, [128, 4, 256], BF16)
                olT = self.sb(stS, "a%d_olT" % L, [128, 4, 256], BF16)
                accs = self.sb(stS, "a%d_accs" % L, [128, 256])
                ckv_c = io["c%d_ckv" % L]
                kpe_c = io["c%d_kpe" % L]
                oacc = [self.banks[6], self.banks[7]]
                P.pool.op(lambda e: e.memset(accs[:], 0.0), writes=[accs.b])
                for i in range(2):
                    P.pool.op(lambda e, i=i: e.memset(lTs[i][:, 4, :], 0.0), writes=[lTs[i].b])
                for h in range(16):
                    wuk = wuks[h % 2]
                    P.sp.op(lambda e, h=h, wuk=wuk: e.dma_start(
                        out=wuk[:], in_=w_kvb[:, h * 256:h * 256 + 128].rearrange("(k p) n -> p k n", p=128)),
                        writes=[wuk.b], dma=True)
                    pt = self.bank()
                    for m in range(4):
                        P.pe.op(lambda e, m=m, pt=pt, wuk=wuk: e.transpose(pt[:, m * 128:(m + 1) * 128], wuk[:, m, :], self.ident[:]),
                                reads=[wuk.b, self.ident.b], writes=[pt.b])
                    P.act.op(lambda e, pt=pt: e.copy(out=wukT[:], in_=pt[:]), reads=[pt.b], writes=[wukT.b])
                    pq = self.bank()
                    for m in range(4):
                        P.pe.op(lambda e, m=m, pq=pq, h=h: e.matmul(pq[:, m * NS:(m + 1) * NS], lhsT=wukT[:, m * 128:(m + 1) * 128],
                                                                    rhs=qs_n[:, h, :], start=True, stop=True),
                                reads=[wukT.b, qs_n.b], writes=[pq.b])
                    P.dve.op(lambda e, pq=pq, h=h: e.tensor_copy(out=qlT[:, :, h * NS:(h + 1) * NS],
                                                                 in_=pq[:, 0:4 * NS].rearrange("p (a b) -> p a b", a=4)),
                             reads=[pq.b], writes=[qlT.b])
                qpe_flat = qs_pe[0:64, :, :].rearrange("p a b -> p (a b)")
                pend = None
                for ti in range(33):
                    own = ti == 32
                    rows = NS if own else 128
                    if not own:
                        ct = cts[ti % 2]
                        ctb = ctbs[ti % 2]
                        lT = lTs[ti % 2]
                        r0 = ti * 128
                        P.sp.op(lambda e, ct=ct, r0=r0: e.dma_start(out=ct[:, 0:512], in_=ckv_c[r0:r0 + 128, :]), writes=[ct.b], dma=True)
                        P.sp.op(lambda e, ct=ct, r0=r0: e.dma_start(out=ct[:, 512:576], in_=kpe_c[r0:r0 + 128, :]), writes=[ct.b], dma=True)
                        pt = self.bank()
                        for m in range(4):
                            P.pe.op(lambda e, m=m, pt=pt, ct=ct: e.transpose(pt[:, m * 128:(m + 1) * 128], ct[:, m * 128:(m + 1) * 128], self.ident[:]),
                                    reads=[ct.b, self.ident.b], writes=[pt.b])
                        P.act.op(lambda e, pt=pt, lT=lT: e.copy(out=lT[:, 0:4, :], in_=pt[:].rearrange("p (a b) -> p a b", a=4)),
                                 reads=[pt.b], writes=[lT.b])
                        pt2 = self.bank()
                        P.pe.op(lambda e, pt2=pt2, ct=ct: e.transpose(pt2[0:64, 0:128], ct[:, 512:576], self.ident[:]),
                                reads=[ct.b, self.ident.b], writes=[pt2.b])
                        P.dve.op(lambda e, pt2=pt2, lT=lT: e.tensor_copy(out=lT[0:64, 4, :], in_=pt2[0:64, 0:128]), reads=[pt2.b], writes=[lT.b])
                        P.pool.op(lambda e, ct=ct, ctb=ctb: e.tensor_copy(out=ctb[:], in_=ct[:, 0:512]), reads=[ct.b], writes=[ctb.b])
                        latm = lambda m, lT=lT: lT[:, m, 0:128]
                        kpem = lambda lT=lT: lT[0:64, 4, 0:128]
                        src_b = lT.b
                        tokm = lambda m, ctb=ctb: ctb[:, m * 128:(m + 1) * 128]
                        tok_b = ctb.b
                    else:
                        latm = lambda m: lat[:, m, NP:NT]
                        kpem = lambda: lat[0:64, 4, NP:NT]
                        src_b = lat.b
                        ctb = ctbs[0]
                        P.pool.op(lambda e, ctb=ctb: e.tensor_copy(out=ctb[0:NS, :], in_=self.ckvn_s[0:NS, :]),
                                  reads=[self.ckvn_s.b], writes=[ctb.b])
                        tokm = lambda m, ctb=ctb: ctb[0:NS, m * 128:(m + 1) * 128]
                        tok_b = ctb.b
                    ps_s = self.bank()
                    for m in range(4):
                        P.pe.op(lambda e, m=m, ps_s=ps_s, latm=latm, rows=rows: e.matmul(ps_s[0:rows, 0:256], lhsT=latm(m), rhs=qlT[:, m, :],
                                                                                     start=(m == 0), stop=False),
                                reads=[src_b, qlT.b], writes=[ps_s.b])
                    P.pe.op(lambda e, ps_s=ps_s, kpem=kpem, rows=rows: e.matmul(ps_s[0:rows, 0:256], lhsT=kpem(), rhs=qpe_flat, start=False, stop=True),
                            reads=[src_b, qs_pe.b], writes=[ps_s.b])
                    pT = pTs[pcount % 3]
                    pcount += 1
                    P.act.op(lambda e, ps_s=ps_s, pT=pT, rows=rows: e.activation(out=pT[0:rows, 0:256], in_=ps_s[0:rows, 0:256], func=AF.Exp,
                                                                                 scale=MLA_SCALE), reads=[ps_s.b], writes=[pT.b])
                    first = ti == 0
                    last = own

                    def pv_s(tokm=tokm, tok_b=tok_b, pT=pT, rows=rows, first=first, last=last):
                        for m in range(4):
                            ob = oacc[m // 2]
                            o0 = (m % 2) * 256
                            P.pe.op(lambda e, m=m, ob=ob, o0=o0: e.matmul(ob[:, o0:o0 + 256], lhsT=tokm(m), rhs=pT[0:rows, 0:256],
                                                                          start=first, stop=last), reads=[tok_b, pT.b], writes=[ob.b])
                        P.dve.op(lambda e: e.tensor_tensor(out=accs[0:rows, :], in0=accs[0:rows, :], in1=pT[0:rows, 0:256], op=ALU.add),
                                 reads=[pT.b, accs.b], writes=[accs.b])
                    if pend is not None:
                        pend()
                    pend = pv_s
                pend()
                for m in range(4):
                    ob = oacc[m // 2]
                    o0 = (m % 2) * 256
                    if m % 2 == 0:
                        P.act.op(lambda e, m=m, ob=ob, o0=o0: e.copy(out=olT[:, m, :], in_=ob[:, o0:o0 + 256]), reads=[ob.b], writes=[olT.b])
                    else:
                        P.act.op(lambda e, m=m, ob=ob, o0=o0: e.copy(out=olT[:, m, :], in_=ob[:, o0:o0 + 256]), reads=[ob.b], writes=[olT.b])
                osb = self.banks[6]
                for h in range(16):
                    wkv = hb[h % 2]["wkv"]
                    P.pool.op(lambda e, h=h, wkv=wkv: e.dma_start(
                        out=wkv[:], in_=w_kvb[:, h * 256:(h + 1) * 256].rearrange("(k p) n -> p k n", p=128)),
                        writes=[wkv.b], dma=True)
                    for m in range(4):
                        P.pe.op(lambda e, m=m, h=h, wkv=wkv: e.matmul(osb[:, h * NS:(h + 1) * NS], lhsT=wkv[:, m, 128:256],
                                                                      rhs=olT[:, m, h * NS:(h + 1) * NS], start=(m == 0), stop=(m == 3)),
                                reads=[wkv.b, olT.b], writes=[osb.b])
                ps = self.bank()
                P.pe.op(lambda e, ps=ps: e.matmul(ps[:, 0:256], lhsT=self.ones[:], rhs=accs[:], start=True, stop=True),
                        reads=[self.ones.b, accs.b], writes=[ps.b])
                P.dve.op(lambda e, ps=ps: e.reciprocal(out=rinv[:, 0:256], in_=ps[:, 0:256]), reads=[ps.b], writes=[rinv.b])
                P.dve.op(lambda e: e.tensor_tensor(out=otmp[:, 0:256], in0=osb[:, 0:256], in1=rinv[:, 0:256], op=ALU.mult),
                         reads=[osb.b, rinv.b], writes=[otmp.b])
                P.dve.op(lambda e: e.tensor_tensor(out=sgT[:, :, NP:NT], in0=otmp[:, 0:256].rearrange("p (a b) -> p a b", a=16),
                                                   in1=sgT[:, :, NP:NT], op=ALU.mult),
                         reads=[otmp.b, sgT.b], writes=[sgT.b])
                P.barrier()


    def transpose_out(self, srcF, src_b, ncols, c0, dst_ap, stT, nchunks=16, width=D):
        P = self.P
        for g in range(nchunks // 4):
            pb = self.bank()
            for q in range(4):
                ch = g * 4 + q
                P.pe.op(lambda e, pb=pb, q=q, ch=ch: e.transpose(pb[0:ncols, q * 128:(q + 1) * 128],
                                                               srcF[:, ch, c0:c0 + ncols], self.ident[:]),
                        reads=[src_b, self.ident.b], writes=[pb.b])
            P.act.op(lambda e, pb=pb, g=g: e.copy(out=stT[0:ncols, g * 512:(g + 1) * 512], in_=pb[0:ncols, :]),
                     reads=[pb.b], writes=[stT.b])
        P.sp.op(lambda e: e.dma_start(out=dst_ap, in_=stT[0:ncols, 0:width]), reads=[stT.b], dma=True)

    def conv_layer(self):
        P, io, nc = self.P, self.io, self.nc
        w_in = io["l1_w_in"]
        gin = nc.dram_tensor("gin1", [128, 512], F32)
        gout = nc.dram_tensor("gout1", [4 * 128, 512], F32)
        b_gin, b_gout = Buf(), Buf()
        with ExitStack() as stL:
            halo = self.sb(stL, "c1_halo", [128, 16, 32])
            uh = self.sb(stL, "c1_uh", [128, 16, 32])
            usall = self.sb(stL, "c1_usall", [128, 16, NS])
            stF = self.sb(stL, "c1_stF", [128, 16, 32])
            stT = self.sb(stL, "c1_stT", [32, D])
            wbs = [self.sb(stL, "c1_wb%d" % i, [128, KC, 256], BF16) for i in range(3)]
            hT = self.sb(stL, "c1_hT", [128, KC, 560], BF16)
            sg = self.sb(stL, "c1_sg", [128, KC, 528], BF16)
            cvT = self.sb(stL, "c1_cvT", [128, KC, 528])
            ups = [self.sb(stL, "c1_up%d" % i, [128, 544]) for i in range(2)]
            uss = [self.sb(stL, "c1_us%d" % i, [128, 48]) for i in range(2)]
            tv = self.sb(stL, "c1_tv", [128, 512])
            tg = self.sb(stL, "c1_tg", [128, 512])
            sq = self.sb(stL, "c1_sq", [128, 512])
            rstd = self.sb(stL, "c1_rstd", [128, 512])
            mu = self.sb(stL, "c1_mu", [128, 512])
            var = self.sb(stL, "c1_var", [128, 512])
            gat = self.sb(stL, "c1_gat", [128, 4, 512])
            P.sp.op(lambda e: e.dma_start(out=stT[0:30, :], in_=io["st1"]), writes=[stT.b], dma=True)
            P.pool.op(lambda e: e.memset(stF[:], 0.0), writes=[stF.b])
            for g in range(4):
                pb = self.bank()
                for q in range(4):
                    ch = g * 4 + q
                    P.pe.op(lambda e, pb=pb, q=q, ch=ch: e.transpose(pb[:, q * 32:q * 32 + 30], stT[0:30, ch * 128:(ch + 1) * 128],
                                                                   self.ident[0:30, 0:30]),
                            reads=[stT.b, self.ident.b], writes=[pb.b])
                P.act.op(lambda e, pb=pb, g=g: e.copy(out=stF[:, g * 4:(g + 1) * 4, 2:32],
                                                      in_=pb[:, 0:128].rearrange("p (a b) -> p a b", a=4)[:, :, 0:30]),
                         reads=[pb.b], writes=[stF.b])
            P.sp.op(lambda e: e.dma_start(out=io["conv1s"][0:14, :], in_=io["st1"][16:30, :]), dma=True)
            wi = 0
            for tile_i in range(2):
                if tile_i == 0:
                    x0, nloc = 480, 560
                    splits = [(0, 32), (32, 512), (544, 16)]
                    out0 = 512
                else:
                    x0, nloc = 0, 512
                    splits = [(0, 512)]
                    out0 = 0
                nout = 528 if tile_i == 0 else 512
                for (c, nn) in self.subtiles(x0, nloc):
                    self.rms_F(hT, c, nn, "norm1", sq, rstd, off=c - x0)
                gsplits = [(32, 512), (544, 16)] if tile_i == 0 else [(0, 512)]
                for gb in range(8):
                    wb = wbs[wi % 3]; wi += 1
                    self.load_wblock(wb, w_in, 4096 + gb * 256, 256)

                    def cons_g(ps, m, c, nn, gb=gb, tile_i=tile_i):
                        oc = c - 32 if tile_i == 0 else c
                        P.act.op(lambda e: e.activation(out=sg[:, gb * 2 + m, oc:oc + nn], in_=ps[:, 0:nn], func=AF.Silu),
                                 reads=[ps.b], writes=[sg.b])
                    self.proj_F(wb, 0, 2, hT, hT.b, 0, nloc, cons_g, splits=gsplits)
                for g8 in range(8):
                    wv = wbs[wi % 3]; wi += 1
                    wg = wbs[wi % 3]; wi += 1
                    self.load_wblock(wv, w_in, g8 * 256, 256)
                    self.load_wblock(wg, w_in, 2048 + g8 * 256, 256)
                    for m in range(2):
                        ch = g8 * 2 + m
                        up = ups[ch % 2]
                        us = uss[ch % 2]
                        if tile_i == 1:
                            P.act.op(lambda e, up=up, ch=ch: e.copy(out=up[:, 0:32], in_=halo[:, ch, :]),
                                     reads=[halo.b], writes=[up.b])
                        else:
                            P.act.op(lambda e, us=us, ch=ch: e.copy(out=us[:, 0:32], in_=stF[:, ch, :]),
                                     reads=[stF.b], writes=[us.b])
                        for (c, nn) in splits:
                            psv = self.bank()
                            psg = self.bank()
                            for k in range(KC):
                                P.pe.op(lambda e, k=k, m=m, c=c, nn=nn, psv=psv, wv=wv: e.matmul(
                                    psv[:, 0:nn], lhsT=wv[:, k, m * 128:(m + 1) * 128], rhs=hT[:, k, c:c + nn],
                                    start=(k == 0), stop=(k == KC - 1)), reads=[wv.b, hT.b], writes=[psv.b])
                            for k in range(KC):
                                P.pe.op(lambda e, k=k, m=m, c=c, nn=nn, psg=psg, wg=wg: e.matmul(
                                    psg[:, 0:nn], lhsT=wg[:, k, m * 128:(m + 1) * 128], rhs=hT[:, k, c:c + nn],
                                    start=(k == 0), stop=(k == KC - 1)), reads=[wg.b, hT.b], writes=[psg.b])
                            P.act.op(lambda e, nn=nn, psg=psg: e.activation(out=tg[:, 0:nn], in_=psg[:, 0:nn], func=AF.Sigmoid),
                                     reads=[psg.b], writes=[tg.b])
                            if tile_i == 0 and c == 544:
                                dst, dstb = us[:, 32:48], us.b
                            elif tile_i == 0:
                                dst, dstb = up[:, c:c + nn], up.b
                            else:
                                dst, dstb = up[:, 32 + c:32 + c + nn], up.b
                            P.dve.op(lambda e, nn=nn, psv=psv, dst=dst: e.tensor_tensor(out=dst, in0=psv[:, 0:nn], in1=tg[:, 0:nn],
                                                                                        op=ALU.mult),
                                     reads=[psv.b, tg.b], writes=[dstb])
                        if tile_i == 0:
                            P.act.op(lambda e, up=up, ch=ch: e.copy(out=uh[:, ch, :], in_=up[:, 512:544]),
                                     reads=[up.b], writes=[uh.b])
                            P.act.op(lambda e, us=us, ch=ch: e.copy(out=usall[:, ch, :], in_=us[:, 32:48]),
                                     reads=[us.b], writes=[usall.b])
                        jobs = [(up, 512, 0)]
                        if tile_i == 0:
                            jobs.append((us, NS, 512))
                        for (buf, n, oc) in jobs:
                            dstc = cvT[:, ch, oc:oc + n]
                            P.dve.op(lambda e, buf=buf, n=n, dstc=dstc, ch=ch: e.tensor_scalar(
                                out=dstc, in0=buf[:, 2:2 + n], scalar1=self.cs("dww", ch * 31, ch * 31 + 1),
                                scalar2=self.cs("dwb", ch, ch + 1), op0=ALU.mult, op1=ALU.add),
                                reads=[buf.b, self.cst_t.b], writes=[cvT.b])
                            for k in range(1, 31):
                                P.dve.op(lambda e, buf=buf, n=n, dstc=dstc, ch=ch, k=k: e.scalar_tensor_tensor(
                                    out=dstc, in0=buf[:, 2 + k:2 + k + n], scalar=self.cs("dww", ch * 31 + k, ch * 31 + k + 1),
                                    in1=dstc, op0=ALU.mult, op1=ALU.add),
                                    reads=[buf.b, self.cst_t.b, cvT.b], writes=[cvT.b])
                if tile_i == 0:
                    P.sp.op(lambda e: e.dma_start(out=gin[:], in_=uh[:].rearrange("p a b -> p (a b)")),
                            reads=[uh.b], writes=[b_gin], dma=True)
                    P.pool.op(lambda e: e.collective_compute("AllGather", ALU.bypass, replica_groups=GROUPS,
                                                             ins=[gin[:]], outs=[gout[:]]),
                              reads=[b_gin], writes=[b_gout], cc=True)
                    P.sp.op(lambda e: e.dma_start(out=gat[:], in_=gout[:].rearrange("(r p) n -> p r n", p=128)),
                            reads=[b_gout], writes=[gat.b], dma=True)
                    hflat = halo[:].rearrange("p a b -> p (a b)")
                    P.dve.op(lambda e: e.tensor_scalar(out=hflat, in0=gat[:, 0, :], scalar1=self.cs("selp", 0, 1), scalar2=None,
                                                       op0=ALU.mult), reads=[gat.b, self.cst_t.b], writes=[halo.b])
                    for r in range(1, 4):
                        P.dve.op(lambda e, r=r: e.scalar_tensor_tensor(out=hflat, in0=gat[:, r, :], scalar=self.cs("selp", r, r + 1),
                                                                      in1=hflat, op0=ALU.mult, op1=ALU.add),
                                 reads=[gat.b, self.cst_t.b, halo.b], writes=[halo.b])
                    self.transpose_out(uh, uh.b, 30, 2, io["conv1p"], stT)
                    self.transpose_out(usall, usall.b, NS, 0, io["conv1s"][14:30, :], stT)
                lsplits = [(0, 512), (512, 16)] if tile_i == 0 else [(0, 512)]
                for (c, nn) in lsplits:
                    pbS = self.bank()
                    pbQ = self.bank()
                    for ch in range(KC):
                        P.pe.op(lambda e, ch=ch, c=c, nn=nn, pbS=pbS: e.matmul(pbS[:, 0:nn], lhsT=self.ones[:], rhs=cvT[:, ch, c:c + nn],
                                                                              start=(ch == 0), stop=(ch == KC - 1)),
                                reads=[self.ones.b, cvT.b], writes=[pbS.b])
                    for ch in range(KC):
                        P.act.op(lambda e, ch=ch, c=c, nn=nn: e.activation(out=sq[:, 0:nn], in_=cvT[:, ch, c:c + nn], func=AF.Square),
                                 reads=[cvT.b], writes=[sq.b])
                        P.pe.op(lambda e, ch=ch, nn=nn, pbQ=pbQ: e.matmul(pbQ[:, 0:nn], lhsT=self.ones[:], rhs=sq[:, 0:nn],
                                                                         start=(ch == 0), stop=(ch == KC - 1)),
                                reads=[self.ones.b, sq.b], writes=[pbQ.b])
                    P.act.op(lambda e, nn=nn, pbS=pbS: e.mul(out=mu[:, 0:nn], in_=pbS[:, 0:nn], mul=1.0 / D), reads=[pbS.b], writes=[mu.b])
                    P.dve.op(lambda e, nn=nn: e.tensor_tensor(out=var[:, 0:nn], in0=mu[:, 0:nn], in1=mu[:, 0:nn], op=ALU.mult),
                             reads=[mu.b], writes=[var.b])
                    P.dve.op(lambda e, nn=nn, pbQ=pbQ: e.scalar_tensor_tensor(out=var[:, 0:nn], in0=pbQ[:, 0:nn], scalar=1.0 / D,
                                                                             in1=var[:, 0:nn], op0=ALU.mult, op1=ALU.subtract),
                             reads=[pbQ.b, var.b], writes=[var.b])
                    P.dve.op(lambda e, nn=nn: e.tensor_scalar(out=var[:, 0:nn], in0=var[:, 0:nn], scalar1=EPS, scalar2=None, op0=ALU.add),
                             reads=[var.b], writes=[var.b])
                    P.act.op(lambda e, nn=nn: e.activation(out=var[:, 0:nn], in_=var[:, 0:nn], func=AF.Sqrt), reads=[var.b], writes=[var.b])
                    P.dve.op(lambda e, nn=nn: e.reciprocal(out=rstd[:, 0:nn], in_=var[:, 0:nn]), reads=[var.b], writes=[rstd.b])
                    for ch in range(KC):
                        P.dve.op(lambda e, ch=ch, c=c, nn=nn: e.tensor_tensor(out=tv[:, 0:nn], in0=cvT[:, ch, c:c + nn], in1=mu[:, 0:nn],
                                                                              op=ALU.subtract), reads=[cvT.b, mu.b], writes=[tv.b])
                        P.dve.op(lambda e, nn=nn: e.tensor_tensor(out=tv[:, 0:nn], in0=tv[:, 0:nn], in1=rstd[:, 0:nn], op=ALU.mult),
                                 reads=[tv.b, rstd.b], writes=[tv.b])
                        P.act.op(lambda e, ch=ch, nn=nn: e.activation(out=tg[:, 0:nn], in_=tv[:, 0:nn], func=AF.Silu,
                                                                      bias=self.cs("lnb", ch, ch + 1), scale=self.cs("lng", ch, ch + 1)),
                                 reads=[tv.b, self.cst_t.b], writes=[tg.b])
                        P.dve.op(lambda e, ch=ch, c=c, nn=nn: e.tensor_tensor(out=sg[:, ch, c:c + nn], in0=tg[:, 0:nn],
                                                                              in1=sg[:, ch, c:c + nn], op=ALU.mult),
                                 reads=[tg.b, sg.b], writes=[sg.b])
                for blk in range(8):
                    wb = wbs[wi % 3]; wi += 1
                    self.load_wblock(wb, io["l1_w_out"], blk * 256, 256)

                    def cons_o(ps, m, c, nn, blk=blk, out0=out0):
                        P.dve.op(lambda e: e.tensor_tensor(out=self.xT[:, blk * 2 + m, out0 + c:out0 + c + nn],
                                                           in0=ps[:, 0:nn], in1=self.xT[:, blk * 2 + m, out0 + c:out0 + c + nn],
                                                           op=ALU.add),
                                 reads=[ps.b, self.xT.b], writes=[self.xT.b])
                    self.proj_F(wb, 0, 2, sg, sg.b, 0, nout, cons_o, splits=lsplits)
            P.barrier()


    def vap(self, ap, dims):
        return bass.AP(ap.tensor, ap.offset, [list(ap.ap[0])] + [list(d) for d in dims])

    def exchange(self, name, src_ap, src_b, ncols, dst, dt_=F32):
        P, nc = self.P, self.nc
        gin = nc.dram_tensor("gin_" + name, [128, ncols], dt_)
        gout = nc.dram_tensor("gout_" + name, [4 * 128, ncols], dt_)
        b1, b2 = Buf(), Buf()
        P.sp.op(lambda e: e.dma_start(out=gin[:], in_=src_ap), reads=[src_b], writes=[b1], dma=True)
        P.pool.op(lambda e: e.collective_compute("AllGather", ALU.bypass, replica_groups=GROUPS, ins=[gin[:]], outs=[gout[:]]),
                  reads=[b1], writes=[b2], cc=True)
        P.sp.op(lambda e: e.dma_start(out=dst[:, :, 0:ncols], in_=gout[:].rearrange("(r p) n -> p r n", p=128)),
                reads=[b2], writes=[dst.b], dma=True)

    def ssd_layer(self):
        P, io, nc = self.P, self.io, self.nc
        w_in = io["l2_w_in"]
        NH = NT + 4
        d_zs = nc.dram_tensor("d_zs", [NT, 4096], BF16)
        d_xs = nc.dram_tensor("d_xs", [NT, 4096], BF16)
        d_bs = nc.dram_tensor("d_bs", [NT, 1024], BF16)
        d_bf = nc.dram_tensor("d_bf", [128, 8 * NT], BF16)
        d_cf = nc.dram_tensor("d_cf", [128, 8 * NT], BF16)
        d_yn = nc.dram_tensor("d_yn", [128, 32 * NT], BF16)
        b_zs, b_xs, b_bs, b_bf, b_cf, b_yn = [Buf() for _ in range(6)]
        with ExitStack() as stL:
            U = self.sb(stL, "s2_U", [128, 128])
            SU = self.sb(stL, "s2_SU", [128, 128])
            aneg = self.sb(stL, "s2_aneg", [128, 64])
            dtT = self.sb(stL, "s2_dtT", [128, 9, 64])
            P.pool.op(lambda e: e.memset(U[:], 1.0), writes=[U.b])
            P.pool.op(lambda e: e.affine_select(out=U[:], in_=U[:], pattern=[[1, 128]], compare_op=ALU.is_ge, fill=0.0,
                                                base=0, channel_multiplier=-1), reads=[U.b], writes=[U.b])
            P.pool.op(lambda e: e.memset(SU[:], 1.0), writes=[SU.b])
            P.pool.op(lambda e: e.affine_select(out=SU[:], in_=SU[:], pattern=[[-1, 128]], compare_op=ALU.is_gt, fill=0.0,
                                                base=0, channel_multiplier=1), reads=[SU.b], writes=[SU.b])
            P.act.op(lambda e: e.activation(out=aneg[:], in_=self.cs("alog"), func=AF.Exp), reads=[self.cst_t.b], writes=[aneg.b])
            P.act.op(lambda e: e.mul(out=aneg[:], in_=aneg[:], mul=-1.0), reads=[aneg.b], writes=[aneg.b])
            with ExitStack() as st:
                xh = self.sb(st, "s2_xh", [128, KC, 4])
                gatx = self.sb(st, "s2_gatx", [128, 4, 64])
                hT = self.sb(st, "s2_hT", [128, KC, NH], BF16)
                sq = self.sb(st, "s2_sq", [128, 512])
                rstd = self.sb(st, "s2_rstd", [128, 512])
                wbs = [self.sb(st, "s2_wb%d" % i, [128, KC, 256], BF16) for i in range(3)]
                cbuf = [self.sb(st, "s2_cbuf%d" % i, [128, 4 + NP]) for i in range(2)]
                cbs = [self.sb(st, "s2_cbs%d" % i, [128, 4 + NS]) for i in range(2)]
                cvo = [self.sb(st, "s2_cvo%d" % i, [128, NT]) for i in range(2)]
                cvb = self.sb(st, "s2_cvb", [128, NT], BF16)
                stg = [self.sb(st, "s2_stg%d" % i, [128, 9, 512], BF16) for i in range(2)]
                zst = [self.sb(st, "s2_zst%d" % i, [128, 256], BF16) for i in range(3)]
                pre_p = self.sb(st, "s2_prep", [128, 48, 4])
                pre_s = self.sb(st, "s2_pres", [128, 48, 4])
                stF = self.sb(st, "s2_stF", [128, 48, 4])
                stT = self.sb(st, "s2_stT", [4, 2048])
                dtt = self.sb(st, "s2_dtt", [128, 4, 64])
                ginx = nc.dram_tensor("gin_xh", [128, 64], F32)
                goutx = nc.dram_tensor("gout_xh", [4 * 128, 64], F32)
                bx1, bx2 = Buf(), Buf()
                P.sp.op(lambda e: e.dma_start(out=ginx[:].rearrange("p (a b) -> p a b", a=16), in_=self.xT[:, :, NP - 4:NP]),
                        reads=[self.xT.b], writes=[bx1], dma=True)
                P.pool.op(lambda e: e.collective_compute("AllGather", ALU.bypass, replica_groups=GROUPS, ins=[ginx[:]], outs=[goutx[:]]),
                          reads=[bx1], writes=[bx2], cc=True)
                P.sp.op(lambda e: e.dma_start(out=gatx[:], in_=goutx[:].rearrange("(r p) n -> p r n", p=128)),
                        reads=[bx2], writes=[gatx.b], dma=True)
                xhf = xh[:].rearrange("p a b -> p (a b)")
                P.dve.op(lambda e: e.tensor_scalar(out=xhf, in0=gatx[:, 0, :], scalar1=self.cs("selp", 0, 1), scalar2=None, op0=ALU.mult),
                         reads=[gatx.b, self.cst_t.b], writes=[xh.b])
                for r in range(1, 4):
                    P.dve.op(lambda e, r=r: e.scalar_tensor_tensor(out=xhf, in0=gatx[:, r, :], scalar=self.cs("selp", r, r + 1), in1=xhf,
                                                                  op0=ALU.mult, op1=ALU.add),
                             reads=[gatx.b, self.cst_t.b, xh.b], writes=[xh.b])
                P.pool.op(lambda e: e.memset(stF[:], 0.0), writes=[stF.b])
                for pc3 in range(3):
                    P.sp.op(lambda e, pc3=pc3: e.dma_start(out=stT[0:3, :], in_=io["st2c"][:, pc3 * 2048:(pc3 + 1) * 2048]),
                            writes=[stT.b], dma=True)
                    for g in range(4):
                        pb = self.bank()
                        for q in range(4):
                            chl = g * 4 + q
                            P.pe.op(lambda e, pb=pb, q=q, chl=chl: e.transpose(pb[:, q * 4:q * 4 + 3], stT[0:3, chl * 128:(chl + 1) * 128],
                                                                             self.ident[0:3, 0:3]),
                                    reads=[stT.b, self.ident.b], writes=[pb.b])
                        gg = pc3 * 4 + g
                        P.act.op(lambda e, pb=pb, gg=gg: e.copy(out=stF[:, gg * 4:(gg + 1) * 4, 1:4],
                                                                in_=pb[:, 0:16].rearrange("p (a b) -> p a b", a=4)[:, :, 0:3]),
                                 reads=[pb.b], writes=[stF.b])
                self.rms_F(hT, 0, 4, "norm2", sq, rstd, off=0, src=xh)
                for (c, nn) in self.subtiles(0, NT):
                    self.rms_F(hT, c, nn, "norm2", sq, rstd, off=4 + c)
                tchunks = [(4 + i * 128, 128, i) for i in range(8)] + [(4 + NP, NS, 8)]
                wi = 0
                zc = 0
                for zb in range(16):
                    wb = wbs[wi % 3]; wi += 1
                    self.load_wblock(wb, w_in, zb * 256, 256)
                    for (c, rows, ti) in tchunks:
                        ps = self.bank()
                        for k in range(KC):
                            P.pe.op(lambda e, k=k, c=c, rows=rows, ps=ps, wb=wb: e.matmul(
                                ps[0:rows, 0:256], lhsT=hT[:, k, c:c + rows], rhs=wb[:, k, 0:256], start=(k == 0), stop=(k == KC - 1)),
                                reads=[hT.b, wb.b], writes=[ps.b])
                        zt = zst[zc % 3]; zc += 1
                        P.act.op(lambda e, rows=rows, ps=ps, zt=zt: e.activation(out=zt[0:rows, :], in_=ps[0:rows, 0:256], func=AF.Silu),
                                 reads=[ps.b], writes=[zt.b])
                        r0 = c - 4
                        P.sp.op(lambda e, rows=rows, r0=r0, zb=zb, zt=zt: e.dma_start(out=d_zs[r0:r0 + rows, zb * 256:(zb + 1) * 256],
                                                                                 in_=zt[0:rows, :]),
                                reads=[zt.b], writes=[b_zs], dma=True)
                wb = wbs[wi % 3]; wi += 1
                self.load_wblock(wb, w_in, 10240, 64)
                for (c, rows, ti) in tchunks:
                    ps = self.bank()
                    for k in range(KC):
                        P.pe.op(lambda e, k=k, c=c, rows=rows, ps=ps, wb=wb: e.matmul(
                            ps[0:rows, 0:64], lhsT=hT[:, k, c:c + rows], rhs=wb[:, k, 0:64], start=(k == 0), stop=(k == KC - 1)),
                            reads=[hT.b, wb.b], writes=[ps.b])
                    P.dve.op(lambda e, rows=rows, ps=ps: e.tensor_tensor(out=dtt[0:rows, 0, :], in0=ps[0:rows, 0:64],
                                                                         in1=self.cs("dtb")[0:rows, :], op=ALU.add),
                             reads=[ps.b, self.cst_t.b], writes=[dtt.b])
                    P.dve.op(lambda e, rows=rows: e.tensor_scalar(out=dtt[0:rows, 2, :], in0=dtt[0:rows, 0, :], scalar1=-1.0, scalar2=None,
                                                                  op0=ALU.mult), reads=[dtt.b], writes=[dtt.b])
                    P.dve.op(lambda e, rows=rows: e.tensor_tensor(out=dtt[0:rows, 1, :], in0=dtt[0:rows, 0, :], in1=dtt[0:rows, 2, :],
                                                                  op=ALU.max), reads=[dtt.b], writes=[dtt.b])
                    P.act.op(lambda e, rows=rows: e.activation(out=dtt[0:rows, 2, :], in_=dtt[0:rows, 1, :], func=AF.Exp, scale=-1.0),
                             reads=[dtt.b], writes=[dtt.b])
                    P.act.op(lambda e, rows=rows: e.activation(out=dtt[0:rows, 3, :], in_=dtt[0:rows, 2, :], func=AF.Ln, bias=1.0),
                             reads=[dtt.b], writes=[dtt.b])
                    P.dve.op(lambda e, rows=rows, ti=ti: e.scalar_tensor_tensor(out=dtT[0:rows, ti, :], in0=dtt[0:rows, 0, :], scalar=0.0,
                                                                               in1=dtt[0:rows, 3, :], op0=ALU.max, op1=ALU.add),
                             reads=[dtt.b], writes=[dtT.b])
                csplits = [(0, 4), (4, 512), (516, 512), (1028, NS)]
                for g24 in range(24):
                    wb = wbs[wi % 3]; wi += 1
                    self.load_wblock(wb, w_in, 4096 + g24 * 256, 256)
                    for m in range(2):
                        ch = g24 * 2 + m
                        cb_ = cbuf[ch % 2]
                        cs_ = cbs[ch % 2]
                        co = cvo[ch % 2]
                        P.act.op(lambda e, cs_=cs_, ch=ch: e.copy(out=cs_[:, 0:4], in_=stF[:, ch, :]), reads=[stF.b], writes=[cs_.b])
                        for (c, nn) in csplits:
                            ps = self.bank()
                            for k in range(KC):
                                P.pe.op(lambda e, k=k, m=m, c=c, nn=nn, ps=ps, wb=wb: e.matmul(
                                    ps[:, 0:nn], lhsT=wb[:, k, m * 128:(m + 1) * 128], rhs=hT[:, k, c:c + nn],
                                    start=(k == 0), stop=(k == KC - 1)), reads=[wb.b, hT.b], writes=[ps.b])
                            if c == 1028:
                                P.act.op(lambda e, ps=ps, cs_=cs_: e.copy(out=cs_[:, 4:4 + NS], in_=ps[:, 0:NS]), reads=[ps.b], writes=[cs_.b])
                            else:
                                P.act.op(lambda e, ps=ps, cb_=cb_, c=c, nn=nn: e.copy(out=cb_[:, c:c + nn], in_=ps[:, 0:nn]),
                                         reads=[ps.b], writes=[cb_.b])
                        P.pool.op(lambda e, cb_=cb_, ch=ch: e.tensor_copy(out=pre_p[:, ch, :], in_=cb_[:, NP:NP + 4]),
                                  reads=[cb_.b], writes=[pre_p.b])
                        P.pool.op(lambda e, cs_=cs_, ch=ch: e.tensor_copy(out=pre_s[:, ch, :], in_=cs_[:, NS:NS + 4]),
                                  reads=[cs_.b], writes=[pre_s.b])
                        for (buf, n, oc) in ((cb_, NP, 0), (cs_, NS, NP)):
                            dstc = co[:, oc:oc + n]
                            P.dve.op(lambda e, buf=buf, n=n, dstc=dstc, ch=ch: e.tensor_scalar(
                                out=dstc, in0=buf[:, 1:1 + n], scalar1=self.cs("cw2", ch * 4, ch * 4 + 1),
                                scalar2=self.cs("cb2", ch, ch + 1), op0=ALU.mult, op1=ALU.add),
                                reads=[buf.b, self.cst_t.b], writes=[co.b])
                            for k in range(1, 4):
                                P.dve.op(lambda e, buf=buf, n=n, dstc=dstc, ch=ch, k=k: e.scalar_tensor_tensor(
                                    out=dstc, in0=buf[:, 1 + k:1 + k + n], scalar=self.cs("cw2", ch * 4 + k, ch * 4 + k + 1),
                                    in1=dstc, op0=ALU.mult, op1=ALU.add), reads=[buf.b, self.cst_t.b, co.b], writes=[co.b])
                        if ch >= 32:
                            P.act.op(lambda e, co=co: e.activation(out=cvb[:], in_=co[:], func=AF.Silu), reads=[co.b], writes=[cvb.b])
                            gq = (ch - 32) % 8
                            dd, bb = (d_bf, b_bf) if ch < 40 else (d_cf, b_cf)
                            P.sp.op(lambda e, dd=dd, gq=gq: e.dma_start(out=dd[:, gq * NT:(gq + 1) * NT], in_=cvb[:]),
                                    reads=[cvb.b], writes=[bb], dma=True)
                        if ch < 40:
                            P.act.op(lambda e, co=co: e.activation(out=co[:], in_=co[:], func=AF.Silu), reads=[co.b], writes=[co.b])
                            sgi = (ch // 4) % 2
                            sg_ = stg[sgi]
                            q = ch % 4
                            for t3 in range(3):
                                pb = self.bank()
                                cnt = 4 if t3 < 2 else 1
                                for tq in range(cnt):
                                    tci = t3 * 4 + tq
                                    rows = 128 if tci < 8 else NS
                                    P.pe.op(lambda e, pb=pb, tq=tq, tci=tci, rows=rows, co=co: e.transpose(
                                        pb[0:rows, tq * 128:(tq + 1) * 128], co[:, tci * 128:tci * 128 + rows], self.ident[:]),
                                        reads=[co.b, self.ident.b], writes=[pb.b])
                                if t3 < 2:
                                    P.act.op(lambda e, pb=pb, sg_=sg_, t3=t3, q=q: e.copy(
                                        out=sg_[:, t3 * 4:(t3 + 1) * 4, q * 128:(q + 1) * 128],
                                        in_=pb[:].rearrange("p (a b) -> p a b", a=4)), reads=[pb.b], writes=[sg_.b])
                                else:
                                    P.act.op(lambda e, pb=pb, sg_=sg_, q=q: e.copy(out=sg_[0:NS, 8, q * 128:(q + 1) * 128], in_=pb[0:NS, 0:128]),
                                             reads=[pb.b], writes=[sg_.b])
                            if q == 3:
                                if ch < 32:
                                    dd, bb, c0 = d_xs, b_xs, (ch // 4) * 512
                                else:
                                    dd, bb, c0 = d_bs, b_bs, ((ch - 32) // 4) * 512
                                P.sp.op(lambda e, dd=dd, c0=c0, sg_=sg_: e.dma_start(
                                    out=dd[0:NP, c0:c0 + 512].rearrange("(t p) n -> p t n", p=128), in_=sg_[:, 0:8, :]),
                                    reads=[sg_.b], writes=[bb], dma=True)
                                P.sp.op(lambda e, dd=dd, c0=c0, sg_=sg_: e.dma_start(out=dd[NP:NT, c0:c0 + 512], in_=sg_[0:NS, 8, :]),
                                        reads=[sg_.b], writes=[bb], dma=True)
                for (pre, dst) in ((pre_p, io["conv2p"]), (pre_s, io["conv2s"])):
                    for pc3 in range(3):
                        for g in range(4):
                            pb = self.bank()
                            for q in range(4):
                                ch = pc3 * 16 + g * 4 + q
                                P.pe.op(lambda e, pb=pb, q=q, ch=ch, pre=pre: e.transpose(pb[0:4, q * 128:(q + 1) * 128], pre[:, ch, :], self.ident[:]),
                                        reads=[pre.b, self.ident.b], writes=[pb.b])
                            P.act.op(lambda e, pb=pb, g=g: e.copy(out=stT[0:4, g * 512:(g + 1) * 512], in_=pb[0:4, :]), reads=[pb.b], writes=[stT.b])
                        P.sp.op(lambda e, dst=dst, pc3=pc3: e.dma_start(out=dst[:, pc3 * 2048:(pc3 + 1) * 2048], in_=stT[1:4, :]),
                                reads=[stT.b], dma=True)
                P.barrier()
            if SSD_STAGE < 2:
                return
            with ExitStack() as st:
                hst = self.sb(st, "s2_hst", [128, 4096])
                hsb = self.sb(st, "s2_hsb", [128, 4096], BF16)
                dlog = self.sb(st, "s2_dlog", [128, 64])
                xs = self.sb(st, "s2_xs", [128, 4096], BF16)
                zs = self.sb(st, "s2_zs", [128, 4096], BF16)
                bs = self.sb(st, "s2_bs", [128, 1024], BF16)
                bf = self.sb(st, "s2_bf", [128, 8, 128], BF16)
                cf = self.sb(st, "s2_cf", [128, 8, 128], BF16)
                dtx = self.sb(st, "s2_dtx", [128, 4096], BF16)
                sm = self.sb(st, "s2_sm", [128, 8, 64])
                Xgs = [self.sb(st, "s2_Xg%d" % i, [128, 1024]) for i in range(2)]
                Dgs = [self.sb(st, "s2_Dg%d" % i, [128, 1024]) for i in range(2)]
                segs = [self.sb(st, "s2_seg%d" % i, [128, 1024], BF16) for i in range(2)]
                Mgs = [self.sb(st, "s2_Mg%d" % i, [128, 1024], BF16) for i in range(2)]
                cbms = [self.sb(st, "s2_cbm%d" % i, [128, 128], BF16) for i in range(2)]
                ygs = [self.sb(st, "s2_yg%d" % i, [128, 512]) for i in range(2)]
                yg2s = [self.sb(st, "s2_yg2%d" % i, [128, 512]) for i in range(2)]
                ssq = self.sb(st, "s2_ssq", [128, 8, 4])
                ynT_ = self.sb(st, "s2_ynT", [128, 4096])
                ynF = self.sb(st, "s2_ynF", [128, 32, 128], BF16)
                class _V2:
                    pass
                gS = _V2()
                gS.b = ynT_.b
                gS_v = ynT_[:, 0:2048].rearrange("p (r n) -> p r n", r=4)
                gD = self.sb(st, "s2_gD", [128, 4, 64])
                tS = self.sb(st, "s2_tS", [128, 512])
                class _V:
                    pass
                hio = _V()
                hio.b = ynT_.b
                hio_v = ynT_[:].rearrange("p (c n) -> p c n", c=32)

                def chunk(r0, L, ti, full, first_state):
                    P.sp.op(lambda e: e.dma_start(out=xs[0:L, :], in_=d_xs[r0:r0 + L, :]), reads=[b_xs], writes=[xs.b], dma=True)
                    P.sp.op(lambda e: e.dma_start(out=bs[0:L, :], in_=d_bs[r0:r0 + L, :]), reads=[b_bs], writes=[bs.b], dma=True)
                    if CHUNK_OPS < 2:
                        return
                    P.dve.op(lambda e: e.tensor_tensor(out=sm[0:L, 0, :], in0=dtT[0:L, ti, :], in1=aneg[0:L, :], op=ALU.mult),
                             reads=[dtT.b, aneg.b], writes=[sm.b])
                    p1 = self.bank()
                    P.pe.op(lambda e: e.matmul(p1[0:L, 0:64], lhsT=U[0:L, 0:L], rhs=sm[0:L, 0, :], start=True, stop=True),
                            reads=[U.b, sm.b], writes=[p1.b])
                    P.pe.op(lambda e: e.matmul(p1[0:L, 64:128], lhsT=SU[0:L, 0:L], rhs=sm[0:L, 0, :], start=True, stop=True),
                            reads=[SU.b, sm.b], writes=[p1.b])
                    P.pe.op(lambda e: e.matmul(p1[:, 128:192], lhsT=self.ones[0:L, :], rhs=sm[0:L, 0, :], start=True, stop=True),
                            reads=[self.ones.b, sm.b], writes=[p1.b])
                    P.act.op(lambda e: e.mul(out=sm[0:L, 1, :], in_=p1[0:L, 0:64], mul=-1.0), reads=[p1.b], writes=[sm.b])
                    P.act.op(lambda e: e.activation(out=sm[0:L, 2, :], in_=p1[0:L, 0:64], func=AF.Exp), reads=[p1.b], writes=[sm.b])
                    P.act.op(lambda e: e.activation(out=sm[0:L, 3, :], in_=p1[0:L, 64:128], func=AF.Exp), reads=[p1.b], writes=[sm.b])
                    P.act.op(lambda e: e.activation(out=sm[:, 4, :], in_=p1[:, 128:192], func=AF.Exp), reads=[p1.b], writes=[sm.b])
                    if not full:
                        P.dve.op(lambda e: e.tensor_tensor(out=dlog[:], in0=dlog[:], in1=sm[:, 4, :], op=ALU.mult),
                                 reads=[sm.b, dlog.b], writes=[dlog.b])
                    P.dve.op(lambda e: e.tensor_tensor(out=sm[0:L, 5, :], in0=dtT[0:L, ti, :], in1=sm[0:L, 3, :], op=ALU.mult),
                             reads=[dtT.b, sm.b], writes=[sm.b])
                    if CHUNK_OPS < 3:
                        return
                    xs3 = xs[0:L, :].rearrange("p (h q) -> p h q", h=64)
                    if not full:
                        P.dve.op(lambda e: e.tensor_tensor(out=dtx[0:L, :].rearrange("p (h q) -> p h q", h=64), in0=xs3,
                                                           in1=self.vap(sm[0:L, 5, :], [[1, 64], [0, 64]]), op=ALU.mult),
                                 reads=[xs.b, sm.b], writes=[dtx.b])
                    if full:
                        P.sp.op(lambda e: e.dma_start(out=zs[0:L, :], in_=d_zs[r0:r0 + L, :]), reads=[b_zs], writes=[zs.b], dma=True)
                        for (dd, bb, tt_) in ((d_bf, b_bf, bf), (d_cf, b_cf, cf)):
                            P.sp.op(lambda e, dd=dd, tt_=tt_: e.dma_start(
                                out=tt_[:, :, 0:L], in_=dd[:].rearrange("p (g t) -> p g t", g=8)[:, :, r0:r0 + L]),
                                reads=[bb], writes=[tt_.b], dma=True)
                        P.dve.op(lambda e: e.tensor_tensor(out=dtx[0:L, :].rearrange("p (h q) -> p h q", h=64), in0=xs3,
                                                           in1=self.vap(dtT[0:L, ti, :], [[1, 64], [0, 64]]), op=ALU.mult),
                                 reads=[xs.b, dtT.b], writes=[dtx.b])
                        for g in range(8):
                            Xg, Dg, segb, Mg, cbm, yg, yg2 = Xgs[g % 2], Dgs[g % 2], segs[g % 2], Mgs[g % 2], cbms[g % 2], ygs[g % 2], yg2s[g % 2]
                            junk = yg2
                            P.dve.op(lambda e, g=g, Xg=Xg: e.tensor_tensor(
                                out=Xg[0:L, 0:8 * L].rearrange("p (a b) -> p a b", a=8),
                                in0=self.vap(sm[0:L, 0, g * 8:(g + 1) * 8], [[1, 8], [0, L]]),
                                in1=self.vap(U[0:L, 0:L], [[0, 8], [1, L]]), op=ALU.mult),
                                reads=[sm.b, U.b], writes=[Xg.b])
                            pa = self.bank()
                            pa2 = self.bank()
                            half = 4 * L
                            P.pe.op(lambda e, pa=pa, Xg=Xg: e.matmul(pa[0:L, 0:half], lhsT=self.ones[0:L, 0:L], rhs=Xg[0:L, 0:half], start=True, stop=True),
                                    reads=[self.ones.b, Xg.b], writes=[pa.b])
                            P.pe.op(lambda e, pa2=pa2, Xg=Xg: e.matmul(pa2[0:L, 0:half], lhsT=self.ones[0:L, 0:L], rhs=Xg[0:L, half:2 * half],
                                                               start=True, stop=True), reads=[self.ones.b, Xg.b], writes=[pa2.b])
                            for e8 in range(8):
                                pp = pa if e8 < 4 else pa2
                                o = (e8 % 4) * L
                                P.dve.op(lambda e, g=g, e8=e8, pp=pp, o=o, Dg=Dg: e.tensor_scalar(
                                    out=Dg[0:L, e8 * L:(e8 + 1) * L], in0=pp[0:L, o:o + L], scalar1=sm[0:L, 1, g * 8 + e8:g * 8 + e8 + 1],
                                    scalar2=0.0, op0=ALU.add, op1=ALU.min), reads=[pp.b, sm.b], writes=[Dg.b])
                            P.act.op(lambda e, segb=segb, Dg=Dg: e.activation(out=segb[0:L, 0:8 * L], in_=Dg[0:L, 0:8 * L], func=AF.Exp),
                                     reads=[Dg.b], writes=[segb.b])
                            pc = self.bank()
                            P.pe.op(lambda e, g=g, pc=pc: e.matmul(pc[0:L, 0:L], lhsT=bf[:, g, 0:L], rhs=cf[:, g, 0:L], start=True, stop=True),
                                    reads=[bf.b, cf.b], writes=[pc.b])
                            P.dve.op(lambda e, pc=pc, cbm=cbm: e.tensor_tensor(out=cbm[0:L, 0:L], in0=pc[0:L, 0:L], in1=U[0:L, 0:L], op=ALU.mult),
                                     reads=[pc.b, U.b], writes=[cbm.b])
                            P.dve.op(lambda e, Mg=Mg, segb=segb, cbm=cbm: e.tensor_tensor(out=Mg[0:L, 0:8 * L].rearrange("p (a b) -> p a b", a=8),
                                                               in0=segb[0:L, 0:8 * L].rearrange("p (a b) -> p a b", a=8),
                                                               in1=self.vap(cbm[0:L, 0:L], [[0, 8], [1, L]]), op=ALU.mult),
                                     reads=[segb.b, cbm.b], writes=[Mg.b])
                            pd = self.bank()
                            for e8 in range(8):
                                h = g * 8 + e8
                                P.pe.op(lambda e, e8=e8, h=h, pd=pd, Mg=Mg: e.matmul(pd[0:L, e8 * 64:(e8 + 1) * 64], lhsT=Mg[0:L, e8 * L:(e8 + 1) * L],
                                                                             rhs=dtx[0:L, h * 64:(h + 1) * 64], start=True, stop=True),
                                        reads=[Mg.b, dtx.b], writes=[pd.b])
                            po = self.bank()
                            P.pe.op(lambda e, g=g, po=po: e.matmul(po[0:L, :], lhsT=cf[:, g, 0:L], rhs=hsb[:, g * 512:(g + 1) * 512],
                                                                  start=True, stop=True), reads=[cf.b, hsb.b], writes=[po.b])
                            gsl = slice(g * 512, (g + 1) * 512)
                            P.dve.op(lambda e, g=g, po=po, yg=yg: e.tensor_tensor(
                                out=yg[0:L, :].rearrange("p (a b) -> p a b", a=8), in0=po[0:L, :].rearrange("p (a b) -> p a b", a=8),
                                in1=self.vap(sm[0:L, 2, g * 8:(g + 1) * 8], [[1, 8], [0, 64]]), op=ALU.mult),
                                reads=[po.b, sm.b], writes=[yg.b])
                            P.dve.op(lambda e, pd=pd, yg=yg: e.tensor_tensor(out=yg[0:L, :], in0=pd[0:L, :], in1=yg[0:L, :], op=ALU.add),
                                     reads=[pd.b, yg.b], writes=[yg.b])
                            P.dve.op(lambda e, g=g, gsl=gsl, yg2=yg2: e.tensor_tensor(
                                out=yg2[0:L, :].rearrange("p (a b) -> p a b", a=8), in0=xs[0:L, gsl].rearrange("p (a b) -> p a b", a=8),
                                in1=self.vap(self.cs("dskip")[0:L, g * 8:(g + 1) * 8], [[1, 8], [0, 64]]), op=ALU.mult),
                                reads=[xs.b, self.cst_t.b], writes=[yg2.b])
                            P.dve.op(lambda e, yg=yg, yg2=yg2: e.tensor_tensor(out=yg[0:L, :], in0=yg[0:L, :], in1=yg2[0:L, :], op=ALU.add),
                                     reads=[yg.b, yg2.b], writes=[yg.b])
                            P.dve.op(lambda e, gsl=gsl, yg=yg: e.tensor_tensor(out=yg[0:L, :], in0=yg[0:L, :], in1=zs[0:L, gsl], op=ALU.mult),
                                     reads=[yg.b, zs.b], writes=[yg.b])
                            P.act.op(lambda e, g=g, junk=junk, yg=yg: e.activation(out=junk[0:L, :], in_=yg[0:L, :], func=AF.Square, accum_out=ssq[0:L, g, 0:1]),
                                     reads=[yg.b], writes=[junk.b, ssq.b])
                            P.dve.op(lambda e, g=g: e.tensor_scalar(out=ssq[0:L, g, 1:2], in0=ssq[0:L, g, 0:1], scalar1=1.0 / 512, scalar2=EPS,
                                                                    op0=ALU.mult, op1=ALU.add), reads=[ssq.b], writes=[ssq.b])
                            P.act.op(lambda e, g=g: e.activation(out=ssq[0:L, g, 2:3], in_=ssq[0:L, g, 1:2], func=AF.Sqrt), reads=[ssq.b], writes=[ssq.b])
                            P.dve.op(lambda e, g=g: e.reciprocal(out=ssq[0:L, g, 3:4], in_=ssq[0:L, g, 2:3]), reads=[ssq.b], writes=[ssq.b])
                            P.dve.op(lambda e, g=g, gsl=gsl, yg=yg: e.tensor_scalar(out=ynT_[0:L, gsl], in0=yg[0:L, :], scalar1=ssq[0:L, g, 3:4], scalar2=None,
                                                                             op0=ALU.mult), reads=[yg.b, ssq.b], writes=[ynT_.b])
                        P.dve.op(lambda e: e.tensor_tensor(out=dtx[0:L, :].rearrange("p (h q) -> p h q", h=64),
                                                           in0=dtx[0:L, :].rearrange("p (h q) -> p h q", h=64),
                                                           in1=self.vap(sm[0:L, 3, :], [[1, 64], [0, 64]]), op=ALU.mult),
                                 reads=[dtx.b, sm.b], writes=[dtx.b])
                        for c4 in range(8):
                            pb = self.bank()
                            for q in range(4):
                                cc = c4 * 4 + q
                                P.pe.op(lambda e, pb=pb, q=q, cc=cc: e.transpose(pb[:, q * 128:q * 128 + L], ynT_[0:L, cc * 128:(cc + 1) * 128],
                                                                               self.ident[0:L, 0:L]), reads=[ynT_.b, self.ident.b], writes=[pb.b])
                            for q in range(4):
                                cc = c4 * 4 + q
                                P.act.op(lambda e, pb=pb, q=q, cc=cc: e.activation(out=ynF[:, cc, 0:L], in_=pb[:, q * 128:q * 128 + L], func=AF.Copy,
                                                                                  scale=self.cs("gn2", cc, cc + 1)),
                                         reads=[pb.b, self.cst_t.b], writes=[ynF.b])
                        P.sp.op(lambda e: e.dma_start(out=d_yn[:].rearrange("p (c t) -> p c t", c=32)[:, :, r0:r0 + L], in_=ynF[:, :, 0:L]),
                                reads=[ynF.b], writes=[b_yn], dma=True)
                    if CHUNK_OPS < 4:
                        return
                    for g in range(8):
                        pcs = self.bank()
                        P.pe.op(lambda e, g=g, pcs=pcs: e.matmul(pcs[:, :], lhsT=bs[0:L, g * 128:(g + 1) * 128], rhs=dtx[0:L, g * 512:(g + 1) * 512],
                                                                start=True, stop=True), reads=[bs.b, dtx.b], writes=[pcs.b])
                        gsl = slice(g * 512, (g + 1) * 512)
                        if first_state and False:
                            pass
                        P.dve.op(lambda e, g=g, gsl=gsl: e.tensor_tensor(
                            out=hst[:, gsl].rearrange("p (a b) -> p a b", a=8), in0=hst[:, gsl].rearrange("p (a b) -> p a b", a=8),
                            in1=self.vap(sm[:, 4, g * 8:(g + 1) * 8], [[1, 8], [0, 64]]), op=ALU.mult),
                            reads=[hst.b, sm.b], writes=[hst.b])
                        P.dve.op(lambda e, gsl=gsl, pcs=pcs: e.tensor_tensor(out=hst[:, gsl], in0=pcs[:, :], in1=hst[:, gsl], op=ALU.add),
                                 reads=[hst.b, pcs.b], writes=[hst.b])
                        if full:
                            P.act.op(lambda e, gsl=gsl: e.copy(out=hsb[:, gsl], in_=hst[:, gsl]), reads=[hst.b], writes=[hsb.b])

                def state_out(dst):
                    for c4 in range(8):
                        pb = self.bank()
                        for q in range(4):
                            cc = c4 * 4 + q
                            P.pe.op(lambda e, pb=pb, q=q, cc=cc: e.transpose(pb[:, q * 128:(q + 1) * 128], hst[:, cc * 128:(cc + 1) * 128], self.ident[:]),
                                    reads=[hst.b, self.ident.b], writes=[pb.b])
                        P.act.op(lambda e, pb=pb, c4=c4: e.copy(out=hio_v[:, c4 * 4:(c4 + 1) * 4, :], in_=pb[:].rearrange("p (a b) -> p a b", a=4)),
                                 reads=[pb.b], writes=[hio.b])
                    P.sp.op(lambda e: e.dma_start(out=dst.rearrange("(c p) n -> p c n", p=128), in_=hio_v), reads=[hio.b], dma=True)

                P.dve.op(lambda e: e.memset(hst[:], 0.0), writes=[hst.b])
                P.dve.op(lambda e: e.memset(dlog[:], 1.0), writes=[dlog.b])
                if SCAN_STEP < 1:
                    return
                for ci in range(8 if SCAN_STEP != 1 else 1):
                    chunk(ci * 128, 128, ci, False, ci == 0)
                if SCAN_STEP < 2:
                    return
                ginD = nc.dram_tensor("gin_sd", [128, 64], F32)
                goutD = nc.dram_tensor("gout_sd", [4 * 128, 64], F32)
                bd1, bd2 = Buf(), Buf()
                P.sp.op(lambda e: e.dma_start(out=ginD[:], in_=dlog[:]), reads=[dlog.b], writes=[bd1], dma=True)
                P.pool.op(lambda e: e.collective_compute("AllGather", ALU.bypass, replica_groups=GROUPS, ins=[ginD[:]], outs=[goutD[:]]),
                          reads=[bd1], writes=[bd2], cc=True)
                P.sp.op(lambda e: e.dma_start(out=gD[:], in_=goutD[:].rearrange("(r p) n -> p r n", p=128)), reads=[bd2], writes=[gD.b], dma=True)
                for r in range(1, 3):
                    P.dve.op(lambda e, r=r: e.tensor_scalar(out=gD[:, r, :], in0=gD[:, r, :], scalar1=self.cs("valid", r, r + 1),
                                                            scalar2=self.cs("nvalid", r, r + 1), op0=ALU.mult, op1=ALU.add),
                             reads=[gD.b, self.cst_t.b], writes=[gD.b])
                for sl in range(8):
                    ginS = nc.dram_tensor("gin_ss%d" % sl, [128, 512], F32)
                    goutS = nc.dram_tensor("gout_ss%d" % sl, [4 * 128, 512], F32)
                    bs1, bs2 = Buf(), Buf()
                    ssl = slice(sl * 512, (sl + 1) * 512)
                    P.sp.op(lambda e, ginS=ginS, ssl=ssl: e.dma_start(out=ginS[:], in_=hst[:, ssl]), reads=[hst.b], writes=[bs1], dma=True)
                    P.pool.op(lambda e, ginS=ginS, goutS=goutS: e.collective_compute("AllGather", ALU.bypass, replica_groups=GROUPS,
                                                                                   ins=[ginS[:]], outs=[goutS[:]]),
                              reads=[bs1], writes=[bs2], cc=True)
                    P.sp.op(lambda e, goutS=goutS: e.dma_start(out=gS_v, in_=goutS[:].rearrange("(r p) n -> p r n", p=128)),
                            reads=[bs2], writes=[gS.b], dma=True)
                    P.dve.op(lambda e: e.tensor_scalar(out=tS[:], in0=gS_v[:, 0, :], scalar1=self.cs("valid", 0, 1), scalar2=None, op0=ALU.mult),
                             reads=[gS.b, self.cst_t.b], writes=[tS.b])
                    for r in range(1, 3):
                        P.dve.op(lambda e, r=r, sl=sl: e.tensor_tensor(
                            out=tS[:].rearrange("p (a b) -> p a b", a=8), in0=tS[:].rearrange("p (a b) -> p a b", a=8),
                            in1=self.vap(gD[:, r, sl * 8:(sl + 1) * 8], [[1, 8], [0, 64]]), op=ALU.mult),
                            reads=[tS.b, gD.b], writes=[tS.b])
                        P.dve.op(lambda e, r=r: e.scalar_tensor_tensor(out=tS[:], in0=gS_v[:, r, :], scalar=self.cs("valid", r, r + 1), in1=tS[:],
                                                                      op0=ALU.mult, op1=ALU.add),
                                 reads=[gS.b, self.cst_t.b, tS.b], writes=[tS.b])
                    P.dve.op(lambda e, ssl=ssl: e.tensor_copy(out=hst[:, ssl], in_=tS[:]), reads=[tS.b], writes=[hst.b])
                    P.act.op(lambda e, ssl=ssl: e.copy(out=hsb[:, ssl], in_=tS[:]), reads=[tS.b], writes=[hsb.b])
                if SCAN_STEP < 3:
                    return
                for ci in range(8 if SCAN_STEP >= 4 else 1):
                    chunk(ci * 128, 128, ci, True, False)
                if SCAN_STEP < 5:
                    return
                state_out(io["ssm2p"])
                if SCAN_STEP < 6:
                    return
                P.sp.op(lambda e: e.dma_start(out=hio_v, in_=io["st2s"].rearrange("(c p) n -> p c n", p=128)), writes=[hio.b], dma=True)
                for c4 in range(8):
                    pb = self.bank()
                    for q in range(4):
                        cc = c4 * 4 + q
                        P.pe.op(lambda e, pb=pb, q=q, cc=cc: e.transpose(pb[:, q * 128:(q + 1) * 128], hio_v[:, cc, :], self.ident[:]),
                                reads=[hio.b, self.ident.b], writes=[pb.b])
                    P.dve.op(lambda e, pb=pb, c4=c4: e.tensor_copy(out=hst[:, c4 * 512:(c4 + 1) * 512], in_=pb[:]), reads=[pb.b], writes=[hst.b])
                    P.act.op(lambda e, c4=c4: e.copy(out=hsb[:, c4 * 512:(c4 + 1) * 512], in_=hst[:, c4 * 512:(c4 + 1) * 512]),
                             reads=[hst.b], writes=[hsb.b])
                chunk(NP, NS, 8, True, False)
                state_out(io["ssm2s"])
                P.barrier()
            if SSD_STAGE < 3:
                return
            with ExitStack() as st:
                ynA = self.sb(st, "s2_ynA", [128, 32, NT], BF16)
                wbs = [self.sb(st, "s2_wo%d" % i, [128, 32, 256], BF16) for i in range(3)]
                P.sp.op(lambda e: e.dma_start(out=ynA[:], in_=d_yn[:].rearrange("p (c t) -> p c t", c=32)), reads=[b_yn], writes=[ynA.b], dma=True)
                for blk in range(8):
                    wb = wbs[blk % 3]
                    self.load_wblock(wb, io["l2_w_out"], blk * 256, 256, kc=32)

                    def cons_o(ps, m, c, nn, blk=blk):
                        P.dve.op(lambda e: e.tensor_tensor(out=self.xT[:, blk * 2 + m, c:c + nn], in0=ps[:, 0:nn],
                                                           in1=self.xT[:, blk * 2 + m, c:c + nn], op=ALU.add),
                                 reads=[ps.b, self.xT.b], writes=[self.xT.b])
                    self.proj_F(wb, 0, 2, ynA, ynA.b, 0, NT, cons_o, kc=32)
                P.barrier()

    def placeholders(self):
        P, io = self.P, self.io
        if not self.todo:
            return
        with ExitStack() as st:
            z = self.sb(st, "zeros", [128, 6144])
            P.pool.op(lambda e: e.memset(z[:], 0.0), writes=[z.b])
            for nm, shp in self.todo:
                r, c = shp
                for r0 in range(0, r, 128):
                    rr = min(128, r - r0)
                    P.sp.op(lambda e, nm=nm, r0=r0, rr=rr, c=c: e.dma_start(out=io[nm][r0:r0 + rr, :], in_=z[0:rr, 0:c]),
                            reads=[z.b], dma=True)
            P.barrier()

    def final_norm(self):
        P, io = self.P, self.io
        with ExitStack() as st:
            sq = self.sb(st, "f_sq", [128, 512])
            rstd = self.sb(st, "f_rstd", [128, 512])
            yF = self.sb(st, "f_yF", [128, KC, 128])
            yTs = [self.sb(st, "f_yT%d" % i, [128, D]) for i in range(2)]
            for t in range(9):
                rows = 128 if t < 8 else NS
                c0 = t * 128
                self.rms_F(None, c0, rows, None, sq, rstd)
                for k in range(KC):
                    eng = P.dve
                    eng.op(lambda e, k=k, c0=c0, rows=rows: e.scalar_tensor_tensor(
                        out=yF[:, k, 0:rows], in0=self.xT[:, k, c0:c0 + rows], scalar=self.cs("fnorm", k, k + 1),
                        in1=rstd[:, 0:rows], op0=ALU.mult, op1=ALU.mult),
                        reads=[self.xT.b, rstd.b, self.cst_t.b], writes=[yF.b])
                yT = yTs[t % 2]
                for k4 in range(4):
                    pb = self.bank()
                    for kk in range(4):
                        k = k4 * 4 + kk
                        P.pe.op(lambda e, pb=pb, k=k, kk=kk, rows=rows: e.transpose(
                            pb[0:rows, kk * 128:(kk + 1) * 128], yF[:, k, 0:rows], self.ident[:]),
                            reads=[yF.b, self.ident.b], writes=[pb.b])
                    if k4 % 2 == 0:
                        P.dve.op(lambda e, pb=pb, yT=yT, k4=k4, rows=rows: e.tensor_copy(
                            out=yT[0:rows, k4 * 512:(k4 + 1) * 512], in_=pb[0:rows, :]),
                            reads=[pb.b], writes=[yT.b])
                    else:
                        P.act.op(lambda e, pb=pb, yT=yT, k4=k4, rows=rows: e.copy(
                            out=yT[0:rows, k4 * 512:(k4 + 1) * 512], in_=pb[0:rows, :]),
                            reads=[pb.b], writes=[yT.b])
                dst = io["yp"][t * 128:(t + 1) * 128, :] if t < 8 else io["ys"]
                P.sp.op(lambda e, dst=dst, yT=yT, rows=rows: e.dma_start(out=dst, in_=yT[0:rows, :]),
                        reads=[yT.b], dma=True)
            P.barrier()


    def load_wblock(self, wb, w_ap, col0, ncols, kc=KC):
        src = w_ap[:, col0:col0 + ncols].rearrange("(k p) n -> p k n", p=128)
        self.P.pool.op(lambda e: e.dma_start(out=wb[:, 0:kc, 0:ncols], in_=src), writes=[wb.b], dma=True)

    def subtiles(self, c0, n):
        out = []
        c = c0
        while c < c0 + n:
            m = min(512 - (c % 512), c0 + n - c)
            out.append((c, m))
            c += m
        return out

    def proj_F(self, wb, wcol0, nchunks, hT, hb, hoff, n, consumer, kc=KC, splits=None):
        P = self.P
        for m in range(nchunks):
            for (c, nn) in (splits if splits is not None else self.subtiles(hoff, n)):
                ps = self.bank()
                for k in range(kc):
                    P.pe.op(lambda e, ps=ps, k=k, m=m, c=c, nn=nn: e.matmul(
                        ps[:, 0:nn], lhsT=wb[:, k, wcol0 + m * 128:wcol0 + (m + 1) * 128], rhs=hT[:, k, c:c + nn],
                        start=(k == 0), stop=(k == kc - 1)),
                        reads=[wb.b, hb], writes=[ps.b])
                consumer(ps, m, c - hoff, nn)

    def mla_layer(self, L):
        P, io, nc = self.P, self.io, self.nc
        w_in = io["l%d_w_in" % L]
        with ExitStack() as stL:
            qanT = self.sb(stL, "m%d_qanT" % L, [128, 4, NT], BF16)
            lat = self.sb(stL, "m%d_lat" % L, [128, 5, NT], BF16)
            sgT = self.sb(stL, "m%d_sgT" % L, [128, KC, NT], BF16)
            self.ckvn_s = self.sb(stL, "m%d_ckvns" % L, [NS, 512], BF16)
            P.pool.op(lambda e: e.memset(lat[:, 4, :], 0.0), writes=[lat.b])
            cstm_t = self.sb(stL, "m%d_cstm" % L, [128, self.nmst])
            self.cstm = cstm_t.t
            P.sp.op(lambda e: e.dma_start(out=cstm_t[:], in_=io["cstm"]), writes=[self.cst_t.b], dma=True)
            with ExitStack() as st:
                wbs = [self.sb(st, "m%d_wb%d" % (L, i), [128, KC, 576], BF16) for i in range(2)]
                hT = self.sb(st, "m%d_hT" % L, [128, KC, 528], BF16)
                sq = self.sb(st, "m%d_sq" % L, [128, 512])
                rstd = self.sb(st, "m%d_rstd" % L, [128, 512])
                qa = self.sb(st, "m%d_qa" % L, [128, 4, 528])
                ckvn = self.sb(st, "m%d_ckvn" % L, [128, 512])
                junk = self.sb(st, "m%d_junk" % L, [128, 512])
                ss = self.sb(st, "m%d_ss" % L, [128, 4])
                kp = self.sb(st, "m%d_kp" % L, [128, 64])
                kr = self.sb(st, "m%d_kr" % L, [128, 4, 32])
                kpo = self.sb(st, "m%d_kpo" % L, [128, 64])
                wi = 0
                for tt in range(2):
                    tc0 = tt * 512
                    n = 512 if tt == 0 else 528
                    for (c, nn) in self.subtiles(tc0, n):
                        self.rms_F(hT, c, nn, "norm%d" % L, sq, rstd, off=c - tc0)
                    wb = wbs[wi % 2]; wi += 1
                    self.load_wblock(wb, w_in, 0, 512)

                    def cons_qa(ps, m, c, nn):
                        P.act.op(lambda e: e.copy(out=qa[:, m, c:c + nn], in_=ps[:, 0:nn]),
                                 reads=[ps.b], writes=[qa.b])
                    self.proj_F(wb, 0, 4, hT, hT.b, 0, n, cons_qa)
                    for (c, nn) in self.subtiles(0, n):
                        pb = self.bank()
                        for m in range(4):
                            P.act.op(lambda e, m=m, c=c, nn=nn: e.activation(out=sq[:, 0:nn], in_=qa[:, m, c:c + nn],
                                                                             func=AF.Square),
                                     reads=[qa.b], writes=[sq.b])
                            P.pe.op(lambda e, m=m, pb=pb, nn=nn: e.matmul(pb[:, 0:nn], lhsT=self.ones[:], rhs=sq[:, 0:nn],
                                                                         start=(m == 0), stop=(m == 3)),
                                    reads=[sq.b, self.ones.b], writes=[pb.b])
                        P.dve.op(lambda e, pb=pb, nn=nn: e.tensor_scalar(out=rstd[:, 0:nn], in0=pb[:, 0:nn],
                                                                        scalar1=1.0 / 512, scalar2=EPS,
                                                                        op0=ALU.mult, op1=ALU.add),
                                 reads=[pb.b], writes=[rstd.b])
                        P.act.op(lambda e, nn=nn: e.activation(out=rstd[:, 0:nn], in_=rstd[:, 0:nn], func=AF.Sqrt),
                                 reads=[rstd.b], writes=[rstd.b])
                        P.dve.op(lambda e, nn=nn: e.reciprocal(out=rstd[:, 0:nn], in_=rstd[:, 0:nn]),
                                 reads=[rstd.b], writes=[rstd.b])
                        for m in range(4):
                            P.dve.op(lambda e, m=m, c=c, nn=nn, tc0=tc0: e.scalar_tensor_tensor(
                                out=qanT[:, m, tc0 + c:tc0 + c + nn], in0=qa[:, m, c:c + nn],
                                scalar=self.cs("qnorm%d" % L, m, m + 1), in1=rstd[:, 0:nn],
                                op0=ALU.mult, op1=ALU.mult),
                                reads=[qa.b, rstd.b, self.cst_t.b], writes=[qanT.b])
                    wb = wbs[wi % 2]; wi += 1
                    self.load_wblock(wb, w_in, 512, 576)
                    chunks = [(tc0 + i * 128, 128) for i in range(4)]
                    if tt == 1:
                        chunks.append((NP, NS))
                    for (c, rows) in chunks:
                        t = c // 128
                        ps1 = self.bank()
                        ps2 = self.bank()
                        for k in range(KC):
                            P.pe.op(lambda e, k=k, c=c, rows=rows, ps1=ps1, wb=wb, tc0=tc0: e.matmul(
                                ps1[0:rows, :], lhsT=hT[:, k, c - tc0:c - tc0 + rows], rhs=wb[:, k, 0:512],
                                start=(k == 0), stop=(k == KC - 1)), reads=[hT.b, wb.b], writes=[ps1.b])
                        for k in range(KC):
                            P.pe.op(lambda e, k=k, c=c, rows=rows, ps2=ps2, wb=wb, tc0=tc0: e.matmul(
                                ps2[0:rows, 0:64], lhsT=hT[:, k, c - tc0:c - tc0 + rows], rhs=wb[:, k, 512:576],
                                start=(k == 0), stop=(k == KC - 1)), reads=[hT.b, wb.b], writes=[ps2.b])
                        P.act.op(lambda e, rows=rows, ps1=ps1: e.activation(out=junk[0:rows, :], in_=ps1[0:rows, :],
                                                                            func=AF.Square, accum_out=ss[0:rows, 0:1]),
                                 reads=[ps1.b], writes=[junk.b, ss.b])
                        P.dve.op(lambda e, rows=rows: e.tensor_scalar(out=ss[0:rows, 1:2], in0=ss[0:rows, 0:1],
                                                                      scalar1=1.0 / 512, scalar2=EPS,
                                                                      op0=ALU.mult, op1=ALU.add),
                                 reads=[ss.b], writes=[ss.b])
                        P.act.op(lambda e, rows=rows: e.activation(out=ss[0:rows, 2:3], in_=ss[0:rows, 1:2], func=AF.Sqrt),
                                 reads=[ss.b], writes=[ss.b])
                        P.dve.op(lambda e, rows=rows: e.reciprocal(out=ss[0:rows, 3:4], in_=ss[0:rows, 2:3]),
                                 reads=[ss.b], writes=[ss.b])
                        P.dve.op(lambda e, rows=rows, ps1=ps1: e.scalar_tensor_tensor(
                            out=ckvn[0:rows, :], in0=ps1[0:rows, :], scalar=ss[0:rows, 3:4],
                            in1=self.cs("kvnorm%d" % L)[0:rows, :], op0=ALU.mult, op1=ALU.mult),
                            reads=[ps1.b, ss.b, self.cst_t.b], writes=[ckvn.b])
                        if c >= NP:
                            P.pool.op(lambda e, rows=rows: e.tensor_copy(out=self.ckvn_s[0:rows, :], in_=ckvn[0:rows, :]),
                                      reads=[ckvn.b], writes=[self.ckvn_s.b])
                        cosv = self.cs("cosT", t * 32, t * 32 + 32)
                        sinv = self.cs("sinT", t * 32, t * 32 + 32)
                        P.act.op(lambda e, rows=rows, ps2=ps2: e.copy(out=kp[0:rows, :], in_=ps2[0:rows, 0:64]),
                                 reads=[ps2.b], writes=[kp.b])
                        for qi, (xa, tb) in enumerate(((0, cosv), (32, sinv), (0, sinv), (32, cosv))):
                            P.dve.op(lambda e, rows=rows, qi=qi, xa=xa, tb=tb: e.tensor_tensor(
                                out=kr[0:rows, qi, :], in0=kp[0:rows, xa:xa + 32], in1=tb[0:rows, :], op=ALU.mult),
                                reads=[kp.b, self.cst_t.b], writes=[kr.b])
                        P.dve.op(lambda e, rows=rows: e.tensor_tensor(out=kpo[0:rows, 0:32], in0=kr[0:rows, 0, :],
                                                                      in1=kr[0:rows, 1, :], op=ALU.subtract),
                                 reads=[kr.b], writes=[kpo.b])
                        P.dve.op(lambda e, rows=rows: e.tensor_tensor(out=kpo[0:rows, 32:64], in0=kr[0:rows, 2, :],
                                                                      in1=kr[0:rows, 3, :], op=ALU.add),
                                 reads=[kr.b], writes=[kpo.b])
                        if c < NP:
                            d1 = io["o%d_ckv_p" % L][c:c + rows, :]
                            d2 = io["o%d_kpe_p" % L][c:c + rows, :]
                        else:
                            d1 = io["o%d_ckv_s" % L]
                            d2 = io["o%d_kpe_s" % L]
                        P.sp.op(lambda e, d1=d1, rows=rows: e.dma_start(out=d1, in_=ckvn[0:rows, :]),
                                reads=[ckvn.b], dma=True)
                        P.sp.op(lambda e, d2=d2, rows=rows: e.dma_start(out=d2, in_=kpo[0:rows, :]),
                                reads=[kpo.b], dma=True)
                        pt = self.bank()
                        for m in range(4):
                            P.pe.op(lambda e, m=m, rows=rows, pt=pt: e.transpose(
                                pt[:, m * 128:m * 128 + rows], ckvn[0:rows, m * 128:(m + 1) * 128],
                                self.ident[0:rows, 0:rows]), reads=[ckvn.b, self.ident.b], writes=[pt.b])
                        P.act.op(lambda e, rows=rows, c=c, pt=pt: e.copy(
                            out=lat[:, 0:4, c:c + rows],
                            in_=pt[:].rearrange("p (a b) -> p a b", a=4)[:, :, 0:rows]),
                            reads=[pt.b], writes=[lat.b])
                        pt2 = self.bank()
                        P.pe.op(lambda e, rows=rows, pt2=pt2: e.transpose(pt2[0:64, 0:rows], kpo[0:rows, :],
                                                                          self.ident[0:rows, 0:rows]),
                                reads=[kpo.b, self.ident.b], writes=[pt2.b])
                        P.act.op(lambda e, rows=rows, c=c, pt2=pt2: e.copy(out=lat[0:64, 4, c:c + rows],
                                                                           in_=pt2[0:64, 0:rows]),
                                 reads=[pt2.b], writes=[lat.b])
                    for gb in range(4):
                        wb = wbs[wi % 2]; wi += 1
                        self.load_wblock(wb, w_in, 1088 + gb * 512, 512)

                        def cons_g(ps, m, c, nn, gb=gb, tc0=tc0):
                            P.act.op(lambda e: e.activation(out=sgT[:, gb * 4 + m, tc0 + c:tc0 + c + nn],
                                                            in_=ps[:, 0:nn], func=AF.Silu),
                                     reads=[ps.b], writes=[sgT.b])
                        self.proj_F(wb, 0, 4, hT, hT.b, 0, n, cons_g)
                P.barrier()
            if ATTN:
                self.mla_attention(L, qanT, lat, sgT)
            with ExitStack() as st:
                wbs = [self.sb(st, "m%d_wo%d" % (L, i), [128, KC, 512], BF16) for i in range(2)]
                for blk in range(4):
                    wb = wbs[blk % 2]
                    self.load_wblock(wb, io["l%d_w_out" % L], blk * 512, 512)

                    def cons_o(ps, m, c, nn, blk=blk):
                        P.dve.op(lambda e: e.tensor_tensor(out=self.xT[:, blk * 4 + m, c:c + nn],
                                                           in0=ps[:, 0:nn], in1=self.xT[:, blk * 4 + m, c:c + nn],
                                                           op=ALU.add),
                                 reads=[ps.b, self.xT.b], writes=[self.xT.b])
                    self.proj_F(wb, 0, 4, sgT, sgT.b, 0, NT, cons_o)
                P.barrier()


LAYERS = [0, 1, 2, 3]
import os
SSD_STAGE = int(os.environ.get('SSD_STAGE', '3'))
SCAN_STEP = int(os.environ.get('SCAN_STEP', '9'))
CHUNK_OPS = int(os.environ.get('CHUNK_OPS', '9'))
ATTN = True


def build_program(coff, ncst, moff, nmst):
    nc = bass.Bass("TRN2", target_bir_lowering=False)
    k = K(nc, coff, ncst, moff, nmst)
    k.build()
    return nc


def kernel(**inp):
    inp = {k: np.asarray(v) for k, v in inp.items()}
    consts = [build_consts(c, inp) for c in range(NCORES)]
    coff = consts[0][0].off
    ncst = consts[0][0].n
    moff = consts[0][1].off
    nmst = consts[0][1].n
    nc = build_program(coff, ncst, moff, nmst)
    in_maps = []
    for c in range(NCORES):
        b, j = c // 4, c % 4
        m = {
            "xp": np.ascontiguousarray(inp["x_prompt"][b, j * NP:(j + 1) * NP]),
            "xs": np.ascontiguousarray(inp["x_sample"][c]),
            "cst": consts[c][0].build(),
            "cstm": consts[c][1].build(),
        }
        for L in (0, 3):
            if L not in LAYERS:
                continue
            m["c%d_ckv" % L] = np.ascontiguousarray(inp["cache_l%d_ckv" % L][c])
            m["c%d_kpe" % L] = np.ascontiguousarray(inp["cache_l%d_kpe" % L][c])
            m["l%d_w_in" % L] = inp["l%d_w_in" % L]
            m["l%d_w_qb" % L] = inp["l%d_w_qb" % L].reshape(512, 16 * 192)
            m["l%d_w_kvb" % L] = inp["l%d_w_kvb" % L].reshape(512, 16 * 256)
            m["l%d_w_out" % L] = inp["l%d_w_out" % L]
        if 2 in LAYERS:
            m["st2c"] = np.ascontiguousarray(inp["state_l2_conv"][c])
            m["st2s"] = np.ascontiguousarray(inp["state_l2_ssm"][c]).reshape(4096, 128)
            m["l2_w_in"] = inp["l2_w_in"]
            m["l2_w_out"] = inp["l2_w_out"]
        if 1 in LAYERS:
            m["st1"] = np.ascontiguousarray(inp["state_l1_conv"][c])
            m["l1_w_in"] = inp["l1_w_in"]
            m["l1_w_out"] = inp["l1_w_out"]
        in_maps.append(m)
    res = run_bass_kernel_spmd(nc, in_maps, core_ids=list(range(NCORES)))
    R = res.results

    def cat_prompt(name):
        return np.stack([np.concatenate([R[b * 4 + j][name] for j in range(4)], axis=0) for b in range(2)], axis=0)

    def cat_sample(name):
        return np.stack([R[c][name] for c in range(NCORES)], axis=0)

    def last_core(name):
        return np.stack([R[b * 4 + 3][name] for b in range(2)], axis=0)

    outs = {}
    outs["y_prompt"] = cat_prompt("yp")
    outs["y_sample"] = cat_sample("ys")
    for L in (0, 3):
        outs["l%d_ckv_p" % L] = cat_prompt("o%d_ckv_p" % L)
        outs["l%d_kpe_p" % L] = cat_prompt("o%d_kpe_p" % L)
        outs["l%d_ckv_s" % L] = cat_sample("o%d_ckv_s" % L)
        outs["l%d_kpe_s" % L] = cat_sample("o%d_kpe_s" % L)
    outs["l1_conv_p"] = last_core("conv1p")
    outs["l1_conv_s"] = cat_sample("conv1s")
    outs["l2_conv_p"] = last_core("conv2p")
    outs["l2_ssm_p"] = last_core("ssm2p").reshape(2, 64, 64, 128)
    outs["l2_conv_s"] = cat_sample("conv2s")
    outs["l2_ssm_s"] = cat_sample("ssm2s").reshape(NCORES, 64, 64, 128)
    order = ["y_prompt", "y_sample", "l0_ckv_p", "l0_kpe_p", "l0_ckv_s", "l0_kpe_s", "l1_conv_p", "l1_conv_s",
             "l2_conv_p", "l2_ssm_p", "l2_conv_s", "l2_ssm_s", "l3_ckv_p", "l3_kpe_p", "l3_ckv_s", "l3_kpe_s"]
    return tuple(np.ascontiguousarray(outs[k], dtype=np.float32) for k in order)
```
